# Optimizing a Trainium2 kernel written in Bass

```python
import math
import jax, jax.numpy as jnp
from jax import lax
import numpy as np

D_MODEL = 1024
BATCH = 32
SEQ = 2048
DEPTH = 1
DEC_BATCH = 1
DEC_SEQ = 16384
PAST_LEN = 128

HY_WIDTH = 512
N_ATTN_HEADS = 8
HEAD_DIM = 64
ATTN_WIDTH = N_ATTN_HEADS * HEAD_DIM
MIX_WIDTH = HY_WIDTH + ATTN_WIDTH
HY_ORDER = 2
HY_DIRS = 2
HY_EMB = 33
HY_BANDS = (HY_EMB - 1) // 2
HY_FILTER_HIDDEN = 64
HY_DECAY_SLOW = 3.07
HY_DECAY_FAST = 15.35
DILATED_BRANCHES = ((128, 1), (512, 4), (2048, 16))
ATTN_BLOCK = 64
D_FF = 2816
NORM_EPS = 1e-6
NEG_INF = -1e30

kernel_name = 'hybrid_hyena_dilated_alibi_encoder'


def rmsnorm(x, g):
    xf = x.astype(jnp.float32)
    y = xf * lax.rsqrt(jnp.mean(xf * xf, axis=-1, keepdims=True) + NORM_EPS)
    return (y * g.astype(jnp.float32)).astype(x.dtype)


def dwconv3(x, w, b):
    xp = jnp.pad(x, ((0, 0), (1, 1), (0, 0)))
    return xp[:, :-2] * w[0] + xp[:, 1:-1] * w[1] + xp[:, 2:] * w[2] + b


def hyena_filters(L, w1, b1, sin_freq, w2, b2, w3, decay):
    f32 = jnp.float32
    t = jnp.arange(L, dtype=f32)
    tn = t / max(L - 1, 1)
    bands = jnp.linspace(1e-4, HY_BANDS - 1, HY_BANDS, dtype=f32)
    ang = (2.0 * math.pi / L) * t[:, None] * bands[None, :]
    feat = jnp.concatenate([tn[:, None], jnp.cos(ang), jnp.sin(ang)], axis=-1)
    sf = sin_freq.astype(f32)
    h = jnp.sin(sf[0] * (feat @ w1.astype(f32) + b1.astype(f32)))
    h = jnp.sin(sf[1] * (h @ w2.astype(f32) + b2.astype(f32)))
    h = (h @ w3.astype(f32)).reshape(L, HY_ORDER, HY_DIRS, HY_WIDTH)
    h = h * jnp.exp(-tn[:, None, None, None] * jnp.abs(decay.astype(f32))[None])
    h = h / jnp.sum(jnp.abs(h), axis=(0, 2), keepdims=True)
    h_fwd = h[:, :, 0]
    h_bwd = h[:, :, 1]
    k = jnp.concatenate([h_fwd, jnp.zeros((1, HY_ORDER, HY_WIDTH), f32), h_bwd[1:][::-1]], axis=0)
    return jnp.fft.rfft(k, axis=0)


def long_conv(z, kf, skip):
    L = z.shape[1]
    zf = z.astype(jnp.float32)
    y = jnp.fft.irfft(jnp.fft.rfft(zf, n=2 * L, axis=1) * kf[None], n=2 * L, axis=1)[:, :L]
    return (y + skip.astype(jnp.float32) * zf).astype(z.dtype)


def hyena_mixer(u, short_w, short_b, w1, b1, sin_freq, w2, b2, w3, decay, skip):
    L = u.shape[1]
    u = dwconv3(u, short_w, short_b)
    v, x1, x2 = jnp.split(u, 3, axis=-1)
    kf = hyena_filters(L, w1, b1, sin_freq, w2, b2, w3, decay)
    z = x1 * long_conv(v, kf[:, 0], skip[0])
    z = x2 * long_conv(z, kf[:, 1], skip[1])
    return z


def alibi_slopes():
    return jnp.asarray(np.array([2.0 ** (-8.0 * (h + 1) / N_ATTN_HEADS) for h in range(N_ATTN_HEADS)], dtype=np.float32))


def dilated_branch(q, k, v, slopes, window, dilation):
    B, T, H, E = q.shape
    R = window // (2 * dilation)
    Ls = T // dilation
    b = math.gcd(ATTN_BLOCK, Ls)
    nblk = Ls // b
    W = b + 2 * R

    def by_residue(a):
        return a.reshape(B, Ls, dilation, H, E).transpose(0, 2, 1, 3, 4)

    qb = by_residue(q).reshape(B, dilation, nblk, b, H, E)
    pad = ((0, 0), (0, 0), (R, R), (0, 0), (0, 0))
    idx = jnp.arange(nblk)[:, None] * b + jnp.arange(W)[None, :]
    kb = jnp.take(jnp.pad(by_residue(k), pad), idx, axis=2)
    vb = jnp.take(jnp.pad(by_residue(v), pad), idx, axis=2)
    s = jnp.einsum('brnqhe,brnkhe->brnhqk', qb, kb).astype(jnp.float32) * (E ** -0.5)
    rel = jnp.arange(W)[None, :] - R - jnp.arange(b)[:, None]
    key_pos = idx - R
    valid = (jnp.abs(rel) <= R)[None] & ((key_pos >= 0) & (key_pos < Ls))[:, None, :]
    bias = -slopes[:, None, None] * (jnp.abs(rel) * dilation).astype(jnp.float32)[None]
    s = jnp.where(valid[None, None, :, None], s + bias[None, None, None], NEG_INF)
    m = jnp.max(s, axis=-1, keepdims=True)
    p = jnp.exp(s - m)
    den = jnp.sum(p, axis=-1)
    o = jnp.einsum('brnhqk,brnkhe->brnqhe', p, vb.astype(jnp.float32)) / jnp.swapaxes(den, -1, -2)[..., None]
    lse = jnp.swapaxes(m[..., 0] + jnp.log(den), -1, -2)
    o = o.reshape(B, dilation, Ls, H, E).transpose(0, 2, 1, 3, 4).reshape(B, T, H, E)
    lse = lse.reshape(B, dilation, Ls, H).transpose(0, 2, 1, 3).reshape(B, T, H)
    return o, lse


def dilated_attention(u):
    B, T, _ = u.shape
    q, k, v = [a.reshape(B, T, N_ATTN_HEADS, HEAD_DIM) for a in jnp.split(u, 3, axis=-1)]
    slopes = alibi_slopes()
    outs, lses = [], []
    for window, dilation in DILATED_BRANCHES:
        o, lse = dilated_branch(q, k, v, slopes, window, dilation)
        outs.append(o)
        lses.append(lse)
    wts = jax.nn.softmax(jnp.stack(lses, axis=0), axis=0)
    o = jnp.sum(wts[..., None] * jnp.stack(outs, axis=0), axis=0)
    return o.reshape(B, T, ATTN_WIDTH).astype(u.dtype)


def encoder_layer(x, c, ada_w, ada_b, norm1_g, w_in, hy_short_w, hy_short_b, hy_pos_w1, hy_pos_b1,
                  hy_sin_freq, hy_pos_w2, hy_pos_b2, hy_pos_w3, hy_decay, hy_skip, hy_out_g, attn_out_g,
                  w_out, norm2_g, ffn_w_gate, ffn_w_up, ffn_conv_w, ffn_conv_b, ffn_w_down):
    mod = jax.nn.silu(c) @ ada_w + ada_b
    sh1, sc1, g1, sh2, sc2, g2 = jnp.split(mod[:, None, :], 6, axis=-1)
    h = rmsnorm(x, norm1_g) * (1 + sc1) + sh1
    proj = h @ w_in
    hy = hyena_mixer(proj[..., :3 * HY_WIDTH], hy_short_w, hy_short_b, hy_pos_w1, hy_pos_b1, hy_sin_freq,
                     hy_pos_w2, hy_pos_b2, hy_pos_w3, hy_decay, hy_skip)
    at = dilated_attention(proj[..., 3 * HY_WIDTH:])
    mix = jnp.concatenate([rmsnorm(hy, hy_out_g), rmsnorm(at, attn_out_g)], axis=-1) @ w_out
    x = x + g1 * mix
    h = rmsnorm(x, norm2_g) * (1 + sc2) + sh2
    gate = dwconv3(h @ ffn_w_gate, ffn_conv_w, ffn_conv_b)
    x = x + g2 * ((jax.nn.gelu(gate) * (h @ ffn_w_up)) @ ffn_w_down)
    return x


def setup_inputs(seed: int = 0) -> dict:
    key = jax.random.key(seed)
    ks = jax.random.split(key, 32)
    f32 = jnp.float32

    def nrm(k, shape, scale):
        return jax.random.normal(k, shape, f32) * scale

    def gain(k, shape):
        return 1.0 + 0.05 * jax.random.normal(k, shape, f32)

    decay_base = jnp.linspace(HY_DECAY_SLOW, HY_DECAY_FAST, HY_WIDTH, dtype=f32)
    return {
        'x_prompt': nrm(ks[0], (BATCH, SEQ, D_MODEL), 1.0),
        'x_sample': nrm(ks[1], (DEC_BATCH, DEC_SEQ, D_MODEL), 1.0),
        'c_prompt': nrm(ks[2], (BATCH, D_MODEL), 1.0),
        'c_sample': nrm(ks[3], (DEC_BATCH, D_MODEL), 1.0),
        'ada_w': nrm(ks[4], (DEPTH, D_MODEL, 6 * D_MODEL), 0.5 * D_MODEL ** -0.5),
        'ada_b': nrm(ks[5], (DEPTH, 6 * D_MODEL), 0.01),
        'norm1_g': gain(ks[6], (DEPTH, D_MODEL)),
        'w_in': nrm(ks[7], (DEPTH, D_MODEL, 3 * HY_WIDTH + 3 * ATTN_WIDTH), D_MODEL ** -0.5),
        'hy_short_w': nrm(ks[8], (DEPTH, 3, 3 * HY_WIDTH), 3 ** -0.5),
        'hy_short_b': nrm(ks[9], (DEPTH, 3 * HY_WIDTH), 0.01),
        'hy_pos_w1': nrm(ks[10], (DEPTH, HY_EMB, HY_FILTER_HIDDEN), HY_EMB ** -0.5),
        'hy_pos_b1': nrm(ks[11], (DEPTH, HY_FILTER_HIDDEN), 0.02),
        'hy_sin_freq': gain(ks[12], (DEPTH, 2, HY_FILTER_HIDDEN)),
        'hy_pos_w2': nrm(ks[13], (DEPTH, HY_FILTER_HIDDEN, HY_FILTER_HIDDEN), HY_FILTER_HIDDEN ** -0.5),
        'hy_pos_b2': nrm(ks[14], (DEPTH, HY_FILTER_HIDDEN), 0.02),
        'hy_pos_w3': nrm(ks[15], (DEPTH, HY_FILTER_HIDDEN, HY_ORDER * HY_DIRS * HY_WIDTH), HY_FILTER_HIDDEN ** -0.5),
        'hy_decay': decay_base * (1.0 + 0.1 * jax.random.normal(ks[16], (DEPTH, HY_ORDER, HY_DIRS, HY_WIDTH), f32)),
        'hy_skip': nrm(ks[17], (DEPTH, HY_ORDER, HY_WIDTH), 0.5),
        'hy_out_g': gain(ks[18], (DEPTH, HY_WIDTH)),
        'attn_out_g': gain(ks[19], (DEPTH, ATTN_WIDTH)),
        'w_out': nrm(ks[20], (DEPTH, MIX_WIDTH, D_MODEL), MIX_WIDTH ** -0.5),
        'norm2_g': gain(ks[21], (DEPTH, D_MODEL)),
        'ffn_w_gate': nrm(ks[22], (DEPTH, D_MODEL, D_FF), D_MODEL ** -0.5),
        'ffn_w_up': nrm(ks[23], (DEPTH, D_MODEL, D_FF), D_MODEL ** -0.5),
        'ffn_conv_w': nrm(ks[24], (DEPTH, 3, D_FF), 3 ** -0.5),
        'ffn_conv_b': nrm(ks[25], (DEPTH, D_FF), 0.01),
        'ffn_w_down': nrm(ks[26], (DEPTH, D_FF, D_MODEL), D_FF ** -0.5),
        'final_g': gain(ks[27], (D_MODEL,)),
    }


def reference(x_prompt, x_sample, c_prompt, c_sample, ada_w, ada_b, norm1_g, w_in, hy_short_w, hy_short_b,
              hy_pos_w1, hy_pos_b1, hy_sin_freq, hy_pos_w2, hy_pos_b2, hy_pos_w3, hy_decay, hy_skip, hy_out_g,
              attn_out_g, w_out, norm2_g, ffn_w_gate, ffn_w_up, ffn_conv_w, ffn_conv_b, ffn_w_down, final_g):
    def trunk(x, c):
        for l in range(DEPTH):
            x = encoder_layer(x, c, ada_w[l], ada_b[l], norm1_g[l], w_in[l], hy_short_w[l], hy_short_b[l],
                              hy_pos_w1[l], hy_pos_b1[l], hy_sin_freq[l], hy_pos_w2[l], hy_pos_b2[l],
                              hy_pos_w3[l], hy_decay[l], hy_skip[l], hy_out_g[l], attn_out_g[l], w_out[l],
                              norm2_g[l], ffn_w_gate[l], ffn_w_up[l], ffn_conv_w[l], ffn_conv_b[l], ffn_w_down[l])
        return rmsnorm(x, final_g)

    y_prompt = trunk(x_prompt, c_prompt)
    y_sample = trunk(x_sample, c_sample)
    return (y_prompt, y_sample)
```

```python
import contextlib
import math
import numpy as np
import ml_dtypes
import concourse.bass as bass
import concourse.mybir as mybir
from concourse.bass_utils import run_bass_kernel_spmd

F32 = mybir.dt.float32
BF16 = mybir.dt.bfloat16
AF = mybir.ActivationFunctionType
ALU = mybir.AluOpType

ENGS = ["pe", "act", "dve", "pool", "sp"]
N_DMA_SEMS = 18
DMA_POOLS = {"sp": list(range(0, 10)), "actq": list(range(10, 16)), "poolq": list(range(16, 18))}


def _mk(name, *args, **kw):
    def f(e):
        return getattr(e, name)(*args, **kw)
    return f


class Buf:
    def __init__(self, name, t):
        self.name = name
        self.t = t
        self.last_write = None
        self.reads = []

    def __getitem__(self, k):
        return self.t[k]


class Op:
    __slots__ = ("eng", "fn", "deps", "is_dma", "dma_sem", "dma_val", "marked", "cnt")

    def __init__(self, eng, fn, is_dma):
        self.eng = eng
        self.fn = fn
        self.deps = []
        self.is_dma = is_dma
        self.dma_sem = None
        self.dma_val = 0
        self.marked = False
        self.cnt = 0


class Sched:
    def __init__(self, nc):
        self.nc = nc
        self.ops = []
        self.stack = contextlib.ExitStack()
        self.n_dma = {q: 0 for q in DMA_POOLS}
        self.dma_last = [None] * N_DMA_SEMS
        self.dma_cnt = [0] * N_DMA_SEMS
        self.last_on = {e: None for e in ENGS}
        self.barrier_deps = set()
        self.pools = [self.stack]

    def push_pool(self):
        st = contextlib.ExitStack()
        self.pools.append(st)
        return st

    def pop_pool(self):
        self.barrier()
        st = self.pools.pop()
        st.close()

    def barrier(self):
        deps = set(self.barrier_deps)
        for e in ENGS:
            if self.last_on[e] is not None:
                deps.add(self.last_on[e])
        for s in range(N_DMA_SEMS):
            if self.dma_last[s] is not None:
                deps.add(self.dma_last[s])
        self.barrier_deps = deps

    def sbuf(self, name, shape, dtype):
        t = self.pools[-1].enter_context(self.nc.sbuf_tensor(name, list(shape), dtype))
        return Buf(name, t)

    def psum(self, name, shape, dtype=F32):
        t = self.pools[-1].enter_context(self.nc.psum_tensor(name, list(shape), dtype))
        return Buf(name, t)

    def dram(self, name, shape, dtype, kind="Internal"):
        t = self.nc.dram_tensor(name, list(shape), dtype, kind=kind)
        return Buf(name, t.ap())

    def op(self, eng, fn, reads=(), writes=(), acc=False):
        is_dma = eng in ("sp", "actq", "poolq")
        real_eng = {"actq": "act", "poolq": "pool"}.get(eng, eng)
        o = Op(real_eng, fn, is_dma)
        oid = len(self.ops)
        deps = set(self.barrier_deps)
        for b in reads:
            if b.last_write is not None:
                deps.add(b.last_write)
        for b in writes:
            if b.last_write is not None:
                lw = self.ops[b.last_write]
                if is_dma or lw.is_dma or lw.eng != real_eng:
                    deps.add(b.last_write)
            for r in b.reads:
                ro = self.ops[r]
                if is_dma or ro.is_dma or ro.eng != real_eng:
                    deps.add(r)
        if is_dma:
            pool_ = DMA_POOLS[eng]
            s = pool_[self.n_dma[eng] % len(pool_)]
            self.n_dma[eng] += 1
            if self.dma_last[s] is not None:
                deps.add(self.dma_last[s])
            self.dma_last[s] = oid
            self.dma_cnt[s] += 1
            o.dma_sem = s
            o.dma_val = 16 * self.dma_cnt[s]
        if real_eng == "pe" and not is_dma:
            deps = {d for d in deps if not (self.ops[d].eng == "pe" and not self.ops[d].is_dma)}
        o.deps = sorted(deps)
        self.ops.append(o)
        self.last_on[real_eng] = oid
        for b in reads:
            if not is_dma:
                b.reads = [r for r in b.reads if self.ops[r].is_dma or self.ops[r].eng != real_eng]
            b.reads.append(oid)
        for b in writes:
            b.last_write = oid
            b.reads = []
        return oid

    def dma(self, out, in_, reads=(), writes=(), q="sp", **kw):
        return self.op(q, _mk("dma_start", out=out, in_=in_, **kw), reads, writes)

    def emit(self, final_wait_bufs=()):
        nc = self.nc
        ops = self.ops
        final_deps = set()
        for b in final_wait_bufs:
            if b.last_write is not None:
                final_deps.add(b.last_write)
        for s in range(N_DMA_SEMS):
            if self.dma_last[s] is not None:
                final_deps.add(self.dma_last[s])
        for o in ops:
            for d in o.deps:
                ops[d].marked = True
        for d in final_deps:
            ops[d].marked = True
        cnt = {e: 0 for e in ENGS}
        for o in ops:
            if not o.is_dma:
                if o.marked:
                    cnt[o.eng] += 1
                o.cnt = cnt[o.eng]
        sems = {e: self.stack.enter_context(nc.semaphore("s_" + e)) for e in ENGS}
        dsems = [self.stack.enter_context(nc.semaphore("d_%d" % i)) for i in range(N_DMA_SEMS)]

        def tok(d):
            od = ops[d]
            if od.is_dma:
                return ("d", od.dma_sem), od.dma_val
            return ("e", od.eng), od.cnt

        streams = {e: [] for e in ENGS}
        seen = {e: {} for e in ENGS}
        for o in ops:
            waits = {}
            for d in o.deps:
                k, v = tok(d)
                if seen[o.eng].get(k, 0) >= v:
                    continue
                if waits.get(k, 0) < v:
                    waits[k] = v
            for k, v in waits.items():
                seen[o.eng][k] = v
            streams[o.eng].append((o, sorted(waits.items())))
        fw = {}
        for d in final_deps:
            k, v = tok(d)
            if fw.get(k, 0) < v:
                fw[k] = v

        def semof(k):
            return dsems[k[1]] if k[0] == "d" else sems[k[1]]

        def run(eng_name, e):
            for o, waits in streams[eng_name]:
                for k, v in waits:
                    e.wait_ge(semof(k), v)
                ins = o.fn(e)
                if o.is_dma:
                    ins.then_inc(dsems[o.dma_sem], 16)
                elif o.marked:
                    ins.then_inc(sems[eng_name], 1)
            if eng_name == "sp":
                for k, v in sorted(fw.items()):
                    e.wait_ge(semof(k), v)

        with nc.Block() as block:
            @block.sync
            def _(e):
                run("sp", e)

            @block.tensor
            def _(e):
                run("pe", e)

            @block.scalar
            def _(e):
                run("act", e)

            @block.vector
            def _(e):
                run("dve", e)

            @block.gpsimd
            def _(e):
                run("pool", e)
        while self.pools:
            self.pools.pop().close()
        return {e: len(streams[e]) for e in ENGS}


DM = 1024
T = 2048
NSEQ = 4
LS = 16384
NCORE = 8
CH = 2048
HW = 512
NH = 8
HD = 64
DFF = 2816
NFF = 22
EPS = 1e-6
WT_S = 34
QT0_S, QT1_S = 8, 26
OT0_S, OT1_S = 9, 25
SEG_WT = [16, 16, 16, 16, WT_S]
SEG_Q = [(0, 16)] * 4 + [(QT0_S, QT1_S)]
SEG_O = [(0, 16)] * 4 + [(OT0_S, OT1_S)]
HY_STUB = False

SLOPES = [2.0 ** (-8.0 * (h + 1) / NH) for h in range(NH)]


def head_deltas(h):
    dmax = min(8, int((30.0 / SLOPES[h] - 1.0) // 128) + 1)
    return list(range(-dmax, dmax + 1))


MASK_OFF = {}
_n = 0
for _h in range(NH):
    for _d in head_deltas(_h):
        MASK_OFF[(_h, _d)] = _n
        _n += 1
N_MASK = _n


def build_masks():
    m = np.zeros((128, N_MASK, 128), np.float32)
    k = np.arange(128)[:, None]
    q = np.arange(128)[None, :]
    for h in range(NH):
        for d in head_deltas(h):
            o = 128 * d + k - q
            a = np.abs(o)
            mult = (a <= 64).astype(np.float64) + ((o % 4 == 0) & (a <= 256)) + ((o % 16 == 0) & (a <= 1024))
            m[:, MASK_OFF[(h, d)], :] = mult * np.exp(-SLOPES[h] * a)
    return m.astype(ml_dtypes.bfloat16)


def build_program():
    nc = bass.Bass("TRN2", target_bir_lowering=False)
    S = Sched(nc)
    ein = lambda name, shape, dt=F32: S.dram(name, shape, dt, kind="ExternalInput")
    xp = ein("xp", [NSEQ * T, DM])
    xsw = ein("xsw", [WT_S * 128, DM])
    flags_d = ein("flags", [128, WT_S])
    edge_d = ein("edge", [128, 2])
    cc_d = ein("cc", [DM, 5])
    ada_w_d = ein("ada_w", [DM, 6 * DM])
    ada_b_d = ein("ada_b", [128, 48])
    n1g_d = ein("n1g", [128, 8])
    n2g_d = ein("n2g", [128, 8])
    fg_d = ein("fgb", [128, DM])
    w_in_d = ein("w_in", [DM, 3072])
    w_out_d = ein("w_out", [DM, DM])
    wg_d = ein("wg", [DM, DFF])
    wu_d = ein("wu", [DM, DFF])
    wd_d = ein("wd", [DFF, DM])
    fcw_d = ein("fcw", [128, NFF, 3])
    fcb_d = ein("fcb", [128, NFF])
    hyg_d = ein("hyg", [128, 4])
    agb_d = ein("agb", [128, HW])
    ident_d = ein("ident", [128, 128])
    masks_d = ein("masks", [128, N_MASK, 128], BF16)
    xsf = ein("xsf", [LS, DM])
    hw1_d = ein("hw1", [33, 64]); hb1_d = ein("hb1", [64, 1]); hsf_d = ein("hsf", [64, 2])
    hw2_d = ein("hw2", [64, 64]); hb2_d = ein("hb2", [64, 1]); hw3_d = ein("hw3", [64, 2048])
    hdec_d = ein("hdec", [128, 16]); hsw_d = ein("hsw", [128, 12, 3]); hsb_d = ein("hsb", [128, 12])
    hskip_d = ein("hskip", [128, 1024])
    feat_d = {"p": ein("feat_p", [33, 2, T]), "s": ein("feat_s", [33, 2, LS])}
    tn_d = {"p": ein("tn_p", [128, 2, T]), "s": ein("tn_s", [128, 2, LS])}
    fc_d = {}
    for cf, P in (("p", 64), ("s", 128)):
        fc_d[cf] = dict(E1=ein("E1_" + cf, [128, 256], BF16), TW1=ein("TW1_" + cf, [P, 256]), W=ein("W_" + cf, [P, 4 * P], BF16),
                        V=ein("V_" + cf, [P, 6 * P], BF16), TW2=ein("TW2_" + cf, [128, 2 * P]), G=ein("G_" + cf, [128, 384], BF16),
                        EF=ein("EF_" + cf, [128, 128] if cf == "p" else [128, 512], BF16),
                        TWF=ein("TWF_" + cf, [P, 128] if cf == "p" else [P, 256]))
    sel_d = ein("sel", [128, 18], BF16)
    selr_d = ein("selr", [128, 128], BF16)
    yp_d = S.dram("yp", [NSEQ * T, DM], F32, kind="ExternalOutput")
    ys_d = S.dram("ys", [CH, DM], F32, kind="ExternalOutput")
    WINF = S.dram("WINF", [24, 128, 8, 128], BF16)
    WV = S.dram("WV", [128, 8, 512], BF16)
    WGU = S.dram("WGU", [NFF, 128, 2, 8, 128], BF16)
    WD = S.dram("WD", [NFF, 128, DM], BF16)
    WO = S.dram("WO", [128, 8, DM], BF16)
    SEG_TOK = [T] * 4 + [(OT1_S - OT0_S + 2) * 128]
    YH = [S.dram("YH%d" % s, [HW, SEG_TOK[s]], BF16) for s in range(5)]
    AT = [S.dram("AT%d" % s, [HW, SEG_TOK[s]], BF16) for s in range(5)]
    UR = {"p": S.dram("UR_p", [1536, NSEQ, T + 2], BF16), "s": S.dram("UR_s", [1536, 1, LS + 2], BF16)}
    UC = {"p": S.dram("UC_p", [1536, NSEQ * T], BF16), "s": S.dram("UC_s", [1536, LS], BF16)}
    HF = {"p": S.dram("HF_p", [1024, 2 * T], BF16), "s": S.dram("HF_s", [1024, 2 * LS], BF16)}
    H2S = {"p": S.dram("H2S_p", [64, 2, T], BF16), "s": S.dram("H2S_s", [64, 2, LS], BF16)}

    ident_f = S.sbuf("ident_f", [128, 128], F32)
    ident_b = S.sbuf("ident_b", [128, 128], BF16)
    ones_f = S.sbuf("ones_f", [128, 128], F32)
    ones_b = S.sbuf("ones_b", [128, 128], BF16)
    modT = S.sbuf("modT", [128, 48, 5], F32)
    SC1 = S.sbuf("SC1", [128, 8, 5], F32)
    SC2 = S.sbuf("SC2", [128, 8, 5], F32)
    fgb = S.sbuf("fgb_t", [128, DM], F32)
    fcw = S.sbuf("fcw_t", [128, NFF, 3], F32)
    fcb = S.sbuf("fcb_t", [128, NFF], F32)
    hyg = S.sbuf("hyg_t", [128, 4], F32)
    agb = S.sbuf("agb_t", [128, HW], F32)
    edge = S.sbuf("edge_t", [128, 2], F32)
    epsb = S.sbuf("epsb", [128, 1], F32)

    S.dma(ident_f[:], ident_d[:], [ident_d], [ident_f])
    S.op("dve", _mk("tensor_copy", out=ident_b[:], in_=ident_f[:]), [ident_f], [ident_b])
    S.op("pool", _mk("memset", ones_f[:], 1.0), [], [ones_f])
    S.op("pool", _mk("memset", ones_b[:], 1.0), [], [ones_b])
    S.op("pool", _mk("memset", epsb[:], EPS), [], [epsb])
    for dst, src in ((fgb, fg_d), (fcw, fcw_d), (fcb, fcb_d), (hyg, hyg_d), (agb, agb_d), (edge, edge_d)):
        S.dma(dst[:], src[:], [src], [dst])

    S.push_pool()
    stg = [S.sbuf("stg%d" % i, [128, 3072], F32) for i in range(2)]
    stb = [S.sbuf("stb%d" % i, [128, 3072], BF16) for i in range(2)]
    cast_i = [0]

    def cast_rows(src_ap, ncols, writes):
        i = cast_i[0] % 2
        cast_i[0] += 1
        st, sb = stg[i], stb[i]
        S.dma(st[:, 0:ncols], src_ap, [], [st])
        eng = "dve" if i == 0 else "pool"
        S.op(eng, _mk("tensor_copy", out=sb[:, 0:ncols], in_=st[:, 0:ncols]), [st], [sb])
        for dbuf, dst_ap, src_view in writes:
            S.dma(dst_ap, src_view(sb), [sb], [dbuf])

    for k in range(8):
        cast_rows(w_in_d[128 * k:128 * (k + 1), :], 3072, [
            (WINF, WINF[:, :, k, :].rearrange("j p m -> p j m"), lambda sb: sb[:, 0:3072].rearrange("p (j m) -> p j m", m=128)),
            (WV, WV[:, k, :], lambda sb: sb[:, 2560:3072]),
        ])
        cast_rows(wg_d[128 * k:128 * (k + 1), :], DFF, [
            (WGU, WGU[:, :, 0, k, :].rearrange("j p m -> p j m"), lambda sb: sb[:, 0:DFF].rearrange("p (j m) -> p j m", m=128)),
        ])
        cast_rows(wu_d[128 * k:128 * (k + 1), :], DFF, [
            (WGU, WGU[:, :, 1, k, :].rearrange("j p m -> p j m"), lambda sb: sb[:, 0:DFF].rearrange("p (j m) -> p j m", m=128)),
        ])
        cast_rows(w_out_d[128 * k:128 * (k + 1), :], DM, [
            (WO, WO[:, k, :], lambda sb: sb[:, 0:DM]),
        ])
    for j in range(NFF):
        cast_rows(wd_d[128 * j:128 * (j + 1), :], DM, [
            (WD, WD[j], lambda sb: sb[:, 0:DM]),
        ])

    cT = S.sbuf("cT", [128, 8, 5], F32)
    scT = S.sbuf("scT", [128, 8, 5], F32)
    abT = S.sbuf("abT", [128, 48], F32)
    n1g = S.sbuf("n1g_t", [128, 8], F32)
    n2g = S.sbuf("n2g_t", [128, 8], F32)
    S.dma(cT[:], cc_d[:].rearrange("(k p) b -> p k b", p=128), [cc_d], [cT])
    S.dma(abT[:], ada_b_d[:], [ada_b_d], [abT])
    S.dma(n1g[:], n1g_d[:], [n1g_d], [n1g])
    S.dma(n2g[:], n2g_d[:], [n2g_d], [n2g])
    S.op("act", _mk("activation", out=scT[:], in_=cT[:], func=AF.Silu), [cT], [scT])
    awt = [S.sbuf("awt%d" % i, [128, 8, 128], F32) for i in range(2)]
    psm = S.psum("psm", [128, 512], F32)
    for j in range(48):
        a = awt[j % 2]
        S.dma(a[:], ada_w_d[:, 128 * j:128 * (j + 1)].rearrange("(k p) m -> p k m", p=128), [], [a])
        for k in range(8):
            S.op("pe", _mk("matmul", psm[:, 5 * j:5 * j + 5], lhsT=a[:, k, :], rhs=scT[:, k, :],
                                                      start=(k == 0), stop=(k == 7)), [a, scT], [psm], acc=(k > 0))
    S.op("dve", _mk("tensor_tensor", out=modT[:], in0=psm[:, 0:240].rearrange("p (j s) -> p j s", s=5),
                                          in1=abT[:].unsqueeze(2).broadcast_to([128, 48, 5]), op=ALU.add), [psm, abT], [modT])
    for (SC, base, ng) in ((SC1, 8, n1g), (SC2, 32, n2g)):
        S.op("dve", _mk("tensor_scalar", out=SC[:], in0=modT[:, base:base + 8, :], scalar1=1.0, scalar2=None,
                                                                  op0=ALU.add), [modT], [SC])
        S.op("dve", _mk("tensor_tensor", out=SC[:], in0=SC[:], in1=ng[:].unsqueeze(2).broadcast_to([128, 8, 5]),
                                                           op=ALU.mult), [SC, ng], [SC])
    S.pop_pool()

    def rstd_from_ss(ss, rs, n, eng_recip="dve"):
        S.op("act", _mk("activation", out=rs[:], in_=ss[:], func=AF.Sqrt, bias=epsb[:], scale=1.0 / n), [ss, epsb], [rs])
        S.op("dve", _mk("reciprocal", out=rs[:], in_=rs[:]), [rs], [rs])

    S.push_pool()
    masks = S.sbuf("masks_t", [128, N_MASK, 128], BF16)
    S.dma(masks[:], masks_d[:], [masks_d], [masks])
    NQMAX = QT1_S - QT0_S
    QT = S.sbuf("QT", [128, 4, NQMAX * 128], BF16)
    KT = S.sbuf("KT", [128, 4, WT_S * 128], BF16)
    VP = S.sbuf("VP", [128, WT_S, NH, HD + 1], BF16)
    flg = S.sbuf("flg", [128, WT_S], F32)
    wv_t = S.sbuf("wv_t", [128, 8, 512], BF16)
    S.dma(wv_t[:].rearrange("p k m -> p (k m)"), WV[:].rearrange("p k m -> p (k m)"), [WV], [wv_t])
    wqk = [S.sbuf("wqk%d" % i, [128, 8, 128], BF16) for i in range(3)]
    xg = [S.sbuf("xg%d" % i, [128, DM], F32) for i in range(3)]
    xn = [S.sbuf("xn%d" % i, [128, DM], BF16) for i in range(2)]
    hTg = [S.sbuf("hTg%d" % i, [128, 8, 512], BF16) for i in range(2)]
    ss_t = [S.sbuf("ss%d" % i, [128, 1], F32) for i in range(4)]
    rs_t = [S.sbuf("rs%d" % i, [128, 1], F32) for i in range(4)]
    junk = S.sbuf("junk", [128, DM], BF16)
    ptrs = [S.psum("ptr%d" % i, [128, 1024], BF16) for i in range(2)]
    ptr = ptrs[0]
    pp = [S.psum("pp%d" % i, [128, 512], F32) for i in range(2)]
    psc = [S.psum("psc%d" % i, [128, 512], F32) for i in range(2)]
    po = [S.psum("po%d" % i, [128, 512], F32) for i in range(2)]
    pT = [S.sbuf("pT%d" % i, [128, 512], BF16) for i in range(3)]
    o_t = [S.sbuf("o_t%d" % i, [128, HW], F32) for i in range(2)]
    on_t = [S.sbuf("on_t%d" % i, [128, HW], BF16) for i in range(2)]
    rden = [S.sbuf("rden%d" % i, [128, NH], F32) for i in range(2)]
    aTt = [S.sbuf("aTt%d" % i, [128, 4, 512], BF16) for i in range(2)]
    cnt = {"g": 0, "t": 0, "pp": 0, "psc": 0, "pT": 0, "q": 0, "x": 0, "w": 0}

    ub = [S.sbuf("ub%d" % i, [128, 512], BF16) for i in range(2)]
    zpad = S.sbuf("zpad", [128, 12, 8], BF16)
    S.op("pool", _mk("memset", zpad[:], 0.0), [], [zpad])

    def hy_proj(h_t, N, cf, sq_, tok0):
        for j in range(12):
            w_ = wqk[cnt["w"] % 3]
            cnt["w"] += 1
            S.dma(w_[:].rearrange("p k m -> p (k m)"), WINF[j].rearrange("p k m -> p (k m)"), [WINF], [w_])
            p_ = pp[cnt["pp"] % 2]
            cnt["pp"] += 1
            for k in range(8):
                S.op("pe", _mk("matmul", p_[:, 0:N], lhsT=w_[:, k, :], rhs=h_t[:, k, 0:N], start=(k == 0), stop=(k == 7)), [w_, h_t], [p_])
            u_ = ub[cnt["pp"] % 2]
            S.op("act", _mk("activation", out=u_[:, 0:N], in_=p_[:, 0:N], func=AF.Copy), [p_], [u_])
            S.dma(UR[cf][128 * j:128 * (j + 1), sq_, 1 + tok0:1 + tok0 + N], u_[:, 0:N], [u_], [UR[cf]], q="actq")

    def norm_to_hT(x_t, h_t, t, seg):
        ti = cnt["t"] % 4
        cnt["t"] += 1
        ss, rs, xnb = ss_t[ti], rs_t[ti], xn[ti % 2]
        S.op("act", _mk("activation", out=junk[:], in_=x_t[:], func=AF.Square, accum_out=ss[:]), [x_t], [junk, ss])
        rstd_from_ss(ss, rs, DM)
        S.op("dve", _mk("tensor_scalar", out=xnb[:], in0=x_t[:], scalar1=rs[:], scalar2=None, op0=ALU.mult), [x_t, rs], [xnb])
        ptr_ = ptrs[ti % 2]
        for k in range(8):
            S.op("pe", _mk("transpose", ptr_[:, 128 * k:128 * (k + 1)], xnb[:, 128 * k:128 * (k + 1)], ident_b[:]), [xnb, ident_b], [ptr_])
        for k in range(8):
            if k % 2 == 0:
                S.op("act", _mk("activation", out=h_t[:, k, 128 * t:128 * (t + 1)], in_=ptr_[:, 128 * k:128 * (k + 1)],
                                func=AF.Identity, scale=SC1[:, k, seg:seg + 1], bias=modT[:, k, seg:seg + 1]), [ptr_, SC1, modT], [h_t])
            else:
                S.op("dve", _mk("tensor_scalar", out=h_t[:, k, 128 * t:128 * (t + 1)], in0=ptr_[:, 128 * k:128 * (k + 1)],
                                scalar1=SC1[:, k, seg:seg + 1], scalar2=modT[:, k, seg:seg + 1], op0=ALU.mult, op1=ALU.add),
                     [ptr_, SC1, modT], [h_t])

    if not HY_STUB:
        for cf, nsq, L_ in (("p", NSEQ, T), ("s", 1, LS)):
            for sq_ in range(nsq):
                for col in (0, L_ + 1):
                    S.dma(UR[cf][:, sq_, col:col + 1].rearrange("(j p) o -> p j o", p=128), zpad[:, :, 0:1], [zpad], [UR[cf]],
                          allow_slow_non_contiguous=True)
        for g0 in range(0, LS // 128, 4):
            h_t = hTg[cnt["g"] % 2]
            cnt["g"] += 1
            for t in range(4):
                x_t = xg[cnt["x"] % 3]
                cnt["x"] += 1
                S.dma(x_t[:], xsf[128 * (g0 + t):128 * (g0 + t + 1), :], [xsf], [x_t])
                norm_to_hT(x_t, h_t, t, 4)
            hy_proj(h_t, 512, "s", 0, 128 * g0)

    for seg in range(5):
        nwt = SEG_WT[seg]
        q0, q1 = SEG_Q[seg]
        xsrc = xp if seg < 4 else xsw
        xoff = seg * T if seg < 4 else 0
        if seg < 4:
            S.op("pool", _mk("memset", flg[:], 1.0), [], [flg])
        else:
            S.dma(flg[:], flags_d[:], [flags_d], [flg])
        for g0 in range(0, nwt, 4):
            nt = min(4, nwt - g0)
            h_t = hTg[cnt["g"] % 2]
            cnt["g"] += 1
            for t in range(nt):
                x_t = xg[cnt["x"] % 3]
                cnt["x"] += 1
                r0 = xoff + 128 * (g0 + t)
                S.dma(x_t[:], xsrc[r0:r0 + 128, :], [xsrc], [x_t])
                ti = cnt["t"] % 4
                cnt["t"] += 1
                ss, rs, xnb = ss_t[ti], rs_t[ti], xn[ti % 2]
                S.op("act", _mk("activation", out=junk[:], in_=x_t[:], func=AF.Square, accum_out=ss[:]),
                     [x_t], [junk, ss])
                rstd_from_ss(ss, rs, DM)
                S.op("dve", _mk("tensor_scalar", out=xnb[:], in0=x_t[:], scalar1=rs[:], scalar2=None,
                                                                           op0=ALU.mult), [x_t, rs], [xnb])
                ptr_ = ptrs[ti % 2]
                for k in range(8):
                    S.op("pe", _mk("transpose", ptr_[:, 128 * k:128 * (k + 1)], xnb[:, 128 * k:128 * (k + 1)], ident_b[:]),
                         [xnb, ident_b], [ptr_])
                for k in range(8):
                    if k % 2 == 0:
                        S.op("act", _mk("activation", out=h_t[:, k, 128 * t:128 * (t + 1)], in_=ptr_[:, 128 * k:128 * (k + 1)],
                                        func=AF.Identity, scale=SC1[:, k, seg:seg + 1], bias=modT[:, k, seg:seg + 1]), [ptr_, SC1, modT], [h_t])
                    else:
                        S.op("dve", _mk("tensor_scalar", out=h_t[:, k, 128 * t:128 * (t + 1)], in0=ptr_[:, 128 * k:128 * (k + 1)],
                                        scalar1=SC1[:, k, seg:seg + 1], scalar2=modT[:, k, seg:seg + 1], op0=ALU.mult, op1=ALU.add),
                             [ptr_, SC1, modT], [h_t])
            N = 128 * nt
            tok0 = 128 * g0
            qa, qb = max(g0, q0), min(g0 + nt, q1)
            for j in range(8):
                if j < 4 and qa >= qb:
                    continue
                w_ = wqk[cnt["w"] % 3]
                cnt["w"] += 1
                S.dma(w_[:].rearrange("p k m -> p (k m)"), WINF[12 + j].rearrange("p k m -> p (k m)"), [WINF], [w_])
                p_ = pp[cnt["pp"] % 2]
                cnt["pp"] += 1
                if j < 4:
                    ca, cb = 128 * (qa - g0), 128 * (qb - g0)
                else:
                    ca, cb = 0, N
                for k in range(8):
                    S.op("pe", _mk("matmul",
                        p_[:, 0:cb - ca], lhsT=w_[:, k, :], rhs=h_t[:, k, ca:cb], start=(k == 0), stop=(k == 7)), [w_, h_t], [p_])
                if j < 4:
                    S.op("act", _mk("activation",
                        out=QT[:, j, 128 * (qa - q0):128 * (qa - q0) + cb - ca], in_=p_[:, 0:cb - ca], func=AF.Copy, scale=0.125), [p_], [QT])
                else:
                    S.op("dve", _mk("tensor_copy", out=KT[:, j - 4, tok0:tok0 + N], in_=p_[:, 0:N]), [p_], [KT])
            if seg < 4 and not HY_STUB:
                hy_proj(h_t, N, "p", seg, tok0)
            for t in range(nt):
                p_ = pp[cnt["pp"] % 2]
                cnt["pp"] += 1
                for k in range(8):
                    S.op("pe", _mk("matmul", p_[:, :], lhsT=h_t[:, k, 128 * t:128 * (t + 1)], rhs=wv_t[:, k, :],
                                                                 start=(k == 0), stop=(k == 7)), [wv_t, h_t], [p_])
                wt = g0 + t
                S.op("dve", _mk("tensor_scalar", out=VP[:, wt, :, 0:HD], in0=p_[:, :].rearrange("p (h e) -> p h e", e=HD),
                                                                   scalar1=flg[:, wt:wt + 1], scalar2=None, op0=ALU.mult), [p_, flg], [VP])
                S.op("pool", _mk("tensor_copy", out=VP[:, wt, :, HD:HD + 1],
                                                           in_=flg[:, wt:wt + 1].unsqueeze(1).broadcast_to([128, NH, 1])), [flg], [VP])
        for qt in range(q0, q1):
            qi = cnt["q"] % 2
            cnt["q"] += 1
            qc = 128 * (qt - q0)
            chunks = []
            for h in range(NH):
                kts = [(d, qt + d) for d in head_deltas(h) if 0 <= qt + d < nwt]
                for c0 in range(0, len(kts), 4):
                    chunks.append((h, c0, kts[c0:c0 + 4], len(kts)))

            def emit_scores(ch):
                h, c0, blk, nk = ch
                hp, hb = h // 2, 64 * (h % 2)
                nb = len(blk)
                sc = psc[cnt["psc"] % 2]
                cnt["psc"] += 1
                for bi, (d, kt) in enumerate(blk):
                    S.op("pe", _mk("matmul", sc[:, 128 * bi:128 * (bi + 1)], lhsT=KT[hb:hb + 64, hp, 128 * kt:128 * (kt + 1)],
                                   rhs=QT[hb:hb + 64, hp, qc:qc + 128], start=True, stop=True), [KT, QT], [sc])
                pt_ = pT[cnt["pT"] % 3]
                cnt["pT"] += 1
                S.op("act", _mk("activation", out=pt_[:, 0:128 * nb], in_=sc[:, 0:128 * nb], func=AF.Exp), [sc], [pt_])
                m0 = MASK_OFF[(h, blk[0][0])]
                S.op("dve", _mk("tensor_tensor", out=pt_[:, 0:128 * nb], in0=pt_[:, 0:128 * nb],
                                in1=masks[:, m0:m0 + nb, :].rearrange("p b q -> p (b q)"), op=ALU.mult), [pt_, masks], [pt_])
                return pt_

            def emit_pv(ch, pt_):
                h, c0, blk, nk = ch
                pob, hs = po[h // 4], h % 4
                for bi, (d, kt) in enumerate(blk):
                    first = (c0 == 0 and bi == 0)
                    last = (c0 + bi == nk - 1)
                    S.op("pe", _mk("matmul", pob[:, 65 * hs:65 * hs + 65], lhsT=pt_[:, 128 * bi:128 * (bi + 1)], rhs=VP[:, kt, h, :],
                                   start=first, stop=last), [pt_, VP], [pob])

            prev = None
            for ch in chunks:
                pt_ = emit_scores(ch)
                if prev is not None:
                    emit_pv(*prev)
                prev = (ch, pt_)
            emit_pv(*prev)
            ot, ont, rd = o_t[qi], on_t[qi], rden[qi]
            for half in range(2):
                S.op("dve", _mk("tensor_scalar",
                    out=rd[:, 4 * half:4 * half + 4], in0=po[half][:, 0:260].rearrange("p (h e) -> p h e", e=65)[:, :, 64],
                    scalar1=1e-30, scalar2=None, op0=ALU.add), [po[half]], [rd])
            S.op("dve", _mk("reciprocal", out=rd[:], in_=rd[:]), [rd], [rd])
            for half in range(2):
                S.op("dve", _mk("tensor_tensor",
                    out=ot[:, 256 * half:256 * half + 256].rearrange("p (h e) -> p h e", e=64),
                    in0=po[half][:, 0:260].rearrange("p (h e) -> p h e", e=65)[:, :, 0:64],
                    in1=rd[:, 4 * half:4 * half + 4].unsqueeze(2).broadcast_to([128, 4, 64]), op=ALU.mult), [po[half], rd], [ot])
            ti = cnt["t"] % 4
            cnt["t"] += 1
            ss, rs = ss_t[ti], rs_t[ti]
            S.op("act", _mk("activation", out=junk[:, 0:HW], in_=ot[:], func=AF.Square, accum_out=ss[:]), [ot], [junk, ss])
            rstd_from_ss(ss, rs, HW)
            S.op("dve", _mk("scalar_tensor_tensor", out=ont[:], in0=ot[:], scalar=rs[:], in1=agb[:],
                                                                               op0=ALU.mult, op1=ALU.mult), [ot, rs, agb], [ont])
            grp = (qt - q0) // 4
            a_t = aTt[grp % 2]
            for k in range(4):
                S.op("pe", _mk("transpose", ptr[:, 128 * k:128 * (k + 1)], ont[:, 128 * k:128 * (k + 1)], ident_b[:]),
                     [ont, ident_b], [ptr])
            tl = (qt - q0) % 4
            S.op("dve", _mk("tensor_copy",
                out=a_t[:, :, 128 * tl:128 * (tl + 1)], in_=ptr[:, 0:512].rearrange("p (k t) -> p k t", t=128)), [ptr], [a_t])
            if tl == 3 or qt == q1 - 1:
                ntok = 128 * (tl + 1)
                c0 = 128 * (qt - q0 - tl)
                S.dma(AT[seg][:, c0:c0 + ntok].rearrange("(k p) t -> p k t", p=128), a_t[:, :, 0:ntok], [a_t], [AT[seg]], q="actq")
    S.pop_pool()

    if HY_STUB:
        S.push_pool()
        z = S.sbuf("zz", [128, 4, T + 512], BF16)
        S.op("pool", _mk("memset", z[:], 0.0), [], [z])
        for s in range(5):
            S.dma(YH[s][:, :].rearrange("(k p) t -> p k t", p=128), z[:, :, 0:SEG_TOK[s]], [z], [YH[s]])
        S.pop_pool()
    else:
        S.push_pool()
        PI = math.pi
        pAh = [S.psum("pA%d" % i, [128, 512], F32) for i in range(2)]
        pZh = [S.psum("pZ%d" % i, [128, 512], F32) for i in range(2)]
        pCh = [S.psum("pC%d" % i, [128, 512], F32) for i in range(2)]
        pY = [S.psum("pY%d" % i, [128, 512], F32) for i in range(2)]
        pA, pZ, pB = pAh[0], pZh[0], pAh[1]
        hw1 = S.sbuf("t_hw1", [33, 64], F32); hb1 = S.sbuf("t_hb1", [64, 1], F32); hsf = S.sbuf("t_hsf", [64, 2], F32)
        hw2 = S.sbuf("t_hw2", [64, 64], F32); hb2 = S.sbuf("t_hb2", [64, 1], F32)
        hw3f = S.sbuf("t_hw3f", [64, 2048], F32); hw3b = S.sbuf("t_hw3b", [64, 2048], BF16)
        hdec = S.sbuf("t_hdec", [128, 16], F32); negd = S.sbuf("t_negd", [128, 16], F32)
        hsw = S.sbuf("t_hsw", [128, 12, 3], F32); hsb = S.sbuf("t_hsb", [128, 12], F32)
        skb = S.sbuf("t_skb", [128, 1024], F32)
        sel = S.sbuf("t_sel", [128, 18], BF16)
        for dst, src in ((hw1, hw1_d), (hb1, hb1_d), (hsf, hsf_d), (hw2, hw2_d), (hb2, hb2_d), (hw3f, hw3_d), (hdec, hdec_d),
                         (hsw, hsw_d), (hsb, hsb_d), (skb, hskip_d), (sel, sel_d)):
            S.dma(dst[:], src[:], [src], [dst])
        S.op("dve", _mk("tensor_copy", out=hw3b[:], in_=hw3f[:]), [hw3f], [hw3b])
        S.op("dve", _mk("tensor_scalar", out=negd[:], in0=hdec[:], scalar1=-1.0, scalar2=None, op0=ALU.mult), [hdec], [negd])
        S.op("dve", _mk("tensor_tensor", out=negd[:], in0=negd[:], in1=hdec[:], op=ALU.min), [negd, hdec], [negd])
        RNB = {cf: S.sbuf("RNB_" + cf, [128, 2, 512], F32) for cf in ("p", "s")}
        SB = S.sbuf("SBt", [128, 2048], F32)
        dg3 = S.sbuf("dg3", [128, 128], F32)
        S.push_pool()
        ft = [S.sbuf("ft%d" % i, [33, 2, 512], F32) for i in range(2)]
        tnc = [S.sbuf("tnc%d" % i, [128, 2, 512], F32) for i in range(2)]
        a1 = S.sbuf("a1", [64, 512], F32); wtmp = S.sbuf("wtmp", [64, 512], F32)
        h1 = S.sbuf("h1", [64, 512], F32); h2b = S.sbuf("h2b", [64, 512], BF16)
        Et = [S.sbuf("Et%d" % i, [128, 512], F32) for i in range(2)]
        hq = [S.sbuf("hq%d" % i, [128, 512], F32) for i in range(2)]
        hqb = [S.sbuf("hqb%d" % i, [128, 512], BF16) for i in range(2)]
        junkf = S.sbuf("junkf", [128, 512], BF16)
        asum = S.sbuf("asum", [128, 16, 32], F32)
        tot = S.sbuf("tot", [128, 16], F32)

        def sin_layer(ps, bias, sfc, out_t):
            S.op("dve", _mk("tensor_scalar", out=a1[:], in0=ps[0:64, :], scalar1=bias[:, 0:1], scalar2=sfc, op0=ALU.add, op1=ALU.mult),
                 [ps, bias, hsf], [a1])
            S.op("dve", _mk("tensor_scalar", out=wtmp[:], in0=a1[:], scalar1=PI, scalar2=-2.0 * PI, op0=ALU.is_gt, op1=ALU.mult), [a1], [wtmp])
            S.op("dve", _mk("tensor_tensor", out=a1[:], in0=a1[:], in1=wtmp[:], op=ALU.add), [a1, wtmp], [a1])
            S.op("dve", _mk("tensor_scalar", out=wtmp[:], in0=a1[:], scalar1=-PI, scalar2=2.0 * PI, op0=ALU.is_lt, op1=ALU.mult), [a1], [wtmp])
            S.op("dve", _mk("tensor_tensor", out=a1[:], in0=a1[:], in1=wtmp[:], op=ALU.add), [a1, wtmp], [a1])
            S.op("act", _mk("activation", out=out_t[:], in_=a1[:], func=AF.Sin), [a1], [out_t])

        ur = [S.sbuf("ur%d" % i, [128, 2050], BF16) for i in range(2)]
        ct = [S.sbuf("ct%d" % i, [128, 2048], F32) for i in range(2)]
        ucb = [S.sbuf("ucb%d" % i, [128, 2048], BF16) for i in range(2)]
        conv_items = []

        def mk_conv(cf, sq_, L_, pc, j, ci):
            def ld():
                u_ = ur[ci % 2]
                S.dma(u_[:], UR[cf][128 * j:128 * (j + 1), sq_, 2048 * pc:2048 * pc + 2050], [UR[cf]], [u_])

            def f():
                u_, c_, o_ = ur[ci % 2], ct[ci % 2], ucb[ci % 2]
                S.op("pool", _mk("tensor_scalar", out=c_[:], in0=u_[:, 1:2049], scalar1=hsw[:, j, 1:2], scalar2=hsb[:, j:j + 1],
                                 op0=ALU.mult, op1=ALU.add), [u_, hsw, hsb], [c_])
                S.op("dve", _mk("scalar_tensor_tensor", out=c_[:], in0=u_[:, 0:2048], scalar=hsw[:, j, 0:1], in1=c_[:],
                                op0=ALU.mult, op1=ALU.add), [u_, hsw, c_], [c_])
                S.op("dve", _mk("scalar_tensor_tensor", out=o_[:], in0=u_[:, 2:2050], scalar=hsw[:, j, 2:3], in1=c_[:],
                                op0=ALU.mult, op1=ALU.add), [u_, hsw, c_], [o_])
                S.dma(UC[cf][128 * j:128 * (j + 1), sq_ * L_ + 2048 * pc: sq_ * L_ + 2048 * (pc + 1)], o_[:], [o_], [UC[cf]], q="actq")
            return ld, f

        _ci = 0
        _pairs = []
        for cf, nsq, L_ in (("p", NSEQ, T), ("s", 1, LS)):
            for sq_ in range(nsq):
                for pc in range(L_ // 2048):
                    for j in range(12):
                        _ci += 1
                        _pairs.append(mk_conv(cf, sq_, L_, pc, j, _ci))
        _pairs[0][0]()
        for i_, (ld_, f_) in enumerate(_pairs):
            nld = _pairs[i_ + 1][0] if i_ + 1 < len(_pairs) else None
            conv_items.append((lambda f_=f_, nld=nld: (nld() if nld else None, f_())))

        hqb3 = [S.sbuf("hqb3_%d" % i, [128, 512], BF16) for i in range(3)]
        NCH = 4
        ma = [S.sbuf("ma%d" % i, [64, 512], F32) for i in range(NCH)]
        mw = [S.sbuf("mw%d" % i, [64, 512], F32) for i in range(NCH)]
        mh1 = [S.sbuf("mh1_%d" % i, [64, 512], F32) for i in range(NCH)]
        mh2 = [S.sbuf("mh2_%d" % i, [64, 512], BF16) for i in range(NCH)]
        mps = [pAh[0], pAh[1], pZh[0], pZh[1]]
        h2l = [S.sbuf("h2l%d" % i, [64, 2, 512], BF16) for i in range(3)]
        fc = {"ci": 0}

        def wrap_sin(ps, bias, sfc, a_, w_, out_t):
            S.op("dve", _mk("tensor_scalar", out=a_[:], in0=ps[0:64, :], scalar1=bias[:, 0:1], scalar2=sfc, op0=ALU.add, op1=ALU.mult),
                 [ps, bias, hsf], [a_])
            S.op("dve", _mk("tensor_scalar", out=w_[:], in0=a_[:], scalar1=PI, scalar2=-2.0 * PI, op0=ALU.is_gt, op1=ALU.mult), [a_], [w_])
            S.op("dve", _mk("tensor_tensor", out=a_[:], in0=a_[:], in1=w_[:], op=ALU.add), [a_, w_], [a_])
            S.op("dve", _mk("tensor_scalar", out=w_[:], in0=a_[:], scalar1=-PI, scalar2=2.0 * PI, op0=ALU.is_lt, op1=ALU.mult), [a_], [w_])
            S.op("dve", _mk("tensor_tensor", out=a_[:], in0=a_[:], in1=w_[:], op=ALU.add), [a_, w_], [a_])
            S.op("act", _mk("activation", out=out_t[:], in_=a_[:], func=AF.Sin), [a_], [out_t])

        for cf, L_ in (("p", T), ("s", LS)):
            nch = L_ // 512
            for pc0 in range(0, nch, 2):
                jobs = []
                for pc in (pc0, pc0 + 1):
                    f_ = ft[pc % 2]
                    S.dma(f_[:], feat_d[cf][:, :, 512 * pc:512 * (pc + 1)], [], [f_])
                    for d in range(2):
                        jobs.append((pc, d, f_))
                for i, (pc, d, f_) in enumerate(jobs):
                    S.op("pe", _mk("matmul", mps[i][0:64, :], lhsT=hw1[:], rhs=f_[:, d, :], start=True, stop=True), [hw1, f_], [mps[i]])
                for i, (pc, d, f_) in enumerate(jobs):
                    wrap_sin(mps[i], hb1, hsf[:, 0:1], ma[i], mw[i], mh1[i])
                for i, (pc, d, f_) in enumerate(jobs):
                    S.op("pe", _mk("matmul", mps[i][0:64, :], lhsT=hw2[:], rhs=mh1[i][:], start=True, stop=True), [hw2, mh1[i]], [mps[i]])
                for i, (pc, d, f_) in enumerate(jobs):
                    wrap_sin(mps[i], hb2, hsf[:, 1:2], ma[i], mw[i], mh2[i])
                for i, (pc, d, f_) in enumerate(jobs):
                    S.dma(H2S[cf][:, d, 512 * pc:512 * (pc + 1)], mh2[i][:], [mh2[i]], [H2S[cf]], q="actq")
                if conv_items:
                    conv_items.pop(0)()

            def load_chunk(pc):
                h_, t_ = h2l[pc % 3], tnc[pc % 2]
                S.dma(h_[:], H2S[cf][:, :, 512 * pc:512 * (pc + 1)], [H2S[cf]], [h_])
                S.dma(t_[:], tn_d[cf][:, :, 512 * pc:512 * (pc + 1)], [], [t_])

            def rc_front(pc, rc):
                fc["ci"] += 1
                ci = fc["ci"]
                d = (rc // 4) % 2
                p3, E_, q_ = pY[ci % 2], Et[ci % 2], hq[ci % 2]
                h_, t_ = h2l[pc % 3], tnc[pc % 2]
                S.op("pe", _mk("matmul", p3[:], lhsT=hw3b[:, 128 * rc:128 * (rc + 1)], rhs=h_[:, d, :], start=True, stop=True), [hw3b, h_], [p3])
                S.op("act", _mk("activation", out=E_[:], in_=t_[:, d, :], func=AF.Exp, scale=negd[:, rc:rc + 1]), [t_, negd], [E_])
                S.op("dve", _mk("tensor_tensor", out=q_[:], in0=p3[:], in1=E_[:], op=ALU.mult), [p3, E_], [q_])
                return q_, ci

            def rc_back(pc, rc, q_, ci):
                o_, d, cch = rc // 8, (rc // 4) % 2, rc % 4
                qb_ = hqb3[ci % 3]
                S.op("act", _mk("activation", out=junkf[:], in_=q_[:], func=AF.Abs, accum_out=asum[:, rc, pc:pc + 1]), [q_], [junkf, asum])
                S.op("pool", _mk("tensor_copy", out=qb_[:], in_=q_[:]), [q_], [qb_])
                if d == 1 and pc == 0:
                    S.op("pool", _mk("memset", qb_[:, 0:1], 0.0), [], [qb_])
                r0 = 512 * o_ + 128 * cch
                col0 = (0 if d == 1 else L_) + 512 * pc
                S.dma(HF[cf][r0:r0 + 128, col0:col0 + 512], qb_[:], [qb_], [HF[cf]])

            load_chunk(0)
            for pc in range(nch):
                if pc + 1 < nch:
                    load_chunk(pc + 1)
                cur = rc_front(pc, 0)
                for rc in range(16):
                    nxt = rc_front(pc, rc + 1) if rc + 1 < 16 else None
                    rc_back(pc, rc, *cur)
                    cur = nxt
                    if rc % 4 == 3 and conv_items:
                        conv_items.pop(0)()
            S.op("dve", _mk("tensor_reduce", out=tot[:], in_=asum[:, :, 0:nch], op=ALU.add, axis=mybir.AxisListType.X), [asum], [tot])
            for rc in range(16):
                S.op("dve", _mk("tensor_scalar", out=dg3[:], in0=ident_f[:], scalar1=tot[:, rc:rc + 1], scalar2=None, op0=ALU.mult),
                     [ident_f, tot], [dg3])
                S.op("pe", _mk("matmul", pB[:, 128 * (rc % 4):128 * (rc % 4 + 1)], lhsT=ones_f[:], rhs=dg3[:], start=True, stop=True),
                     [ones_f, dg3], [pB])
                if rc % 4 == 3:
                    S.op("act", _mk("activation", out=SB[:, 128 * (rc - 3):128 * (rc + 1)], in_=pB[:], func=AF.Copy), [pB], [SB])
            sbv = SB[:].rearrange("p (o d c) -> p o d c", o=2, d=2)
            S.op("dve", _mk("tensor_tensor", out=RNB[cf][:], in0=sbv[:, :, 0, :], in1=sbv[:, :, 1, :], op=ALU.add), [SB], [RNB[cf]])
            S.op("dve", _mk("reciprocal", out=RNB[cf][:], in_=RNB[cf][:]), [RNB[cf]], [RNB[cf]])

        while conv_items:
            conv_items.pop(0)()
        S.pop_pool()
        GC, NB, NHF = 8, 2, 4
        HR = list(range(NHF))
        cm1 = [[S.sbuf("cm1_%d%d" % (h, i), [128, NB, 2, 128], F32) for i in range(1)] for h in HR]
        cm2 = [[S.sbuf("cm2_%d%d" % (h, i), [128, NB, 2, 128], F32) for i in range(1)] for h in HR]
        Xb = [[S.sbuf("Xb%d_%d" % (h, i), [128, NB, 2, 128], BF16) for i in range(2)] for h in HR]
        Kc = [[S.sbuf("Kc%d%d" % (h, i), [128, NB, 2, 2, 128], F32) for i in range(1)] for h in HR]
        gt = [S.sbuf("gt%d" % h, [128, NB, 128], F32) for h in HR]
        yv = [S.sbuf("yv%d" % h, [128, NB, 128], F32) for h in HR]
        z1 = [S.sbuf("z1_%d" % h, [128, NB, 128], BF16) for h in HR]
        pH = [pAh[0], pAh[1], pZh[0], pZh[1]]
        pAh = pZh = pCh = pYh = pH
        cmi = [0] * NHF
        xbi = [0] * NHF

        def next_xb(h):
            xbi[h] += 1
            return Xb[h][xbi[h] % 2]

        T1b = [[S.sbuf("T1b%d_%d" % (h, i), [128, NB, 2, 128], BF16) for i in range(2)] for h in HR]
        T2b = [[S.sbuf("T2b%d_%d" % (h, i), [128, NB, 2, 128], BF16) for i in range(2)] for h in HR]

        def cprod(h, src4, R, W, cr3, ci3, src_buf, cbufs):
            cmi[h] += 1
            a_, b_ = T1b[h][cmi[h] % 2], T2b[h][cmi[h] % 2]
            crb = cr3.unsqueeze(2).broadcast_to([R, NB, 2, W])
            cib = ci3.unsqueeze(2).broadcast_to([R, NB, 2, W])
            S.op("dve", _mk("tensor_tensor", out=a_[0:R, :, :, 0:W], in0=src4, in1=crb, op=ALU.mult), [src_buf] + cbufs, [a_])
            S.op("dve", _mk("tensor_tensor", out=b_[0:R, :, :, 0:W], in0=src4, in1=cib, op=ALU.mult), [src_buf] + cbufs, [b_])
            return a_, b_

        for cf, P in (("p", 64), ("s", 128)):
            S.push_pool()
            E1 = S.sbuf("E1" + cf, [128, 256], BF16); TW1 = S.sbuf("TW1" + cf, [P, 2, 128], F32)
            Wc_ = S.sbuf("W" + cf, [P, 4 * P], BF16); Vc = S.sbuf("V" + cf, [P, 6 * P], BF16)
            TW2 = S.sbuf("TW2" + cf, [128, 2, P], F32); Gc = S.sbuf("G" + cf, [128, 384], BF16)
            WF = 64 if cf == "p" else 128
            EF = S.sbuf("EF" + cf, [128, 128] if cf == "p" else [128, 2, 256], BF16)
            TWF = S.sbuf("TWF" + cf, [P, 2, WF], F32)
            S.dma(E1[:], fc_d[cf]["E1"][:], [], [E1])
            S.dma(TW1[:], fc_d[cf]["TW1"][:].rearrange("p (c w) -> p c w", c=2), [], [TW1])
            S.dma(Wc_[:], fc_d[cf]["W"][:], [], [Wc_])
            S.dma(Vc[:], fc_d[cf]["V"][:], [], [Vc])
            S.dma(TW2[:], fc_d[cf]["TW2"][:].rearrange("p (c w) -> p c w", c=2), [], [TW2])
            S.dma(Gc[:], fc_d[cf]["G"][:], [], [Gc])
            if cf == "p":
                S.dma(EF[:], fc_d[cf]["EF"][:], [], [EF])
            else:
                S.dma(EF[:], fc_d[cf]["EF"][:].rearrange("p (k w) -> p k w", k=2), [], [EF])
            S.dma(TWF[:], fc_d[cf]["TWF"][:].rearrange("p (c w) -> p c w", c=2), [], [TWF])
            if cf == "p":
                hf = [S.sbuf("hf%s%d" % (cf, i), [128, GC, P], BF16) for i in range(2)]
            else:
                hf = [S.sbuf("hf%s%d" % (cf, i), [128, 2, 2, GC, P], BF16) for i in range(2)]
            vx = [S.sbuf("vx%s%d" % (cf, i), [128, 3, GC, P], BF16) for i in range(2)]
            zo = [S.sbuf("zo%s%d" % (cf, i), [128, GC, P], BF16) for i in range(2)]
            zs = [S.sbuf("zs%s%d" % (cf, i), [18, GC, 128], BF16) for i in range(2)]

            def s2_complex(h, t1_, t2_, Wd):
                pz = pZh[h][0:P, :].rearrange("p (g c w) -> p g c w", g=NB, c=2)
                Wre, Wim, nWim, nWre = (Wc_[:, 0:P], Wc_[:, P:2 * P], Wc_[:, 2 * P:3 * P], Wc_[:, 3 * P:4 * P])
                for ri, terms in ((0, ((Wre, t1_, 0), (nWre, t2_, 1), (nWim, t2_, 0), (nWim, t1_, 1))),
                                  (1, ((Wre, t2_, 0), (Wre, t1_, 1), (Wim, t1_, 0), (nWim, t2_, 1)))):
                    for ti, (w_, tb, c_) in enumerate(terms):
                        S.op("pe", _mk("matmul", pz[:, :, ri, 0:Wd], lhsT=w_, rhs=tb[0:P, :, c_, 0:Wd], start=(ti == 0), stop=(ti == 3)),
                             [Wc_, tb], [pZh[h]])
                return pz

            def combine(h, t1_, t2_, R, W):
                xb = next_xb(h)
                S.op("pool", _mk("tensor_tensor", out=xb[0:R, :, 0, 0:W], in0=t1_[0:R, :, 0, 0:W], in1=t2_[0:R, :, 1, 0:W], op=ALU.subtract), [t1_, t2_], [xb])
                S.op("pool", _mk("tensor_tensor", out=xb[0:R, :, 1, 0:W], in0=t1_[0:R, :, 1, 0:W], in1=t2_[0:R, :, 0, 0:W], op=ALU.add), [t1_, t2_], [xb])
                return xb

            for c0 in range(0, HW, GC):
                bi_ = (c0 // GC) % 2
                hf_, vx_, zo_, zs_ = hf[bi_], vx[bi_], zo[bi_], zs[bi_]
                L_ = T if cf == "p" else LS
                for o in range(2):
                    src_rows = HF[cf][512 * o + c0:512 * o + c0 + GC, :]
                    if cf == "p":
                        S.dma(hf_[64 * o:64 * (o + 1), :, :], src_rows.rearrange("g (i p) -> i g p", p=P), [HF[cf]], [hf_])
                    else:
                        for kc in range(2):
                            S.dma(hf_[:, o, kc, :, :], src_rows[:, L_ * kc:L_ * (kc + 1)].rearrange("g (i p) -> i g p", p=P), [HF[cf]], [hf_])
                for w3 in range(3):
                    S.dma(vx_[:, w3, :, :], UC[cf][512 * w3 + c0:512 * w3 + c0 + GC, :].rearrange("g (r p) -> r g p", p=P), [UC[cf]], [vx_])
                for q4 in range(1):
                    gi = 0
                    cbase = [c0 + NB * h for h in HR]
                    gl = [NB * h for h in HR]
                    for o in ([None] if cf == "p" else [0, 1]):
                        for h in HR:
                            for g in range(NB):
                                if cf == "p":
                                    S.op("pe", _mk("matmul", pAh[h][0:P, 256 * g:256 * g + 128], lhsT=hf_[:, gl[h] + g, :], rhs=EF[:], start=True, stop=True),
                                         [hf_, EF], [pAh[h]])
                                else:
                                    for kc in range(2):
                                        S.op("pe", _mk("matmul", pAh[h][:, 256 * g:256 * (g + 1)], lhsT=hf_[:, o, kc, gl[h] + g, :], rhs=EF[:, kc, :],
                                                       start=(kc == 0), stop=(kc == 1)), [hf_, EF], [pAh[h]])
                        fps = {}
                        for h in HR:
                            src4 = pAh[h][0:P, :].rearrange("p (g c w) -> p g c w", g=NB, c=2)[:, :, :, 0:WF] if cf == "s" else \
                                pAh[h][0:P, :].rearrange("p (g x) -> p g x", g=NB)[:, :, 0:128].rearrange("p g (c w) -> p g c w", c=2)
                            fps[h] = cprod(h, src4, P, WF, TWF[:, 0, :].unsqueeze(1).broadcast_to([P, NB, WF]),
                                           TWF[:, 1, :].unsqueeze(1).broadcast_to([P, NB, WF]), pAh[h], [TWF])
                        for h in HR:
                            pz = s2_complex(h, fps[h][0], fps[h][1], WF)
                            for g in range(NB):
                                c = cbase[h] + g
                                if cf == "p":
                                    for oo in range(2):
                                        S.op("act", _mk("activation", out=Kc[h][gi][0:P, g, oo, :, :].rearrange("p c (s f) -> p c s f", s=4),
                                                        in_=pz[:, g, :, 32 * oo:32 * (oo + 1)].unsqueeze(2).broadcast_to([P, 2, 4, 32]),
                                                        func=AF.Copy, scale=RNB[cf][0:P, oo, c:c + 1]), [pZh[h], RNB[cf]], [Kc[h][gi]])
                                else:
                                    S.op("act", _mk("activation", out=Kc[h][gi][:, g, o, :, :], in_=pz[:, g, :, :], func=AF.Copy,
                                                    scale=RNB[cf][:, o, c:c + 1]), [pZh[h], RNB[cf]], [Kc[h][gi]])
                    for o in range(2):
                        for h in HR:
                            for g in range(NB):
                                zin = vx_[:, 0, gl[h] + g, :] if o == 0 else z1[h][:, g, 0:P]
                                S.op("pe", _mk("matmul", pAh[h][0:P, 256 * g:256 * (g + 1)], lhsT=zin, rhs=E1[:], start=True, stop=True),
                                     [vx_ if o == 0 else z1[h], E1], [pAh[h]])
                        aps, yhs, ccs = {}, {}, {}
                        for h in HR:
                            aps[h] = cprod(h, pAh[h][0:P, :].rearrange("p (g c w) -> p g c w", g=NB, c=2), P, 128,
                                           TW1[:, 0, :].unsqueeze(1).broadcast_to([P, NB, 128]), TW1[:, 1, :].unsqueeze(1).broadcast_to([P, NB, 128]),
                                           pAh[h], [TW1])
                        for h in HR:
                            s2_complex(h, aps[h][0], aps[h][1], 128)
                        for h in HR:
                            yhs[h] = cprod(h, pZh[h][0:P, :].rearrange("p (g c w) -> p g c w", g=NB, c=2), P, 128,
                                           Kc[h][gi][0:P, :, o, 0, :], Kc[h][gi][0:P, :, o, 1, :], pZh[h], [Kc[h][gi]])
                        for h in HR:
                            yhs[h] = combine(h, yhs[h][0], yhs[h][1], P, 128)
                        for h in HR:
                            pc_ = pCh[h][:, :].rearrange("p (g x) -> p g x", g=NB)
                            VA, VB = Vc[:, 0:2 * P], Vc[:, 2 * P:4 * P]
                            for g in range(NB):
                                S.op("pe", _mk("matmul", pc_[:, g, 0:2 * P], lhsT=yhs[h][0:P, g, 0, :], rhs=VA, start=True, stop=False), [yhs[h], Vc], [pCh[h]])
                                S.op("pe", _mk("matmul", pc_[:, g, 0:2 * P], lhsT=yhs[h][0:P, g, 1, :], rhs=VB, start=False, stop=True), [yhs[h], Vc], [pCh[h]])
                        for h in HR:
                            src4 = pCh[h][:, :].rearrange("p (g x) -> p g x", g=NB)[:, :, 0:2 * P].rearrange("p g (c w) -> p g c w", c=2)
                            ccs[h] = cprod(h, src4, 128, P, TW2[:, 0, :].unsqueeze(1).broadcast_to([128, NB, P]),
                                           TW2[:, 1, :].unsqueeze(1).broadcast_to([128, NB, P]), pCh[h], [TW2])
                        for h in HR:
                            ccs[h] = combine(h, ccs[h][0], ccs[h][1], 128, P)
                        for h in HR:
                            Gre, Gim = Gc[:, 0:128], Gc[:, 128:256]
                            py3 = pYh[h][:, 0:128 * NB].rearrange("p (g x) -> p g x", g=NB)[:, :, 0:P]
                            S.op("pe", _mk("matmul", py3, lhsT=Gre, rhs=ccs[h][:, :, 0, 0:P], start=True, stop=False), [Gc, ccs[h]], [pYh[h]])
                            S.op("pe", _mk("matmul", py3, lhsT=Gim, rhs=ccs[h][:, :, 1, 0:P], start=False, stop=True), [Gc, ccs[h]], [pYh[h]])
                        for h in HR:
                            cb = 512 * o + cbase[h]
                            zin3 = vx_[:, 0, gl[h]:gl[h] + NB, :] if o == 0 else z1[h][:, :, 0:P]
                            xg3 = vx_[:, 1 + o, gl[h]:gl[h] + NB, :]
                            for g in range(NB):
                                S.op("act", _mk("activation", out=gt[h][:, g, 0:P], in_=zin3[:, g, :], func=AF.Copy, scale=skb[:, cb + g:cb + g + 1]),
                                     [vx_ if o == 0 else z1[h], skb], [gt[h]])
                            S.op("act", _mk("activation", out=yv[h][:, :, 0:P], in_=pYh[h][:, 0:128 * NB].rearrange("p (g x) -> p g x", g=NB)[:, :, 0:P],
                                            func=AF.Copy), [pYh[h]], [yv[h]])
                            S.op("pool", _mk("tensor_tensor", out=gt[h][:, :, 0:P], in0=yv[h][:, :, 0:P], in1=gt[h][:, :, 0:P], op=ALU.add),
                                 [yv[h], gt[h]], [gt[h]])
                            dst = z1[h][:, :, 0:P] if o == 0 else zo_[:, gl[h]:gl[h] + NB, :]
                            S.op("pool", _mk("tensor_tensor", out=dst, in0=gt[h][:, :, 0:P], in1=xg3, op=ALU.mult), [gt[h], vx_], [z1[h] if o == 0 else zo_])
                    if cf == "s":
                        for h in HR:
                            for g in range(NB):
                                S.op("pe", _mk("matmul", pAh[h][0:18, 128 * g:128 * (g + 1)], lhsT=sel[:], rhs=zo_[:, gl[h] + g, :], start=True, stop=True),
                                     [sel, zo_], [pAh[h]])
                            S.op("act", _mk("activation", out=zs_[:, gl[h]:gl[h] + NB, :], in_=pAh[h][0:18, 0:128 * NB].rearrange("p (g x) -> p g x", g=NB),
                                            func=AF.Copy), [pAh[h]], [zs_])
                if cf == "p":
                    for sq_ in range(NSEQ):
                        S.dma(YH[sq_][c0:c0 + GC, :].rearrange("g (i p) -> i g p", p=P), zo_[32 * sq_:32 * (sq_ + 1), :, :], [zo_], [YH[sq_]], q="actq")
                else:
                    S.dma(YH[4][c0:c0 + GC, :].rearrange("g (r p) -> r g p", p=128), zs_[:], [zs_], [YH[4]], q="actq")
            S.pop_pool()
        S.pop_pool()

    S.push_pool()
    X1S = S.dram("X1S", [18 * 128, DM], F32)
    G1B = S.sbuf("G1B", [128, DM], F32)
    G2B = S.sbuf("G2B", [128, DM], F32)
    dg = S.sbuf("dg", [128, 128], F32)
    hT2 = S.sbuf("hT2", [128, 8, 18 * 128 + 2], BF16)
    x1 = [S.sbuf("x1_%d" % i, [128, DM], F32) for i in range(3)]
    x1r = [S.sbuf("x1r%d" % i, [128, DM], F32) for i in range(2)]
    xin = [S.sbuf("xin%d" % i, [128, DM], F32) for i in range(2)]
    mixT = S.sbuf("mixT", [128, 8, 512], BF16)
    yh_t = S.sbuf("yh", [128, 4, 512], BF16)
    sq = S.sbuf("sq", [128, 4, 512], BF16)
    rsb = S.sbuf("rsb", [128, 512], F32)
    wo_t = S.sbuf("wo_t", [128, 8, DM], BF16)
    S.dma(wo_t[:].rearrange("p k m -> p (k m)"), WO[:].rearrange("p k m -> p (k m)"), [WO], [wo_t])
    wgu_t = [S.sbuf("wgu_t%d" % i, [128, 2, 8, 128], BF16) for i in range(4)]
    wd_t = [S.sbuf("wd_t%d" % i, [128, DM], BF16) for i in range(3)]
    aT = S.sbuf("aT", [128, NFF, 512], BF16)
    gext = [S.sbuf("gext%d" % i, [128, 514], F32) for i in range(3)]
    upsb = [S.sbuf("upsb%d" % i, [128, 512], F32) for i in range(3)]
    cvs = [S.sbuf("cv%d" % i, [128, 512], F32) for i in range(2)]
    sgs = [S.sbuf("sg%d" % i, [128, 512], F32) for i in range(2)]
    tmp = [S.sbuf("tmp%d" % i, [128, 512], F32) for i in range(2)]
    xn2 = [S.sbuf("xn2_%d" % i, [128, DM], BF16) for i in range(2)]
    x2 = S.sbuf("x2", [128, DM], F32)
    yo = [S.sbuf("yo%d" % i, [128, DM], F32) for i in range(2)]
    ss2 = [S.sbuf("ss2_%d" % i, [128, 1], F32) for i in range(4)]
    rs2 = [S.sbuf("rs2_%d" % i, [128, 1], F32) for i in range(4)]
    junk2 = S.sbuf("junk2", [128, DM], BF16)
    p_ss = S.psum("p_ss", [128, 512], F32)
    p_o = [S.psum("p_o%d" % i, [128, 512], F32) for i in range(2)]
    p_trb = S.psum("p_trf", [128, 512], F32)
    p_tr = p_trb[:, :].bitcast(BF16)
    p_g = S.psum("p_g", [128, 512], F32)
    p_h = S.psum("p_h", [128, 512], F32)
    p_us = [S.psum("p_u%d" % i, [128, 512], F32) for i in range(2)]
    p_u = p_us[0]
    p_b = p_ss
    c2 = {"o": 0, "t": 0, "w": 0, "e": 0, "x1": 0, "xin": 0, "y": 0, "g": 0, "wd": 0}

    for seg in range(5):
        o0, o1 = SEG_O[seg]
        q0, q1 = SEG_Q[seg]
        na = q1 - q0
        hal = o0 - q0
        xsrc = xp if seg < 4 else xsw
        xoff = seg * T if seg < 4 else 128 * q0
        ydst = yp_d if seg < 4 else ys_d
        yoff = seg * T if seg < 4 else 0
        for (GB, base) in ((G1B, 16), (G2B, 40)):
            for k in range(8):
                S.op("dve", _mk("tensor_scalar", out=dg[:], in0=ident_f[:], scalar1=modT[:, base + k, seg:seg + 1],
                                                                      scalar2=None, op0=ALU.mult), [ident_f, modT], [dg])
                S.op("pe", _mk("matmul", p_b[:, 128 * (k % 4):128 * (k % 4 + 1)], lhsT=ones_f[:], rhs=dg[:], start=True, stop=True),
                     [ones_f, dg], [p_b])
                if k % 4 == 3:
                    S.op("act", _mk("activation", out=GB[:, 128 * (k - 3):128 * (k + 1)], in_=p_b[:], func=AF.Copy),
                         [p_b], [GB])
        S.op("pool", _mk("memset", hT2[:, :, 0:1], 0.0), [], [hT2])
        S.op("pool", _mk("memset", hT2[:, :, 1 + 128 * na:2 + 128 * na], 0.0), [], [hT2])

        def stage_a(ta0, nta):
            N = 128 * nta
            c0 = 128 * ta0
            mx = mixT
            S.dma(yh_t[:, :, 0:N], YH[seg][:, c0:c0 + N].rearrange("(k p) t -> p k t", p=128), [YH[seg]], [yh_t])
            S.dma(mx[:, 4:8, 0:N], AT[seg][:, c0:c0 + N].rearrange("(k p) t -> p k t", p=128), [AT[seg]], [mx])
            S.op("pool", _mk("tensor_tensor", out=sq[:, :, 0:N], in0=yh_t[:, :, 0:N], in1=yh_t[:, :, 0:N], op=ALU.mult), [yh_t], [sq])
            for k in range(4):
                S.op("pe", _mk("matmul", p_ss[:, 0:N], lhsT=ones_b[:], rhs=sq[:, k, 0:N], start=(k == 0), stop=(k == 3)),
                     [ones_b, sq], [p_ss])
            S.op("act", _mk("activation", out=rsb[:, 0:N], in_=p_ss[:, 0:N], func=AF.Sqrt, bias=epsb[:], scale=1.0 / HW),
                 [p_ss, epsb], [rsb])
            S.op("dve", _mk("reciprocal", out=rsb[:, 0:N], in_=rsb[:, 0:N]), [rsb], [rsb])
            for k in range(4):
                S.op("dve", _mk("scalar_tensor_tensor", out=mx[:, k, 0:N], in0=yh_t[:, k, 0:N], scalar=hyg[:, k:k + 1],
                                                                  in1=rsb[:, 0:N], op0=ALU.mult, op1=ALU.mult), [yh_t, hyg, rsb], [mx])
            def part1(t):
                xi = xin[c2["xin"] % 2]
                c2["xin"] += 1
                x1t = x1[c2["x1"] % 3]
                c2["x1"] += 1
                r0 = xoff + c0 + 128 * t
                S.dma(xi[:], xsrc[r0:r0 + 128, :], [xsrc], [xi])
                for nh_ in range(2):
                    p_ = p_o[c2["o"] % 2]
                    c2["o"] += 1
                    for k in range(8):
                        S.op("pe", _mk("matmul", p_[:], lhsT=mx[:, k, 128 * t:128 * (t + 1)], rhs=wo_t[:, k, 512 * nh_:512 * (nh_ + 1)],
                                       start=(k == 0), stop=(k == 7)), [mx, wo_t], [p_])
                    tm = tmp[c2["e"] % 2]
                    c2["e"] += 1
                    S.op("dve", _mk("tensor_tensor", out=tm[:], in0=p_[:], in1=G1B[:, 512 * nh_:512 * (nh_ + 1)], op=ALU.mult), [p_, G1B], [tm])
                    S.op("pool", _mk("tensor_tensor", out=x1t[:, 512 * nh_:512 * (nh_ + 1)], in0=xi[:, 512 * nh_:512 * (nh_ + 1)], in1=tm[:],
                                     op=ALU.add), [xi, tm], [x1t])
                S.dma(X1S[c0 + 128 * t:c0 + 128 * (t + 1), :], x1t[:], [x1t], [X1S], q="actq")
                return x1t

            def part2(t, x1t):
                ti = c2["t"] % 4
                c2["t"] += 1
                ss, rs, xnb = ss2[ti], rs2[ti], xn2[ti % 2]
                S.op("act", _mk("activation", out=junk2[:], in_=x1t[:], func=AF.Square, accum_out=ss[:]), [x1t], [junk2, ss])
                rstd_from_ss(ss, rs, DM)
                S.op("dve", _mk("tensor_scalar", out=xnb[:], in0=x1t[:], scalar1=rs[:], scalar2=None, op0=ALU.mult), [x1t, rs], [xnb])
                for k in range(8):
                    S.op("pe", _mk("transpose", p_tr[:, 128 * k:128 * (k + 1)], xnb[:, 128 * k:128 * (k + 1)], ident_b[:]), [xnb, ident_b], [p_trb])
                col = 1 + c0 + 128 * t
                for k in range(8):
                    if k % 2 == 0:
                        S.op("act", _mk("activation", out=hT2[:, k, col:col + 128], in_=p_tr[:, 128 * k:128 * (k + 1)], func=AF.Identity,
                                        scale=SC2[:, k, seg:seg + 1], bias=modT[:, 24 + k, seg:seg + 1]), [p_trb, SC2, modT], [hT2])
                    else:
                        S.op("dve", _mk("tensor_scalar", out=hT2[:, k, col:col + 128], in0=p_tr[:, 128 * k:128 * (k + 1)],
                                        scalar1=SC2[:, k, seg:seg + 1], scalar2=modT[:, 24 + k, seg:seg + 1], op0=ALU.mult, op1=ALU.add),
                             [p_trb, SC2, modT], [hT2])

            prev = None
            for t in range(nta):
                x1t = part1(t)
                if prev is not None:
                    part2(*prev)
                prev = (t, x1t)
            part2(*prev)

        def stage_b(tb0):
            colb = 1 + 128 * tb0
            pgs = [p_g, p_h]

            def ffn_mm(j):
                wbuf = wgu_t[c2["w"] % 4]
                c2["w"] += 1
                S.dma(wbuf[:].rearrange("p a k m -> p (a k m)"), WGU[j].rearrange("p a k m -> p (a k m)"), [WGU], [wbuf])
                wgt, wut = wbuf[:, 0], wbuf[:, 1]
                pg_, pu_ = pgs[j % 2], p_us[j % 2]
                for k in range(8):
                    S.op("pe", _mk("matmul", pg_[:], lhsT=wgt[:, k, :], rhs=hT2[:, k, colb:colb + 512], start=(k == 0), stop=(k == 7)),
                         [wbuf, hT2], [pg_])
                for k in range(8):
                    S.op("pe", _mk("matmul", p_ss[:, 0:2], lhsT=wgt[:, k, :], rhs=hT2[:, k, colb - 1:colb + 513:513], start=(k == 0), stop=(k == 7)),
                         [wbuf, hT2], [p_ss])
                for k in range(8):
                    S.op("pe", _mk("matmul", pu_[:], lhsT=wut[:, k, :], rhs=hT2[:, k, colb:colb + 512], start=(k == 0), stop=(k == 7)),
                         [wbuf, hT2], [pu_])
                ge = gext[j % 3]
                S.op("act", _mk("activation", out=ge[:, 1:513], in_=pg_[:], func=AF.Copy), [pg_], [ge])
                S.op("act", _mk("activation", out=ge[:, 0:514:513], in_=p_ss[:, 0:2], func=AF.Copy), [p_ss], [ge])
                S.op("act", _mk("activation", out=upsb[j % 3][:], in_=pu_[:], func=AF.Copy), [pu_], [upsb[j % 3]])

            def ffn_chain(j):
                ge, pu_ = gext[j % 3], upsb[j % 3]
                cv, sgj = cvs[j % 2], sgs[j % 2]
                if seg == 4 and tb0 == hal:
                    S.op("dve", _mk("tensor_scalar", out=ge[:, 0:1], in0=ge[:, 0:1], scalar1=edge[:, 0:1], scalar2=None, op0=ALU.mult), [ge, edge], [ge])
                if seg == 4 and tb0 + 4 == na - hal:
                    S.op("dve", _mk("tensor_scalar", out=ge[:, 513:514], in0=ge[:, 513:514], scalar1=edge[:, 1:2], scalar2=None, op0=ALU.mult),
                         [ge, edge], [ge])
                S.op("dve", _mk("tensor_scalar", out=cv[:], in0=ge[:, 1:513], scalar1=fcw[:, j, 1:2], scalar2=fcb[:, j:j + 1], op0=ALU.mult, op1=ALU.add),
                     [ge, fcw, fcb], [cv])
                S.op("dve", _mk("scalar_tensor_tensor", out=cv[:], in0=ge[:, 0:512], scalar=fcw[:, j, 0:1], in1=cv[:], op0=ALU.mult, op1=ALU.add),
                     [ge, fcw, cv], [cv])
                S.op("dve", _mk("scalar_tensor_tensor", out=cv[:], in0=ge[:, 2:514], scalar=fcw[:, j, 2:3], in1=cv[:], op0=ALU.mult, op1=ALU.add),
                     [ge, fcw, cv], [cv])
                S.op("act", _mk("activation", out=sgj[:], in_=cv[:], func=AF.Gelu_apprx_tanh), [cv], [sgj])
                S.op("pool", _mk("tensor_tensor", out=aT[:, j, :], in0=sgj[:], in1=pu_[:], op=ALU.mult), [sgj, pu_], [aT])

            for j in range(NFF):
                ffn_mm(j)
                if j >= 2:
                    ffn_chain(j - 2)
            ffn_chain(NFF - 2)
            ffn_chain(NFF - 1)
            accs = [p_o[0], p_o[1], p_us[0], p_us[1], p_g, p_h, p_ss, p_trb]
            for j in range(NFF):
                wdt = wd_t[c2["wd"] % 3]
                c2["wd"] += 1
                S.dma(wdt[:], WD[j], [WD], [wdt])
                for t in range(4):
                    for nh_ in range(2):
                        acc_ = accs[2 * t + nh_]
                        S.op("pe", _mk("matmul", acc_[:], lhsT=aT[:, j, 128 * t:128 * (t + 1)], rhs=wdt[:, 512 * nh_:512 * (nh_ + 1)],
                                       start=(j == 0), stop=(j == NFF - 1)), [aT, wdt], [acc_])
            for t in range(4):
                x1t = x1r[c2["x1"] % 2]
                c2["x1"] += 1
                S.dma(x1t[:], X1S[128 * (tb0 + t):128 * (tb0 + t + 1), :], [X1S], [x1t])
                for nh_ in range(2):
                    acc_ = accs[2 * t + nh_]
                    tm = tmp[c2["e"] % 2]
                    c2["e"] += 1
                    S.op("dve", _mk("tensor_tensor", out=tm[:], in0=acc_[:], in1=G2B[:, 512 * nh_:512 * (nh_ + 1)], op=ALU.mult), [acc_, G2B], [tm])
                    S.op("pool", _mk("tensor_tensor", out=x2[:, 512 * nh_:512 * (nh_ + 1)], in0=x1t[:, 512 * nh_:512 * (nh_ + 1)], in1=tm[:],
                                     op=ALU.add), [x1t, tm], [x2])
                yot = yo[c2["y"] % 2]
                c2["y"] += 1
                ti = c2["t"] % 4
                c2["t"] += 1
                ss, rs = ss2[ti], rs2[ti]
                S.op("act", _mk("activation", out=junk2[:], in_=x2[:], func=AF.Square, accum_out=ss[:]), [x2], [junk2, ss])
                rstd_from_ss(ss, rs, DM)
                S.op("dve", _mk("scalar_tensor_tensor", out=yot[:], in0=x2[:], scalar=rs[:], in1=fgb[:], op0=ALU.mult, op1=ALU.mult),
                     [x2, rs, fgb], [yot])
                r0 = yoff + 128 * (tb0 - hal + t)
                S.dma(ydst[r0:r0 + 128, :], yot[:], [yot], [ydst], q="actq")

        if seg < 4:
            a_groups = [(0, 4), (4, 4), (8, 4), (12, 4)]
            b_groups = [(0, 0), (4, 1), (8, 2), (12, 3)]
        else:
            a_groups = [(0, 1), (1, 4), (5, 4), (9, 4), (13, 4), (17, 1)]
            b_groups = [(1, 1), (5, 2), (9, 3), (13, 4)]
        bi = 0
        for ai, (ta0, nta) in enumerate(a_groups):
            stage_a(ta0, nta)
            while bi < len(b_groups) and b_groups[bi][1] < ai:
                stage_b(b_groups[bi][0])
                bi += 1
        while bi < len(b_groups):
            stage_b(b_groups[bi][0])
            bi += 1
    S.pop_pool()
    n = S.emit(final_wait_bufs=[yp_d, ys_d])
    return nc, n


_CACHE = {}


def fft_consts():
    out = {}
    bf = ml_dtypes.bfloat16
    for cf, P, I, Sq, L in (("p", 64, 32, 4, T), ("s", 128, 128, 1, LS)):
        N2, N = 2 * I, 2 * L
        t = np.arange(L, dtype=np.float64)
        tn = (t / (L - 1)).astype(np.float32)
        bands = np.linspace(1e-4, 15.0, 16).astype(np.float32).astype(np.float64)
        ang = (2.0 * math.pi / L) * t[:, None] * bands[None, :]
        feat = np.concatenate([tn[:, None].astype(np.float64), np.cos(ang), np.sin(ang)], axis=1).astype(np.float32)
        rev = (L - np.arange(L)) % L
        out["feat_" + cf] = np.ascontiguousarray(np.stack([feat.T, feat[rev].T], axis=1))
        tn2 = np.stack([tn, tn[rev]], axis=0)
        out["tn_" + cf] = np.ascontiguousarray(np.broadcast_to(tn2[None], (128, 2, L)).astype(np.float32))
        i = np.arange(I)[:, None]
        fb = np.arange(I)[None, :]
        th1 = 2 * math.pi * i * (fb + 0.5) / N2
        E1 = np.zeros((Sq, I, 2, Sq, I))
        G = np.zeros((Sq, I, 2, Sq, I))
        for s_ in range(Sq):
            E1[s_, :, 0, s_, :] = np.cos(th1)
            E1[s_, :, 1, s_, :] = -np.sin(th1)
            G[s_, :, 0, s_, :] = (2.0 / N) * np.cos(th1).T
            G[s_, :, 1, s_, :] = -(2.0 / N) * np.sin(th1).T
        out["E1_" + cf] = E1.reshape(128, 256).astype(bf)
        G2 = G.reshape(128, 2, 128)
        out["G_" + cf] = np.concatenate([G2[:, 0], G2[:, 1], -G2[:, 0]], axis=1).astype(bf)
        p = np.arange(P)[:, None]
        th2 = 2 * math.pi * p * (np.arange(I)[None, :] + 0.5) / N
        cr = np.tile(np.cos(th2), (1, Sq))
        ci = np.tile(-np.sin(th2), (1, Sq))
        out["TW1_" + cf] = np.concatenate([cr, ci], axis=1).astype(np.float32)
        cr2 = np.tile(np.cos(th2).T, (Sq, 1))
        ci2 = np.tile(np.sin(th2).T, (Sq, 1))
        out["TW2_" + cf] = np.concatenate([cr2, ci2], axis=1).astype(np.float32)
        thw = 2 * math.pi * p * np.arange(P)[None, :] / P
        out["W_" + cf] = np.concatenate([np.cos(thw), -np.sin(thw), np.sin(thw), -np.cos(thw)], axis=1).astype(bf)
        out["V_" + cf] = np.concatenate([np.cos(thw), np.sin(thw), -np.sin(thw), np.cos(thw), -np.cos(thw), -np.sin(thw)],
                                        axis=1).astype(bf)
        ifull = np.arange(N2)[:, None]
        thf = 2 * math.pi * ifull * (fb + 0.5) / N2
        twf = np.exp(-1j * th2) * (1j * (-1.0) ** np.arange(I))[None, :]
        if cf == "p":
            EF = np.zeros((2, N2, 2, 2, I))
            for o in range(2):
                EF[o, :, 0, o, :] = np.cos(thf)
                EF[o, :, 1, o, :] = -np.sin(thf)
            out["EF_" + cf] = EF.reshape(128, 128).astype(bf)
            out["TWF_" + cf] = np.concatenate([np.tile(twf.real, (1, 2)), np.tile(twf.imag, (1, 2))], axis=1).astype(np.float32)
        else:
            EF = np.stack([np.cos(thf), -np.sin(thf)], axis=1)
            EF = EF.reshape(2, 128, 256).transpose(1, 0, 2)
            out["EF_" + cf] = np.ascontiguousarray(EF.reshape(128, 512)).astype(bf)
            out["TWF_" + cf] = np.concatenate([twf.real, twf.imag], axis=1).astype(np.float32)
    return out


def kernel(x_prompt, x_sample, c_prompt, c_sample, ada_w, ada_b, norm1_g, w_in, hy_short_w, hy_short_b,
           hy_pos_w1, hy_pos_b1, hy_sin_freq, hy_pos_w2, hy_pos_b2, hy_pos_w3, hy_decay, hy_skip, hy_out_g,
           attn_out_g, w_out, norm2_g, ffn_w_gate, ffn_w_up, ffn_conv_w, ffn_conv_b, ffn_w_down, final_g):
    f = lambda a: np.ascontiguousarray(np.asarray(a, dtype=np.float32))
    if "nc" not in _CACHE:
        _CACHE["nc"] = build_program()
    nc, nops = _CACHE["nc"]
    x_prompt, x_sample = f(x_prompt), f(x_sample)
    xs = x_sample[0]
    pc = lambda v: f(np.asarray(v).reshape(-1, 128).T)
    common = {
        "ada_w": f(ada_w[0]), "ada_b": pc(ada_b[0]), "n1g": pc(norm1_g[0]), "n2g": pc(norm2_g[0]),
        "fgb": f(np.broadcast_to(np.asarray(final_g)[None, :], (128, DM))),
        "w_in": f(w_in[0]), "w_out": f(w_out[0]), "wg": f(ffn_w_gate[0]), "wu": f(ffn_w_up[0]), "wd": f(ffn_w_down[0]),
        "fcw": f(np.asarray(ffn_conv_w[0]).reshape(3, NFF, 128).transpose(2, 1, 0)),
        "fcb": pc(ffn_conv_b[0]), "hyg": pc(hy_out_g[0]),
        "agb": f(np.broadcast_to(np.asarray(attn_out_g[0])[None, :], (128, HW))),
        "ident": np.eye(128, dtype=np.float32), "masks": build_masks(),
        "xsf": xs,
        "hw1": f(hy_pos_w1[0]), "hb1": f(np.asarray(hy_pos_b1[0]).reshape(64, 1)), "hsf": f(np.asarray(hy_sin_freq[0]).T),
        "hw2": f(hy_pos_w2[0]), "hb2": f(np.asarray(hy_pos_b2[0]).reshape(64, 1)), "hw3": f(hy_pos_w3[0]),
        "hdec": pc(np.asarray(hy_decay[0]).reshape(-1)),
        "hsw": f(np.asarray(hy_short_w[0]).reshape(3, 12, 128).transpose(2, 1, 0)), "hsb": pc(hy_short_b[0]),
        "hskip": f(np.broadcast_to(np.asarray(hy_skip[0]).reshape(1, 1024), (128, 1024))),
    }
    common.update(fft_consts())
    in_maps = []
    for c in range(NCORE):
        lo = CH * c - 128 * QT0_S - 0
        lo = CH * c - 128 * OT0_S
        win = np.zeros((WT_S * 128, DM), np.float32)
        valid = np.zeros((WT_S * 128,), np.float32)
        a, b = max(lo, 0), min(lo + WT_S * 128, LS)
        win[a - lo:b - lo] = xs[a:b]
        valid[a - lo:b - lo] = 1.0
        edge = np.zeros((128, 2), np.float32)
        edge[:, 0] = 1.0 if c > 0 else 0.0
        edge[:, 1] = 1.0 if c < NCORE - 1 else 0.0
        cc = np.concatenate([np.asarray(c_prompt[NSEQ * c:NSEQ * (c + 1)]), np.asarray(c_sample)], axis=0)
        m = dict(common)
        m.update({
            "xp": x_prompt[NSEQ * c:NSEQ * (c + 1)].reshape(NSEQ * T, DM),
            "xsw": win, "flags": f(valid.reshape(WT_S, 128).T), "edge": edge, "cc": f(cc.T),
        })
        sel = np.zeros((128, 18), np.float32)
        for r in range(18):
            gt_ = 16 * c - 1 + r
            if 0 <= gt_ < 128:
                sel[gt_, r] = 1.0
        m["sel"] = sel.astype(ml_dtypes.bfloat16)
        m["selr"] = np.zeros((128, 128), ml_dtypes.bfloat16)
        in_maps.append(m)
    res = run_bass_kernel_spmd(nc, in_maps, core_ids=list(range(NCORE)))
    yp = np.concatenate([np.asarray(r["yp"]).reshape(NSEQ, T, DM) for r in res.results], axis=0)
    ys = np.concatenate([np.asarray(r["ys"]) for r in res.results], axis=0).reshape(1, LS, DM)
    return (yp.astype(np.float32), ys.astype(np.float32))
```

```python
import contextlib
import math
import numpy as np
import ml_dtypes
import concourse.bass as bass
import concourse.mybir as mybir
from concourse.bass_utils import run_bass_kernel_spmd

F32 = mybir.dt.float32
BF16 = mybir.dt.bfloat16
AF = mybir.ActivationFunctionType
ALU = mybir.AluOpType

ENGS = ["pe", "act", "dve", "pool", "sp"]
N_DMA_SEMS = 18
DMA_POOLS = {"sp": list(range(0, 10)), "actq": list(range(10, 16)), "poolq": list(range(16, 18))}


def _mk(name, *args, **kw):
    def f(e):
        return getattr(e, name)(*args, **kw)
    return f


class Buf:
    def __init__(self, name, t):
        self.name = name
        self.t = t
        self.last_write = None
        self.reads = []

    def __getitem__(self, k):
        return self.t[k]


class Op:
    __slots__ = ("eng", "fn", "deps", "is_dma", "dma_sem", "dma_val", "marked", "cnt")

    def __init__(self, eng, fn, is_dma):
        self.eng = eng
        self.fn = fn
        self.deps = []
        self.is_dma = is_dma
        self.dma_sem = None
        self.dma_val = 0
        self.marked = False
        self.cnt = 0


class Sched:
    def __init__(self, nc):
        self.nc = nc
        self.ops = []
        self.stack = contextlib.ExitStack()
        self.n_dma = {q: 0 for q in DMA_POOLS}
        self.dma_last = [None] * N_DMA_SEMS
        self.dma_cnt = [0] * N_DMA_SEMS
        self.last_on = {e: None for e in ENGS}
        self.barrier_deps = set()
        self.pools = [self.stack]

    def push_pool(self):
        st = contextlib.ExitStack()
        self.pools.append(st)
        return st

    def pop_pool(self):
        self.barrier()
        st = self.pools.pop()
        st.close()

    def barrier(self):
        deps = set(self.barrier_deps)
        for e in ENGS:
            if self.last_on[e] is not None:
                deps.add(self.last_on[e])
        for s in range(N_DMA_SEMS):
            if self.dma_last[s] is not None:
                deps.add(self.dma_last[s])
        self.barrier_deps = deps

    def sbuf(self, name, shape, dtype):
        t = self.pools[-1].enter_context(self.nc.sbuf_tensor(name, list(shape), dtype))
        return Buf(name, t)

    def psum(self, name, shape, dtype=F32):
        t = self.pools[-1].enter_context(self.nc.psum_tensor(name, list(shape), dtype))
        return Buf(name, t)

    def dram(self, name, shape, dtype, kind="Internal"):
        t = self.nc.dram_tensor(name, list(shape), dtype, kind=kind)
        return Buf(name, t.ap())

    def op(self, eng, fn, reads=(), writes=(), acc=False):
        is_dma = eng in ("sp", "actq", "poolq")
        real_eng = {"actq": "act", "poolq": "pool"}.get(eng, eng)
        o = Op(real_eng, fn, is_dma)
        oid = len(self.ops)
        deps = set(self.barrier_deps)
        for b in reads:
            if b.last_write is not None:
                deps.add(b.last_write)
        for b in writes:
            if b.last_write is not None:
                lw = self.ops[b.last_write]
                if is_dma or lw.is_dma or lw.eng != real_eng:
                    deps.add(b.last_write)
            for r in b.reads:
                ro = self.ops[r]
                if is_dma or ro.is_dma or ro.eng != real_eng:
                    deps.add(r)
        if is_dma:
            pool_ = DMA_POOLS[eng]
            s = pool_[self.n_dma[eng] % len(pool_)]
            self.n_dma[eng] += 1
            if self.dma_last[s] is not None:
                deps.add(self.dma_last[s])
            self.dma_last[s] = oid
            self.dma_cnt[s] += 1
            o.dma_sem = s
            o.dma_val = 16 * self.dma_cnt[s]
        if real_eng == "pe" and not is_dma:
            deps = {d for d in deps if not (self.ops[d].eng == "pe" and not self.ops[d].is_dma)}
        o.deps = sorted(deps)
        self.ops.append(o)
        self.last_on[real_eng] = oid
        for b in reads:
            if not is_dma:
                b.reads = [r for r in b.reads if self.ops[r].is_dma or self.ops[r].eng != real_eng]
            b.reads.append(oid)
        for b in writes:
            b.last_write = oid
            b.reads = []
        return oid

    def dma(self, out, in_, reads=(), writes=(), q="sp", **kw):
        return self.op(q, _mk("dma_start", out=out, in_=in_, **kw), reads, writes)

    def emit(self, final_wait_bufs=()):
        nc = self.nc
        ops = self.ops
        final_deps = set()
        for b in final_wait_bufs:
            if b.last_write is not None:
                final_deps.add(b.last_write)
        for s in range(N_DMA_SEMS):
            if self.dma_last[s] is not None:
                final_deps.add(self.dma_last[s])
        for o in ops:
            for d in o.deps:
                ops[d].marked = True
        for d in final_deps:
            ops[d].marked = True
        cnt = {e: 0 for e in ENGS}
        for o in ops:
            if not o.is_dma:
                if o.marked:
                    cnt[o.eng] += 1
                o.cnt = cnt[o.eng]
        sems = {e: self.stack.enter_context(nc.semaphore("s_" + e)) for e in ENGS}
        dsems = [self.stack.enter_context(nc.semaphore("d_%d" % i)) for i in range(N_DMA_SEMS)]

        def tok(d):
            od = ops[d]
            if od.is_dma:
                return ("d", od.dma_sem), od.dma_val
            return ("e", od.eng), od.cnt

        streams = {e: [] for e in ENGS}
        seen = {e: {} for e in ENGS}
        for o in ops:
            waits = {}
            for d in o.deps:
                k, v = tok(d)
                if seen[o.eng].get(k, 0) >= v:
                    continue
                if waits.get(k, 0) < v:
                    waits[k] = v
            for k, v in waits.items():
                seen[o.eng][k] = v
            streams[o.eng].append((o, sorted(waits.items())))
        fw = {}
        for d in final_deps:
            k, v = tok(d)
            if fw.get(k, 0) < v:
                fw[k] = v

        def semof(k):
            return dsems[k[1]] if k[0] == "d" else sems[k[1]]

        def run(eng_name, e):
            for o, waits in streams[eng_name]:
                for k, v in waits:
                    e.wait_ge(semof(k), v)
                ins = o.fn(e)
                if o.is_dma:
                    ins.then_inc(dsems[o.dma_sem], 16)
                elif o.marked:
                    ins.then_inc(sems[eng_name], 1)
            if eng_name == "sp":
                for k, v in sorted(fw.items()):
                    e.wait_ge(semof(k), v)

        with nc.Block() as block:
            @block.sync
            def _(e):
                run("sp", e)

            @block.tensor
            def _(e):
                run("pe", e)

            @block.scalar
            def _(e):
                run("act", e)

            @block.vector
            def _(e):
                run("dve", e)

            @block.gpsimd
            def _(e):
                run("pool", e)
        while self.pools:
            self.pools.pop().close()
        return {e: len(streams[e]) for e in ENGS}


DM = 1024
T = 2048
NSEQ = 4
LS = 16384
NCORE = 8
CH = 2048
HW = 512
NH = 8
HD = 64
DFF = 2816
NFF = 22
EPS = 1e-6
WT_S = 34
QT0_S, QT1_S = 8, 26
OT0_S, OT1_S = 9, 25
SEG_WT = [16, 16, 16, 16, WT_S]
SEG_Q = [(0, 16)] * 4 + [(QT0_S, QT1_S)]
SEG_O = [(0, 16)] * 4 + [(OT0_S, OT1_S)]
HY_STUB = False

SLOPES = [2.0 ** (-8.0 * (h + 1) / NH) for h in range(NH)]


def head_deltas(h):
    dmax = min(8, int((30.0 / SLOPES[h] - 1.0) // 128) + 1)
    return list(range(-dmax, dmax + 1))


MASK_OFF = {}
_n = 0
for _h in range(NH):
    for _d in head_deltas(_h):
        MASK_OFF[(_h, _d)] = _n
        _n += 1
N_MASK = _n


def build_masks():
    m = np.zeros((128, N_MASK, 128), np.float32)
    k = np.arange(128)[:, None]
    q = np.arange(128)[None, :]
    for h in range(NH):
        for d in head_deltas(h):
            o = 128 * d + k - q
            a = np.abs(o)
            mult = (a <= 64).astype(np.float64) + ((o % 4 == 0) & (a <= 256)) + ((o % 16 == 0) & (a <= 1024))
            m[:, MASK_OFF[(h, d)], :] = mult * np.exp(-SLOPES[h] * a)
    return m.astype(ml_dtypes.bfloat16)


def build_program():
    nc = bass.Bass("TRN2", target_bir_lowering=False)
    S = Sched(nc)
    ein = lambda name, shape, dt=F32: S.dram(name, shape, dt, kind="ExternalInput")
    xp = ein("xp", [NSEQ * T, DM])
    xsw = ein("xsw", [WT_S * 128, DM])
    flags_d = ein("flags", [128, WT_S])
    edge_d = ein("edge", [128, 2])
    cc_d = ein("cc", [DM, 5])
    ada_w_d = ein("ada_w", [DM, 6 * DM])
    ada_b_d = ein("ada_b", [128, 48])
    n1g_d = ein("n1g", [128, 8])
    n2g_d = ein("n2g", [128, 8])
    fg_d = ein("fgb", [128, DM])
    w_in_d = ein("w_in", [DM, 3072])
    w_out_d = ein("w_out", [DM, DM])
    wg_d = ein("wg", [DM, DFF])
    wu_d = ein("wu", [DM, DFF])
    wd_d = ein("wd", [DFF, DM])
    fcw_d = ein("fcw", [128, NFF, 3])
    fcb_d = ein("fcb", [128, NFF])
    hyg_d = ein("hyg", [128, 4])
    agb_d = ein("agb", [128, HW])
    ident_d = ein("ident", [128, 128])
    masks_d = ein("masks", [128, N_MASK, 128], BF16)
    xsf = ein("xsf", [LS, DM])
    hw1_d = ein("hw1", [33, 64]); hb1_d = ein("hb1", [64, 1]); hsf_d = ein("hsf", [64, 2])
    hw2_d = ein("hw2", [64, 64]); hb2_d = ein("hb2", [64, 1]); hw3_d = ein("hw3", [64, 2048])
    hdec_d = ein("hdec", [128, 16]); hsw_d = ein("hsw", [128, 12, 3]); hsb_d = ein("hsb", [128, 12])
    hskip_d = ein("hskip", [128, 1024])
    feat_d = {"p": ein("feat_p", [33, 2, T]), "s": ein("feat_s", [33, 2, LS])}
    tn_d = {"p": ein("tn_p", [128, 2, T]), "s": ein("tn_s", [128, 2, LS])}
    fc_d = {}
    for cf, P in (("p", 64), ("s", 128)):
        fc_d[cf] = dict(E1=ein("E1_" + cf, [128, 256], BF16), TW1=ein("TW1_" + cf, [P, 256]), W=ein("W_" + cf, [P, 4 * P], BF16),
                        V=ein("V_" + cf, [P, 6 * P], BF16), TW2=ein("TW2_" + cf, [128, 2 * P]), G=ein("G_" + cf, [128, 384], BF16),
                        EF=ein("EF_" + cf, [128, 128] if cf == "p" else [128, 512], BF16),
                        TWF=ein("TWF_" + cf, [P, 128] if cf == "p" else [P, 256]))
    sel_d = ein("sel", [128, 18], BF16)
    selr_d = ein("selr", [128, 128], BF16)
    yp_d = S.dram("yp", [NSEQ * T, DM], F32, kind="ExternalOutput")
    ys_d = S.dram("ys", [CH, DM], F32, kind="ExternalOutput")
    WINF = S.dram("WINF", [24, 128, 8, 128], BF16)
    WV = S.dram("WV", [128, 8, 512], BF16)
    WGU = S.dram("WGU", [NFF, 128, 2, 8, 128], BF16)
    WD = S.dram("WD", [NFF, 128, DM], BF16)
    WO = S.dram("WO", [128, 8, DM], BF16)
    SEG_TOK = [T] * 4 + [(OT1_S - OT0_S + 2) * 128]
    YH = [S.dram("YH%d" % s, [HW, SEG_TOK[s]], BF16) for s in range(5)]
    AT = [S.dram("AT%d" % s, [HW, SEG_TOK[s]], BF16) for s in range(5)]
    UR = {"p": S.dram("UR_p", [1536, NSEQ, T + 2], BF16), "s": S.dram("UR_s", [1536, 1, LS + 2], BF16)}
    UC = {"p": S.dram("UC_p", [1536, NSEQ * T], BF16), "s": S.dram("UC_s", [1536, LS], BF16)}
    HF = {"p": S.dram("HF_p", [1024, 2 * T], BF16), "s": S.dram("HF_s", [1024, 2 * LS], BF16)}
    H2S = {"p": S.dram("H2S_p", [64, 2, T], BF16), "s": S.dram("H2S_s", [64, 2, LS], BF16)}

    ident_f = S.sbuf("ident_f", [128, 128], F32)
    ident_b = S.sbuf("ident_b", [128, 128], BF16)
    ones_f = S.sbuf("ones_f", [128, 128], F32)
    ones_b = S.sbuf("ones_b", [128, 128], BF16)
    modT = S.sbuf("modT", [128, 48, 5], F32)
    SC1 = S.sbuf("SC1", [128, 8, 5], F32)
    SC2 = S.sbuf("SC2", [128, 8, 5], F32)
    fgb = S.sbuf("fgb_t", [128, DM], F32)
    fcw = S.sbuf("fcw_t", [128, NFF, 3], F32)
    fcb = S.sbuf("fcb_t", [128, NFF], F32)
    hyg = S.sbuf("hyg_t", [128, 4], F32)
    agb = S.sbuf("agb_t", [128, HW], F32)
    edge = S.sbuf("edge_t", [128, 2], F32)
    epsb = S.sbuf("epsb", [128, 1], F32)

    S.dma(ident_f[:], ident_d[:], [ident_d], [ident_f])
    S.op("dve", _mk("tensor_copy", out=ident_b[:], in_=ident_f[:]), [ident_f], [ident_b])
    S.op("pool", _mk("memset", ones_f[:], 1.0), [], [ones_f])
    S.op("pool", _mk("memset", ones_b[:], 1.0), [], [ones_b])
    S.op("pool", _mk("memset", epsb[:], EPS), [], [epsb])
    for dst, src in ((fgb, fg_d), (fcw, fcw_d), (fcb, fcb_d), (hyg, hyg_d), (agb, agb_d), (edge, edge_d)):
        S.dma(dst[:], src[:], [src], [dst])

    S.push_pool()
    stg = [S.sbuf("stg%d" % i, [128, 3072], F32) for i in range(2)]
    stb = [S.sbuf("stb%d" % i, [128, 3072], BF16) for i in range(2)]
    cast_i = [0]

    def cast_rows(src_ap, ncols, writes):
        i = cast_i[0] % 2
        cast_i[0] += 1
        st, sb = stg[i], stb[i]
        S.dma(st[:, 0:ncols], src_ap, [], [st])
        eng = "dve" if i == 0 else "pool"
        S.op(eng, _mk("tensor_copy", out=sb[:, 0:ncols], in_=st[:, 0:ncols]), [st], [sb])
        for dbuf, dst_ap, src_view in writes:
            S.dma(dst_ap, src_view(sb), [sb], [dbuf])

    for k in range(8):
        cast_rows(w_in_d[128 * k:128 * (k + 1), :], 3072, [
            (WINF, WINF[:, :, k, :].rearrange("j p m -> p j m"), lambda sb: sb[:, 0:3072].rearrange("p (j m) -> p j m", m=128)),
            (WV, WV[:, k, :], lambda sb: sb[:, 2560:3072]),
        ])
        cast_rows(wg_d[128 * k:128 * (k + 1), :], DFF, [
            (WGU, WGU[:, :, 0, k, :].rearrange("j p m -> p j m"), lambda sb: sb[:, 0:DFF].rearrange("p (j m) -> p j m", m=128)),
        ])
        cast_rows(wu_d[128 * k:128 * (k + 1), :], DFF, [
            (WGU, WGU[:, :, 1, k, :].rearrange("j p m -> p j m"), lambda sb: sb[:, 0:DFF].rearrange("p (j m) -> p j m", m=128)),
        ])
        cast_rows(w_out_d[128 * k:128 * (k + 1), :], DM, [
            (WO, WO[:, k, :], lambda sb: sb[:, 0:DM]),
        ])
    for j in range(NFF):
        cast_rows(wd_d[128 * j:128 * (j + 1), :], DM, [
            (WD, WD[j], lambda sb: sb[:, 0:DM]),
        ])

    cT = S.sbuf("cT", [128, 8, 5], F32)
    scT = S.sbuf("scT", [128, 8, 5], F32)
    abT = S.sbuf("abT", [128, 48], F32)
    n1g = S.sbuf("n1g_t", [128, 8], F32)
    n2g = S.sbuf("n2g_t", [128, 8], F32)
    S.dma(cT[:], cc_d[:].rearrange("(k p) b -> p k b", p=128), [cc_d], [cT])
    S.dma(abT[:], ada_b_d[:], [ada_b_d], [abT])
    S.dma(n1g[:], n1g_d[:], [n1g_d], [n1g])
    S.dma(n2g[:], n2g_d[:], [n2g_d], [n2g])
    S.op("act", _mk("activation", out=scT[:], in_=cT[:], func=AF.Silu), [cT], [scT])
    awt = [S.sbuf("awt%d" % i, [128, 8, 128], F32) for i in range(2)]
    psm = S.psum("psm", [128, 512], F32)
    for j in range(48):
        a = awt[j % 2]
        S.dma(a[:], ada_w_d[:, 128 * j:128 * (j + 1)].rearrange("(k p) m -> p k m", p=128), [], [a])
        for k in range(8):
            S.op("pe", _mk("matmul", psm[:, 5 * j:5 * j + 5], lhsT=a[:, k, :], rhs=scT[:, k, :],
                                                      start=(k == 0), stop=(k == 7)), [a, scT], [psm], acc=(k > 0))
    S.op("dve", _mk("tensor_tensor", out=modT[:], in0=psm[:, 0:240].rearrange("p (j s) -> p j s", s=5),
                                          in1=abT[:].unsqueeze(2).broadcast_to([128, 48, 5]), op=ALU.add), [psm, abT], [modT])
    for (SC, base, ng) in ((SC1, 8, n1g), (SC2, 32, n2g)):
        S.op("dve", _mk("tensor_scalar", out=SC[:], in0=modT[:, base:base + 8, :], scalar1=1.0, scalar2=None,
                                                                  op0=ALU.add), [modT], [SC])
        S.op("dve", _mk("tensor_tensor", out=SC[:], in0=SC[:], in1=ng[:].unsqueeze(2).broadcast_to([128, 8, 5]),
                                                           op=ALU.mult), [SC, ng], [SC])
    S.pop_pool()

    def rstd_from_ss(ss, rs, n, eng_recip="dve"):
        S.op("act", _mk("activation", out=rs[:], in_=ss[:], func=AF.Sqrt, bias=epsb[:], scale=1.0 / n), [ss, epsb], [rs])
        S.op("dve", _mk("reciprocal", out=rs[:], in_=rs[:]), [rs], [rs])

    S.push_pool()
    masks = S.sbuf("masks_t", [128, N_MASK, 128], BF16)
    S.dma(masks[:], masks_d[:], [masks_d], [masks])
    NQMAX = QT1_S - QT0_S
    QT = S.sbuf("QT", [128, 4, NQMAX * 128], BF16)
    KT = S.sbuf("KT", [128, 4, WT_S * 128], BF16)
    VP = S.sbuf("VP", [128, WT_S, NH, HD + 1], BF16)
    flg = S.sbuf("flg", [128, WT_S], F32)
    wv_t = S.sbuf("wv_t", [128, 8, 512], BF16)
    S.dma(wv_t[:].rearrange("p k m -> p (k m)"), WV[:].rearrange("p k m -> p (k m)"), [WV], [wv_t])
    wqk = [S.sbuf("wqk%d" % i, [128, 8, 128], BF16) for i in range(3)]
    xg = [S.sbuf("xg%d" % i, [128, DM], F32) for i in range(3)]
    xn = [S.sbuf("xn%d" % i, [128, DM], BF16) for i in range(2)]
    hTg = [S.sbuf("hTg%d" % i, [128, 8, 512], BF16) for i in range(2)]
    ss_t = [S.sbuf("ss%d" % i, [128, 1], F32) for i in range(4)]
    rs_t = [S.sbuf("rs%d" % i, [128, 1], F32) for i in range(4)]
    junk = S.sbuf("junk", [128, DM], BF16)
    ptrs = [S.psum("ptr%d" % i, [128, 1024], BF16) for i in range(2)]
    ptr = ptrs[0]
    pp = [S.psum("pp%d" % i, [128, 512], F32) for i in range(2)]
    psc = [S.psum("psc%d" % i, [128, 512], F32) for i in range(2)]
    po = [S.psum("po%d" % i, [128, 512], F32) for i in range(2)]
    pT = [S.sbuf("pT%d" % i, [128, 512], BF16) for i in range(3)]
    o_t = [S.sbuf("o_t%d" % i, [128, HW], F32) for i in range(2)]
    on_t = [S.sbuf("on_t%d" % i, [128, HW], BF16) for i in range(2)]
    rden = [S.sbuf("rden%d" % i, [128, NH], F32) for i in range(2)]
    aTt = [S.sbuf("aTt%d" % i, [128, 4, 512], BF16) for i in range(2)]
    cnt = {"g": 0, "t": 0, "pp": 0, "psc": 0, "pT": 0, "q": 0, "x": 0, "w": 0}

    ub = [S.sbuf("ub%d" % i, [128, 512], BF16) for i in range(2)]
    zpad = S.sbuf("zpad", [128, 12, 8], BF16)
    S.op("pool", _mk("memset", zpad[:], 0.0), [], [zpad])

    def hy_proj(h_t, N, cf, sq_, tok0):
        for j in range(12):
            w_ = wqk[cnt["w"] % 3]
            cnt["w"] += 1
            S.dma(w_[:].rearrange("p k m -> p (k m)"), WINF[j].rearrange("p k m -> p (k m)"), [WINF], [w_])
            p_ = pp[cnt["pp"] % 2]
            cnt["pp"] += 1
            for k in range(8):
                S.op("pe", _mk("matmul", p_[:, 0:N], lhsT=w_[:, k, :], rhs=h_t[:, k, 0:N], start=(k == 0), stop=(k == 7)), [w_, h_t], [p_])
            u_ = ub[cnt["pp"] % 2]
            S.op("act", _mk("activation", out=u_[:, 0:N], in_=p_[:, 0:N], func=AF.Copy), [p_], [u_])
            S.dma(UR[cf][128 * j:128 * (j + 1), sq_, 1 + tok0:1 + tok0 + N], u_[:, 0:N], [u_], [UR[cf]], q="actq")

    def norm_to_hT(x_t, h_t, t, seg):
        ti = cnt["t"] % 4
        cnt["t"] += 1
        ss, rs, xnb = ss_t[ti], rs_t[ti], xn[ti % 2]
        S.op("act", _mk("activation", out=junk[:], in_=x_t[:], func=AF.Square, accum_out=ss[:]), [x_t], [junk, ss])
        rstd_from_ss(ss, rs, DM)
        S.op("dve", _mk("tensor_scalar", out=xnb[:], in0=x_t[:], scalar1=rs[:], scalar2=None, op0=ALU.mult), [x_t, rs], [xnb])
        ptr_ = ptrs[ti % 2]
        for k in range(8):
            S.op("pe", _mk("transpose", ptr_[:, 128 * k:128 * (k + 1)], xnb[:, 128 * k:128 * (k + 1)], ident_b[:]), [xnb, ident_b], [ptr_])
        for k in range(8):
            if k % 2 == 0:
                S.op("act", _mk("activation", out=h_t[:, k, 128 * t:128 * (t + 1)], in_=ptr_[:, 128 * k:128 * (k + 1)],
                                func=AF.Identity, scale=SC1[:, k, seg:seg + 1], bias=modT[:, k, seg:seg + 1]), [ptr_, SC1, modT], [h_t])
            else:
                S.op("dve", _mk("tensor_scalar", out=h_t[:, k, 128 * t:128 * (t + 1)], in0=ptr_[:, 128 * k:128 * (k + 1)],
                                scalar1=SC1[:, k, seg:seg + 1], scalar2=modT[:, k, seg:seg + 1], op0=ALU.mult, op1=ALU.add),
                     [ptr_, SC1, modT], [h_t])

    if not HY_STUB:
        for cf, nsq, L_ in (("p", NSEQ, T), ("s", 1, LS)):
            for sq_ in range(nsq):
                for col in (0, L_ + 1):
                    S.dma(UR[cf][:, sq_, col:col + 1].rearrange("(j p) o -> p j o", p=128), zpad[:, :, 0:1], [zpad], [UR[cf]],
                          allow_slow_non_contiguous=True)
        for g0 in range(0, LS // 128, 4):
            h_t = hTg[cnt["g"] % 2]
            cnt["g"] += 1
            for t in range(4):
                x_t = xg[cnt["x"] % 3]
                cnt["x"] += 1
                S.dma(x_t[:], xsf[128 * (g0 + t):128 * (g0 + t + 1), :], [xsf], [x_t])
                norm_to_hT(x_t, h_t, t, 4)
            hy_proj(h_t, 512, "s", 0, 128 * g0)

    for seg in range(5):
        nwt = SEG_WT[seg]
        q0, q1 = SEG_Q[seg]
        xsrc = xp if seg < 4 else xsw
        xoff = seg * T if seg < 4 else 0
        if seg < 4:
            S.op("pool", _mk("memset", flg[:], 1.0), [], [flg])
        else:
            S.dma(flg[:], flags_d[:], [flags_d], [flg])
        for g0 in range(0, nwt, 4):
            nt = min(4, nwt - g0)
            h_t = hTg[cnt["g"] % 2]
            cnt["g"] += 1
            for t in range(nt):
                x_t = xg[cnt["x"] % 3]
                cnt["x"] += 1
                r0 = xoff + 128 * (g0 + t)
                S.dma(x_t[:], xsrc[r0:r0 + 128, :], [xsrc], [x_t])
                ti = cnt["t"] % 4
                cnt["t"] += 1
                ss, rs, xnb = ss_t[ti], rs_t[ti], xn[ti % 2]
                S.op("act", _mk("activation", out=junk[:], in_=x_t[:], func=AF.Square, accum_out=ss[:]),
                     [x_t], [junk, ss])
                rstd_from_ss(ss, rs, DM)
                S.op("dve", _mk("tensor_scalar", out=xnb[:], in0=x_t[:], scalar1=rs[:], scalar2=None,
                                                                           op0=ALU.mult), [x_t, rs], [xnb])
                ptr_ = ptrs[ti % 2]
                for k in range(8):
                    S.op("pe", _mk("transpose", ptr_[:, 128 * k:128 * (k + 1)], xnb[:, 128 * k:128 * (k + 1)], ident_b[:]),
                         [xnb, ident_b], [ptr_])
                for k in range(8):
                    if k % 2 == 0:
                        S.op("act", _mk("activation", out=h_t[:, k, 128 * t:128 * (t + 1)], in_=ptr_[:, 128 * k:128 * (k + 1)],
                                        func=AF.Identity, scale=SC1[:, k, seg:seg + 1], bias=modT[:, k, seg:seg + 1]), [ptr_, SC1, modT], [h_t])
                    else:
                        S.op("dve", _mk("tensor_scalar", out=h_t[:, k, 128 * t:128 * (t + 1)], in0=ptr_[:, 128 * k:128 * (k + 1)],
                                        scalar1=SC1[:, k, seg:seg + 1], scalar2=modT[:, k, seg:seg + 1], op0=ALU.mult, op1=ALU.add),
                             [ptr_, SC1, modT], [h_t])
            N = 128 * nt
            tok0 = 128 * g0
            qa, qb = max(g0, q0), min(g0 + nt, q1)
            for j in range(8):
                if j < 4 and qa >= qb:
                    continue
                w_ = wqk[cnt["w"] % 3]
                cnt["w"] += 1
                S.dma(w_[:].rearrange("p k m -> p (k m)"), WINF[12 + j].rearrange("p k m -> p (k m)"), [WINF], [w_])
                p_ = pp[cnt["pp"] % 2]
                cnt["pp"] += 1
                if j < 4:
                    ca, cb = 128 * (qa - g0), 128 * (qb - g0)
                else:
                    ca, cb = 0, N
                for k in range(8):
                    S.op("pe", _mk("matmul",
                        p_[:, 0:cb - ca], lhsT=w_[:, k, :], rhs=h_t[:, k, ca:cb], start=(k == 0), stop=(k == 7)), [w_, h_t], [p_])
                if j < 4:
                    S.op("act", _mk("activation",
                        out=QT[:, j, 128 * (qa - q0):128 * (qa - q0) + cb - ca], in_=p_[:, 0:cb - ca], func=AF.Copy, scale=0.125), [p_], [QT])
                else:
                    S.op("dve", _mk("tensor_copy", out=KT[:, j - 4, tok0:tok0 + N], in_=p_[:, 0:N]), [p_], [KT])
            if seg < 4 and not HY_STUB:
                hy_proj(h_t, N, "p", seg, tok0)
            for t in range(nt):
                p_ = pp[cnt["pp"] % 2]
                cnt["pp"] += 1
                for k in range(8):
                    S.op("pe", _mk("matmul", p_[:, :], lhsT=h_t[:, k, 128 * t:128 * (t + 1)], rhs=wv_t[:, k, :],
                                                                 start=(k == 0), stop=(k == 7)), [wv_t, h_t], [p_])
                wt = g0 + t
                S.op("dve", _mk("tensor_scalar", out=VP[:, wt, :, 0:HD], in0=p_[:, :].rearrange("p (h e) -> p h e", e=HD),
                                                                   scalar1=flg[:, wt:wt + 1], scalar2=None, op0=ALU.mult), [p_, flg], [VP])
                S.op("pool", _mk("tensor_copy", out=VP[:, wt, :, HD:HD + 1],
                                                           in_=flg[:, wt:wt + 1].unsqueeze(1).broadcast_to([128, NH, 1])), [flg], [VP])
        for qt in range(q0, q1):
            qi = cnt["q"] % 2
            cnt["q"] += 1
            qc = 128 * (qt - q0)
            chunks = []
            for h in range(NH):
                kts = [(d, qt + d) for d in head_deltas(h) if 0 <= qt + d < nwt]
                for c0 in range(0, len(kts), 4):
                    chunks.append((h, c0, kts[c0:c0 + 4], len(kts)))

            def emit_scores(ch):
                h, c0, blk, nk = ch
                hp, hb = h // 2, 64 * (h % 2)
                nb = len(blk)
                sc = psc[cnt["psc"] % 2]
                cnt["psc"] += 1
                for bi, (d, kt) in enumerate(blk):
                    S.op("pe", _mk("matmul", sc[:, 128 * bi:128 * (bi + 1)], lhsT=KT[hb:hb + 64, hp, 128 * kt:128 * (kt + 1)],
                                   rhs=QT[hb:hb + 64, hp, qc:qc + 128], start=True, stop=True), [KT, QT], [sc])
                pt_ = pT[cnt["pT"] % 3]
                cnt["pT"] += 1
                S.op("act", _mk("activation", out=pt_[:, 0:128 * nb], in_=sc[:, 0:128 * nb], func=AF.Exp), [sc], [pt_])
                m0 = MASK_OFF[(h, blk[0][0])]
                S.op("dve", _mk("tensor_tensor", out=pt_[:, 0:128 * nb], in0=pt_[:, 0:128 * nb],
                                in1=masks[:, m0:m0 + nb, :].rearrange("p b q -> p (b q)"), op=ALU.mult), [pt_, masks], [pt_])
                return pt_

            def emit_pv(ch, pt_):
                h, c0, blk, nk = ch
                pob, hs = po[h // 4], h % 4
                for bi, (d, kt) in enumerate(blk):
                    first = (c0 == 0 and bi == 0)
                    last = (c0 + bi == nk - 1)
                    S.op("pe", _mk("matmul", pob[:, 65 * hs:65 * hs + 65], lhsT=pt_[:, 128 * bi:128 * (bi + 1)], rhs=VP[:, kt, h, :],
                                   start=first, stop=last), [pt_, VP], [pob])

            prev = None
            for ch in chunks:
                pt_ = emit_scores(ch)
                if prev is not None:
                    emit_pv(*prev)
                prev = (ch, pt_)
            emit_pv(*prev)
            ot, ont, rd = o_t[qi], on_t[qi], rden[qi]
            for half in range(2):
                S.op("dve", _mk("tensor_scalar",
                    out=rd[:, 4 * half:4 * half + 4], in0=po[half][:, 0:260].rearrange("p (h e) -> p h e", e=65)[:, :, 64],
                    scalar1=1e-30, scalar2=None, op0=ALU.add), [po[half]], [rd])
            S.op("dve", _mk("reciprocal", out=rd[:], in_=rd[:]), [rd], [rd])
            for half in range(2):
                S.op("dve", _mk("tensor_tensor",
                    out=ot[:, 256 * half:256 * half + 256].rearrange("p (h e) -> p h e", e=64),
                    in0=po[half][:, 0:260].rearrange("p (h e) -> p h e", e=65)[:, :, 0:64],
                    in1=rd[:, 4 * half:4 * half + 4].unsqueeze(2).broadcast_to([128, 4, 64]), op=ALU.mult), [po[half], rd], [ot])
            ti = cnt["t"] % 4
            cnt["t"] += 1
            ss, rs = ss_t[ti], rs_t[ti]
            S.op("act", _mk("activation", out=junk[:, 0:HW], in_=ot[:], func=AF.Square, accum_out=ss[:]), [ot], [junk, ss])
            rstd_from_ss(ss, rs, HW)
            S.op("dve", _mk("scalar_tensor_tensor", out=ont[:], in0=ot[:], scalar=rs[:], in1=agb[:],
                                                                               op0=ALU.mult, op1=ALU.mult), [ot, rs, agb], [ont])
            grp = (qt - q0) // 4
            a_t = aTt[grp % 2]
            for k in range(4):
                S.op("pe", _mk("transpose", ptr[:, 128 * k:128 * (k + 1)], ont[:, 128 * k:128 * (k + 1)], ident_b[:]),
                     [ont, ident_b], [ptr])
            tl = (qt - q0) % 4
            S.op("dve", _mk("tensor_copy",
                out=a_t[:, :, 128 * tl:128 * (tl + 1)], in_=ptr[:, 0:512].rearrange("p (k t) -> p k t", t=128)), [ptr], [a_t])
            if tl == 3 or qt == q1 - 1:
                ntok = 128 * (tl + 1)
                c0 = 128 * (qt - q0 - tl)
                S.dma(AT[seg][:, c0:c0 + ntok].rearrange("(k p) t -> p k t", p=128), a_t[:, :, 0:ntok], [a_t], [AT[seg]], q="actq")
    S.pop_pool()

    if HY_STUB:
        S.push_pool()
        z = S.sbuf("zz", [128, 4, T + 512], BF16)
        S.op("pool", _mk("memset", z[:], 0.0), [], [z])
        for s in range(5):
            S.dma(YH[s][:, :].rearrange("(k p) t -> p k t", p=128), z[:, :, 0:SEG_TOK[s]], [z], [YH[s]])
        S.pop_pool()
    else:
        S.push_pool()
        PI = math.pi
        pAh = [S.psum("pA%d" % i, [128, 512], F32) for i in range(2)]
        pZh = [S.psum("pZ%d" % i, [128, 512], F32) for i in range(2)]
        pCh = [S.psum("pC%d" % i, [128, 512], F32) for i in range(2)]
        pY = [S.psum("pY%d" % i, [128, 512], F32) for i in range(2)]
        pA, pZ, pB = pAh[0], pZh[0], pAh[1]
        hw1 = S.sbuf("t_hw1", [33, 64], F32); hb1 = S.sbuf("t_hb1", [64, 1], F32); hsf = S.sbuf("t_hsf", [64, 2], F32)
        hw2 = S.sbuf("t_hw2", [64, 64], F32); hb2 = S.sbuf("t_hb2", [64, 1], F32)
        hw3f = S.sbuf("t_hw3f", [64, 2048], F32); hw3b = S.sbuf("t_hw3b", [64, 2048], BF16)
        hdec = S.sbuf("t_hdec", [128, 16], F32); negd = S.sbuf("t_negd", [128, 16], F32)
        hsw = S.sbuf("t_hsw", [128, 12, 3], F32); hsb = S.sbuf("t_hsb", [128, 12], F32)
        skb = S.sbuf("t_skb", [128, 1024], F32)
        sel = S.sbuf("t_sel", [128, 18], BF16)
        for dst, src in ((hw1, hw1_d), (hb1, hb1_d), (hsf, hsf_d), (hw2, hw2_d), (hb2, hb2_d), (hw3f, hw3_d), (hdec, hdec_d),
                         (hsw, hsw_d), (hsb, hsb_d), (skb, hskip_d), (sel, sel_d)):
            S.dma(dst[:], src[:], [src], [dst])
        S.op("dve", _mk("tensor_copy", out=hw3b[:], in_=hw3f[:]), [hw3f], [hw3b])
        S.op("dve", _mk("tensor_scalar", out=negd[:], in0=hdec[:], scalar1=-1.0, scalar2=None, op0=ALU.mult), [hdec], [negd])
        S.op("dve", _mk("tensor_tensor", out=negd[:], in0=negd[:], in1=hdec[:], op=ALU.min), [negd, hdec], [negd])
        RNB = {cf: S.sbuf("RNB_" + cf, [128, 2, 512], F32) for cf in ("p", "s")}
        SB = S.sbuf("SBt", [128, 2048], F32)
        dg3 = S.sbuf("dg3", [128, 128], F32)
        S.push_pool()
        ft = [S.sbuf("ft%d" % i, [33, 2, 512], F32) for i in range(2)]
        tnc = [S.sbuf("tnc%d" % i, [128, 2, 512], F32) for i in range(2)]
        a1 = S.sbuf("a1", [64, 512], F32); wtmp = S.sbuf("wtmp", [64, 512], F32)
        h1 = S.sbuf("h1", [64, 512], F32); h2b = S.sbuf("h2b", [64, 512], BF16)
        Et = [S.sbuf("Et%d" % i, [128, 512], F32) for i in range(2)]
        hq = [S.sbuf("hq%d" % i, [128, 512], F32) for i in range(2)]
        hqb = [S.sbuf("hqb%d" % i, [128, 512], BF16) for i in range(2)]
        junkf = S.sbuf("junkf", [128, 512], BF16)
        asum = S.sbuf("asum", [128, 16, 32], F32)
        tot = S.sbuf("tot", [128, 16], F32)

        def sin_layer(ps, bias, sfc, out_t):
            S.op("dve", _mk("tensor_scalar", out=a1[:], in0=ps[0:64, :], scalar1=bias[:, 0:1], scalar2=sfc, op0=ALU.add, op1=ALU.mult),
                 [ps, bias, hsf], [a1])
            S.op("dve", _mk("tensor_scalar", out=wtmp[:], in0=a1[:], scalar1=PI, scalar2=-2.0 * PI, op0=ALU.is_gt, op1=ALU.mult), [a1], [wtmp])
            S.op("dve", _mk("tensor_tensor", out=a1[:], in0=a1[:], in1=wtmp[:], op=ALU.add), [a1, wtmp], [a1])
            S.op("dve", _mk("tensor_scalar", out=wtmp[:], in0=a1[:], scalar1=-PI, scalar2=2.0 * PI, op0=ALU.is_lt, op1=ALU.mult), [a1], [wtmp])
            S.op("dve", _mk("tensor_tensor", out=a1[:], in0=a1[:], in1=wtmp[:], op=ALU.add), [a1, wtmp], [a1])
            S.op("act", _mk("activation", out=out_t[:], in_=a1[:], func=AF.Sin), [a1], [out_t])

        ur = [S.sbuf("ur%d" % i, [128, 2050], BF16) for i in range(2)]
        ct = [S.sbuf("ct%d" % i, [128, 2048], F32) for i in range(2)]
        ucb = [S.sbuf("ucb%d" % i, [128, 2048], BF16) for i in range(2)]
        conv_items = []

        def mk_conv(cf, sq_, L_, pc, j, ci):
            def ld():
                u_ = ur[ci % 2]
                S.dma(u_[:], UR[cf][128 * j:128 * (j + 1), sq_, 2048 * pc:2048 * pc + 2050], [UR[cf]], [u_])

            def f():
                u_, c_, o_ = ur[ci % 2], ct[ci % 2], ucb[ci % 2]
                S.op("pool", _mk("tensor_scalar", out=c_[:], in0=u_[:, 1:2049], scalar1=hsw[:, j, 1:2], scalar2=hsb[:, j:j + 1],
                                 op0=ALU.mult, op1=ALU.add), [u_, hsw, hsb], [c_])
                S.op("dve", _mk("scalar_tensor_tensor", out=c_[:], in0=u_[:, 0:2048], scalar=hsw[:, j, 0:1], in1=c_[:],
                                op0=ALU.mult, op1=ALU.add), [u_, hsw, c_], [c_])
                S.op("dve", _mk("scalar_tensor_tensor", out=o_[:], in0=u_[:, 2:2050], scalar=hsw[:, j, 2:3], in1=c_[:],
                                op0=ALU.mult, op1=ALU.add), [u_, hsw, c_], [o_])
                S.dma(UC[cf][128 * j:128 * (j + 1), sq_ * L_ + 2048 * pc: sq_ * L_ + 2048 * (pc + 1)], o_[:], [o_], [UC[cf]], q="actq")
            return ld, f

        _ci = 0
        _pairs = []
        for cf, nsq, L_ in (("p", NSEQ, T), ("s", 1, LS)):
            for sq_ in range(nsq):
                for pc in range(L_ // 2048):
                    for j in range(12):
                        _ci += 1
                        _pairs.append(mk_conv(cf, sq_, L_, pc, j, _ci))
        _pairs[0][0]()
        for i_, (ld_, f_) in enumerate(_pairs):
            nld = _pairs[i_ + 1][0] if i_ + 1 < len(_pairs) else None
            conv_items.append((lambda f_=f_, nld=nld: (nld() if nld else None, f_())))

        hqb3 = [S.sbuf("hqb3_%d" % i, [128, 512], BF16) for i in range(3)]
        NCH = 4
        ma = [S.sbuf("ma%d" % i, [64, 512], F32) for i in range(NCH)]
        mw = [S.sbuf("mw%d" % i, [64, 512], F32) for i in range(NCH)]
        mh1 = [S.sbuf("mh1_%d" % i, [64, 512], F32) for i in range(NCH)]
        mh2 = [S.sbuf("mh2_%d" % i, [64, 512], BF16) for i in range(NCH)]
        mps = [pAh[0], pAh[1], pZh[0], pZh[1]]
        h2l = [S.sbuf("h2l%d" % i, [64, 2, 512], BF16) for i in range(3)]
        fc = {"ci": 0}

        def wrap_sin(ps, bias, sfc, a_, w_, out_t):
            S.op("dve", _mk("tensor_scalar", out=a_[:], in0=ps[0:64, :], scalar1=bias[:, 0:1], scalar2=sfc, op0=ALU.add, op1=ALU.mult),
                 [ps, bias, hsf], [a_])
            S.op("dve", _mk("tensor_scalar", out=w_[:], in0=a_[:], scalar1=PI, scalar2=-2.0 * PI, op0=ALU.is_gt, op1=ALU.mult), [a_], [w_])
            S.op("dve", _mk("tensor_tensor", out=a_[:], in0=a_[:], in1=w_[:], op=ALU.add), [a_, w_], [a_])
            S.op("dve", _mk("tensor_scalar", out=w_[:], in0=a_[:], scalar1=-PI, scalar2=2.0 * PI, op0=ALU.is_lt, op1=ALU.mult), [a_], [w_])
            S.op("dve", _mk("tensor_tensor", out=a_[:], in0=a_[:], in1=w_[:], op=ALU.add), [a_, w_], [a_])
            S.op("act", _mk("activation", out=out_t[:], in_=a_[:], func=AF.Sin), [a_], [out_t])

        for cf, L_ in (("p", T), ("s", LS)):
            nch = L_ // 512
            for pc0 in range(0, nch, 2):
                jobs = []
                for pc in (pc0, pc0 + 1):
                    f_ = ft[pc % 2]
                    S.dma(f_[:], feat_d[cf][:, :, 512 * pc:512 * (pc + 1)], [], [f_])
                    for d in range(2):
                        jobs.append((pc, d, f_))
                for i, (pc, d, f_) in enumerate(jobs):
                    S.op("pe", _mk("matmul", mps[i][0:64, :], lhsT=hw1[:], rhs=f_[:, d, :], start=True, stop=True), [hw1, f_], [mps[i]])
                for i, (pc, d, f_) in enumerate(jobs):
                    wrap_sin(mps[i], hb1, hsf[:, 0:1], ma[i], mw[i], mh1[i])
                for i, (pc, d, f_) in enumerate(jobs):
                    S.op("pe", _mk("matmul", mps[i][0:64, :], lhsT=hw2[:], rhs=mh1[i][:], start=True, stop=True), [hw2, mh1[i]], [mps[i]])
                for i, (pc, d, f_) in enumerate(jobs):
                    wrap_sin(mps[i], hb2, hsf[:, 1:2], ma[i], mw[i], mh2[i])
                for i, (pc, d, f_) in enumerate(jobs):
                    S.dma(H2S[cf][:, d, 512 * pc:512 * (pc + 1)], mh2[i][:], [mh2[i]], [H2S[cf]], q="actq")
                if conv_items:
                    conv_items.pop(0)()

            def load_chunk(pc):
                h_, t_ = h2l[pc % 3], tnc[pc % 2]
                S.dma(h_[:], H2S[cf][:, :, 512 * pc:512 * (pc + 1)], [H2S[cf]], [h_])
                S.dma(t_[:], tn_d[cf][:, :, 512 * pc:512 * (pc + 1)], [], [t_])

            def rc_front(pc, rc):
                fc["ci"] += 1
                ci = fc["ci"]
                d = (rc // 4) % 2
                p3, E_, q_ = pY[ci % 2], Et[ci % 2], hq[ci % 2]
                h_, t_ = h2l[pc % 3], tnc[pc % 2]
                S.op("pe", _mk("matmul", p3[:], lhsT=hw3b[:, 128 * rc:128 * (rc + 1)], rhs=h_[:, d, :], start=True, stop=True), [hw3b, h_], [p3])
                S.op("act", _mk("activation", out=E_[:], in_=t_[:, d, :], func=AF.Exp, scale=negd[:, rc:rc + 1]), [t_, negd], [E_])
                S.op("dve", _mk("tensor_tensor", out=q_[:], in0=p3[:], in1=E_[:], op=ALU.mult), [p3, E_], [q_])
                return q_, ci

            def rc_back(pc, rc, q_, ci):
                o_, d, cch = rc // 8, (rc // 4) % 2, rc % 4
                qb_ = hqb3[ci % 3]
                S.op("act", _mk("activation", out=junkf[:], in_=q_[:], func=AF.Abs, accum_out=asum[:, rc, pc:pc + 1]), [q_], [junkf, asum])
                S.op("pool", _mk("tensor_copy", out=qb_[:], in_=q_[:]), [q_], [qb_])
                if d == 1 and pc == 0:
                    S.op("pool", _mk("memset", qb_[:, 0:1], 0.0), [], [qb_])
                r0 = 512 * o_ + 128 * cch
                col0 = (0 if d == 1 else L_) + 512 * pc
                S.dma(HF[cf][r0:r0 + 128, col0:col0 + 512], qb_[:], [qb_], [HF[cf]])

            load_chunk(0)
            for pc in range(nch):
                if pc + 1 < nch:
                    load_chunk(pc + 1)
                cur = rc_front(pc, 0)
                for rc in range(16):
                    nxt = rc_front(pc, rc + 1) if rc + 1 < 16 else None
                    rc_back(pc, rc, *cur)
                    cur = nxt
                    if rc % 4 == 3 and conv_items:
                        conv_items.pop(0)()
            S.op("dve", _mk("tensor_reduce", out=tot[:], in_=asum[:, :, 0:nch], op=ALU.add, axis=mybir.AxisListType.X), [asum], [tot])
            for rc in range(16):
                S.op("dve", _mk("tensor_scalar", out=dg3[:], in0=ident_f[:], scalar1=tot[:, rc:rc + 1], scalar2=None, op0=ALU.mult),
                     [ident_f, tot], [dg3])
                S.op("pe", _mk("matmul", pB[:, 128 * (rc % 4):128 * (rc % 4 + 1)], lhsT=ones_f[:], rhs=dg3[:], start=True, stop=True),
                     [ones_f, dg3], [pB])
                if rc % 4 == 3:
                    S.op("act", _mk("activation", out=SB[:, 128 * (rc - 3):128 * (rc + 1)], in_=pB[:], func=AF.Copy), [pB], [SB])
            sbv = SB[:].rearrange("p (o d c) -> p o d c", o=2, d=2)
            S.op("dve", _mk("tensor_tensor", out=RNB[cf][:], in0=sbv[:, :, 0, :], in1=sbv[:, :, 1, :], op=ALU.add), [SB], [RNB[cf]])
            S.op("dve", _mk("reciprocal", out=RNB[cf][:], in_=RNB[cf][:]), [RNB[cf]], [RNB[cf]])

        while conv_items:
            conv_items.pop(0)()
        S.pop_pool()
        GC, NB, NHF = 16, 2, 8
        HR = list(range(NHF))
        Xb = [[S.sbuf("Xb%d_%d" % (h, i), [128, NB, 2, 128], BF16) for i in range(1)] for h in HR]
        Kc = [[S.sbuf("Kc%d%d" % (h, i), [128, NB, 2, 2, 128], F32) for i in range(1)] for h in HR]
        gt = [S.sbuf("gt%d" % h, [128, NB, 128], F32) for h in HR]
        z1 = [S.sbuf("z1_%d" % h, [128, NB, 128], BF16) for h in HR]
        pH = [pAh[0], pAh[1], pZh[0], pZh[1], pCh[0], pCh[1], pY[0], pY[1]]
        pAh = pZh = pCh = pYh = pH
        cmi = [0] * NHF
        xbi = [0] * NHF

        def next_xb(h):
            xbi[h] += 1
            return Xb[h][0]

        T1b = [[S.sbuf("T1b%d_%d" % (h, i), [128, NB, 2, 128], BF16) for i in range(1)] for h in HR]
        T2b = [[S.sbuf("T2b%d_%d" % (h, i), [128, NB, 2, 128], BF16) for i in range(1)] for h in HR]

        def cprod(h, src4, R, W, cr3, ci3, src_buf, cbufs):
            cmi[h] += 1
            a_, b_ = T1b[h][0], T2b[h][0]
            crb = cr3.unsqueeze(2).broadcast_to([R, NB, 2, W])
            cib = ci3.unsqueeze(2).broadcast_to([R, NB, 2, W])
            S.op("dve", _mk("tensor_tensor", out=a_[0:R, :, :, 0:W], in0=src4, in1=crb, op=ALU.mult), [src_buf] + cbufs, [a_])
            S.op("dve", _mk("tensor_tensor", out=b_[0:R, :, :, 0:W], in0=src4, in1=cib, op=ALU.mult), [src_buf] + cbufs, [b_])
            return a_, b_

        for cf, P in (("p", 64), ("s", 128)):
            S.push_pool()
            E1 = S.sbuf("E1" + cf, [128, 256], BF16); TW1 = S.sbuf("TW1" + cf, [P, 2, 128], F32)
            Wc_ = S.sbuf("W" + cf, [P, 4 * P], BF16); Vc = S.sbuf("V" + cf, [P, 6 * P], BF16)
            TW2 = S.sbuf("TW2" + cf, [128, 2, P], F32); Gc = S.sbuf("G" + cf, [128, 384], BF16)
            WF = 64 if cf == "p" else 128
            EF = S.sbuf("EF" + cf, [128, 128] if cf == "p" else [128, 2, 256], BF16)
            TWF = S.sbuf("TWF" + cf, [P, 2, WF], F32)
            S.dma(E1[:], fc_d[cf]["E1"][:], [], [E1])
            S.dma(TW1[:], fc_d[cf]["TW1"][:].rearrange("p (c w) -> p c w", c=2), [], [TW1])
            S.dma(Wc_[:], fc_d[cf]["W"][:], [], [Wc_])
            S.dma(Vc[:], fc_d[cf]["V"][:], [], [Vc])
            S.dma(TW2[:], fc_d[cf]["TW2"][:].rearrange("p (c w) -> p c w", c=2), [], [TW2])
            S.dma(Gc[:], fc_d[cf]["G"][:], [], [Gc])
            if cf == "p":
                S.dma(EF[:], fc_d[cf]["EF"][:], [], [EF])
            else:
                S.dma(EF[:], fc_d[cf]["EF"][:].rearrange("p (k w) -> p k w", k=2), [], [EF])
            S.dma(TWF[:], fc_d[cf]["TWF"][:].rearrange("p (c w) -> p c w", c=2), [], [TWF])
            if cf == "p":
                hf = [S.sbuf("hf%s%d" % (cf, i), [128, GC, P], BF16) for i in range(2)]
            else:
                hf = [S.sbuf("hf%s%d" % (cf, i), [128, 2, 2, GC, P], BF16) for i in range(2)]
            vx = [S.sbuf("vx%s%d" % (cf, i), [128, 3, GC, P], BF16) for i in range(2)]
            zo = [S.sbuf("zo%s%d" % (cf, i), [128, GC, P], BF16) for i in range(2)]
            zs = [S.sbuf("zs%s%d" % (cf, i), [18, GC, 128], BF16) for i in range(2)]

            def s2_complex(h, t1_, t2_, Wd):
                pz = pZh[h][0:P, :].rearrange("p (g c w) -> p g c w", g=NB, c=2)
                Wre, Wim, nWim, nWre = (Wc_[:, 0:P], Wc_[:, P:2 * P], Wc_[:, 2 * P:3 * P], Wc_[:, 3 * P:4 * P])
                for ri, terms in ((0, ((Wre, t1_, 0), (nWre, t2_, 1), (nWim, t2_, 0), (nWim, t1_, 1))),
                                  (1, ((Wre, t2_, 0), (Wre, t1_, 1), (Wim, t1_, 0), (nWim, t2_, 1)))):
                    for ti, (w_, tb, c_) in enumerate(terms):
                        S.op("pe", _mk("matmul", pz[:, :, ri, 0:Wd], lhsT=w_, rhs=tb[0:P, :, c_, 0:Wd], start=(ti == 0), stop=(ti == 3)),
                             [Wc_, tb], [pZh[h]])
                return pz

            def combine(h, t1_, t2_, R, W):
                xb = next_xb(h)
                S.op("pool", _mk("tensor_tensor", out=xb[0:R, :, 0, 0:W], in0=t1_[0:R, :, 0, 0:W], in1=t2_[0:R, :, 1, 0:W], op=ALU.subtract), [t1_, t2_], [xb])
                S.op("pool", _mk("tensor_tensor", out=xb[0:R, :, 1, 0:W], in0=t1_[0:R, :, 1, 0:W], in1=t2_[0:R, :, 0, 0:W], op=ALU.add), [t1_, t2_], [xb])
                return xb

            for c0 in range(0, HW, GC):
                bi_ = (c0 // GC) % 2
                hf_, vx_, zo_, zs_ = hf[bi_], vx[bi_], zo[bi_], zs[bi_]
                L_ = T if cf == "p" else LS
                for o in range(2):
                    src_rows = HF[cf][512 * o + c0:512 * o + c0 + GC, :]
                    if cf == "p":
                        S.dma(hf_[64 * o:64 * (o + 1), :, :], src_rows.rearrange("g (i p) -> i g p", p=P), [HF[cf]], [hf_])
                    else:
                        for kc in range(2):
                            S.dma(hf_[:, o, kc, :, :], src_rows[:, L_ * kc:L_ * (kc + 1)].rearrange("g (i p) -> i g p", p=P), [HF[cf]], [hf_])
                for w3 in range(3):
                    S.dma(vx_[:, w3, :, :], UC[cf][512 * w3 + c0:512 * w3 + c0 + GC, :].rearrange("g (r p) -> r g p", p=P), [UC[cf]], [vx_])
                for q4 in range(1):
                    gi = 0
                    cbase = [c0 + NB * h for h in HR]
                    gl = [NB * h for h in HR]
                    for o in ([None] if cf == "p" else [0, 1]):
                        for h in HR:
                            for g in range(NB):
                                if cf == "p":
                                    S.op("pe", _mk("matmul", pAh[h][0:P, 256 * g:256 * g + 128], lhsT=hf_[:, gl[h] + g, :], rhs=EF[:], start=True, stop=True),
                                         [hf_, EF], [pAh[h]])
                                else:
                                    for kc in range(2):
                                        S.op("pe", _mk("matmul", pAh[h][:, 256 * g:256 * (g + 1)], lhsT=hf_[:, o, kc, gl[h] + g, :], rhs=EF[:, kc, :],
                                                       start=(kc == 0), stop=(kc == 1)), [hf_, EF], [pAh[h]])
                        fps = {}
                        for h in HR:
                            src4 = pAh[h][0:P, :].rearrange("p (g c w) -> p g c w", g=NB, c=2)[:, :, :, 0:WF] if cf == "s" else \
                                pAh[h][0:P, :].rearrange("p (g x) -> p g x", g=NB)[:, :, 0:128].rearrange("p g (c w) -> p g c w", c=2)
                            fps[h] = cprod(h, src4, P, WF, TWF[:, 0, :].unsqueeze(1).broadcast_to([P, NB, WF]),
                                           TWF[:, 1, :].unsqueeze(1).broadcast_to([P, NB, WF]), pAh[h], [TWF])
                        for h in HR:
                            pz = s2_complex(h, fps[h][0], fps[h][1], WF)
                            for g in range(NB):
                                c = cbase[h] + g
                                if cf == "p":
                                    for oo in range(2):
                                        S.op("act", _mk("activation", out=Kc[h][gi][0:P, g, oo, :, :].rearrange("p c (s f) -> p c s f", s=4),
                                                        in_=pz[:, g, :, 32 * oo:32 * (oo + 1)].unsqueeze(2).broadcast_to([P, 2, 4, 32]),
                                                        func=AF.Copy, scale=RNB[cf][0:P, oo, c:c + 1]), [pZh[h], RNB[cf]], [Kc[h][gi]])
                                else:
                                    S.op("act", _mk("activation", out=Kc[h][gi][:, g, o, :, :], in_=pz[:, g, :, :], func=AF.Copy,
                                                    scale=RNB[cf][:, o, c:c + 1]), [pZh[h], RNB[cf]], [Kc[h][gi]])
                    for o in range(2):
                        for h in HR:
                            for g in range(NB):
                                zin = vx_[:, 0, gl[h] + g, :] if o == 0 else z1[h][:, g, 0:P]
                                S.op("pe", _mk("matmul", pAh[h][0:P, 256 * g:256 * (g + 1)], lhsT=zin, rhs=E1[:], start=True, stop=True),
                                     [vx_ if o == 0 else z1[h], E1], [pAh[h]])
                        aps, yhs, ccs = {}, {}, {}
                        for h in HR:
                            aps[h] = cprod(h, pAh[h][0:P, :].rearrange("p (g c w) -> p g c w", g=NB, c=2), P, 128,
                                           TW1[:, 0, :].unsqueeze(1).broadcast_to([P, NB, 128]), TW1[:, 1, :].unsqueeze(1).broadcast_to([P, NB, 128]),
                                           pAh[h], [TW1])
                        for h in HR:
                            s2_complex(h, aps[h][0], aps[h][1], 128)
                        for h in HR:
                            yhs[h] = cprod(h, pZh[h][0:P, :].rearrange("p (g c w) -> p g c w", g=NB, c=2), P, 128,
                                           Kc[h][gi][0:P, :, o, 0, :], Kc[h][gi][0:P, :, o, 1, :], pZh[h], [Kc[h][gi]])
                        for h in HR:
                            yhs[h] = combine(h, yhs[h][0], yhs[h][1], P, 128)
                        for h in HR:
                            pc_ = pCh[h][:, :].rearrange("p (g x) -> p g x", g=NB)
                            VA, VB = Vc[:, 0:2 * P], Vc[:, 2 * P:4 * P]
                            for g in range(NB):
                                S.op("pe", _mk("matmul", pc_[:, g, 0:2 * P], lhsT=yhs[h][0:P, g, 0, :], rhs=VA, start=True, stop=False), [yhs[h], Vc], [pCh[h]])
                                S.op("pe", _mk("matmul", pc_[:, g, 0:2 * P], lhsT=yhs[h][0:P, g, 1, :], rhs=VB, start=False, stop=True), [yhs[h], Vc], [pCh[h]])
                        for h in HR:
                            src4 = pCh[h][:, :].rearrange("p (g x) -> p g x", g=NB)[:, :, 0:2 * P].rearrange("p g (c w) -> p g c w", c=2)
                            ccs[h] = cprod(h, src4, 128, P, TW2[:, 0, :].unsqueeze(1).broadcast_to([128, NB, P]),
                                           TW2[:, 1, :].unsqueeze(1).broadcast_to([128, NB, P]), pCh[h], [TW2])
                        for h in HR:
                            ccs[h] = combine(h, ccs[h][0], ccs[h][1], 128, P)
                        for h in HR:
                            Gre, Gim = Gc[:, 0:128], Gc[:, 128:256]
                            py3 = pYh[h][:, 0:128 * NB].rearrange("p (g x) -> p g x", g=NB)[:, :, 0:P]
                            S.op("pe", _mk("matmul", py3, lhsT=Gre, rhs=ccs[h][:, :, 0, 0:P], start=True, stop=False), [Gc, ccs[h]], [pYh[h]])
                            S.op("pe", _mk("matmul", py3, lhsT=Gim, rhs=ccs[h][:, :, 1, 0:P], start=False, stop=True), [Gc, ccs[h]], [pYh[h]])
                        for h in HR:
                            cb = 512 * o + cbase[h]
                            zin3 = vx_[:, 0, gl[h]:gl[h] + NB, :] if o == 0 else z1[h][:, :, 0:P]
                            xg3 = vx_[:, 1 + o, gl[h]:gl[h] + NB, :]
                            for g in range(NB):
                                S.op("act", _mk("activation", out=gt[h][:, g, 0:P], in_=zin3[:, g, :], func=AF.Copy, scale=skb[:, cb + g:cb + g + 1]),
                                     [vx_ if o == 0 else z1[h], skb], [gt[h]])
                            S.op("dve", _mk("tensor_tensor", out=gt[h][:, :, 0:P], in0=pYh[h][:, 0:128 * NB].rearrange("p (g x) -> p g x", g=NB)[:, :, 0:P],
                                            in1=gt[h][:, :, 0:P], op=ALU.add), [pYh[h], gt[h]], [gt[h]])
                            dst = z1[h][:, :, 0:P] if o == 0 else zo_[:, gl[h]:gl[h] + NB, :]
                            S.op("pool", _mk("tensor_tensor", out=dst, in0=gt[h][:, :, 0:P], in1=xg3, op=ALU.mult), [gt[h], vx_], [z1[h] if o == 0 else zo_])
                    if cf == "s":
                        for h in HR:
                            for g in range(NB):
                                S.op("pe", _mk("matmul", pAh[h][0:18, 128 * g:128 * (g + 1)], lhsT=sel[:], rhs=zo_[:, gl[h] + g, :], start=True, stop=True),
                                     [sel, zo_], [pAh[h]])
                            S.op("act", _mk("activation", out=zs_[:, gl[h]:gl[h] + NB, :], in_=pAh[h][0:18, 0:128 * NB].rearrange("p (g x) -> p g x", g=NB),
                                            func=AF.Copy), [pAh[h]], [zs_])
                if cf == "p":
                    for sq_ in range(NSEQ):
                        S.dma(YH[sq_][c0:c0 + GC, :].rearrange("g (i p) -> i g p", p=P), zo_[32 * sq_:32 * (sq_ + 1), :, :], [zo_], [YH[sq_]], q="actq")
                else:
                    S.dma(YH[4][c0:c0 + GC, :].rearrange("g (r p) -> r g p", p=128), zs_[:], [zs_], [YH[4]], q="actq")
            S.pop_pool()
        S.pop_pool()

    S.push_pool()
    X1S = S.dram("X1S", [18 * 128, DM], F32)
    G1B = S.sbuf("G1B", [128, DM], F32)
    G2B = S.sbuf("G2B", [128, DM], F32)
    dg = S.sbuf("dg", [128, 128], F32)
    hT2 = S.sbuf("hT2", [128, 8, 18 * 128 + 2], BF16)
    x1 = [S.sbuf("x1_%d" % i, [128, DM], F32) for i in range(3)]
    x1r = [S.sbuf("x1r%d" % i, [128, DM], F32) for i in range(2)]
    xin = [S.sbuf("xin%d" % i, [128, DM], F32) for i in range(2)]
    mixT = S.sbuf("mixT", [128, 8, 512], BF16)
    yh_t = S.sbuf("yh", [128, 4, 512], BF16)
    sq = S.sbuf("sq", [128, 4, 512], BF16)
    rsb = S.sbuf("rsb", [128, 512], F32)
    wo_t = S.sbuf("wo_t", [128, 8, DM], BF16)
    S.dma(wo_t[:].rearrange("p k m -> p (k m)"), WO[:].rearrange("p k m -> p (k m)"), [WO], [wo_t])
    wgu_t = [S.sbuf("wgu_t%d" % i, [128, 2, 8, 128], BF16) for i in range(4)]
    wd_t = [S.sbuf("wd_t%d" % i, [128, DM], BF16) for i in range(3)]
    aT = S.sbuf("aT", [128, NFF, 512], BF16)
    gext = [S.sbuf("gext%d" % i, [128, 514], F32) for i in range(3)]
    upsb = [S.sbuf("upsb%d" % i, [128, 512], F32) for i in range(3)]
    cvs = [S.sbuf("cv%d" % i, [128, 512], F32) for i in range(2)]
    sgs = [S.sbuf("sg%d" % i, [128, 512], F32) for i in range(2)]
    tmp = [S.sbuf("tmp%d" % i, [128, 512], F32) for i in range(2)]
    xn2 = [S.sbuf("xn2_%d" % i, [128, DM], BF16) for i in range(2)]
    x2 = S.sbuf("x2", [128, DM], F32)
    yo = [S.sbuf("yo%d" % i, [128, DM], F32) for i in range(2)]
    ss2 = [S.sbuf("ss2_%d" % i, [128, 1], F32) for i in range(4)]
    rs2 = [S.sbuf("rs2_%d" % i, [128, 1], F32) for i in range(4)]
    junk2 = S.sbuf("junk2", [128, DM], BF16)
    p_ss = S.psum("p_ss", [128, 512], F32)
    p_o = [S.psum("p_o%d" % i, [128, 512], F32) for i in range(2)]
    p_trb = S.psum("p_trf", [128, 512], F32)
    p_tr = p_trb[:, :].bitcast(BF16)
    p_g = S.psum("p_g", [128, 512], F32)
    p_h = S.psum("p_h", [128, 512], F32)
    p_us = [S.psum("p_u%d" % i, [128, 512], F32) for i in range(2)]
    p_u = p_us[0]
    p_b = p_ss
    c2 = {"o": 0, "t": 0, "w": 0, "e": 0, "x1": 0, "xin": 0, "y": 0, "g": 0, "wd": 0}

    for seg in range(5):
        o0, o1 = SEG_O[seg]
        q0, q1 = SEG_Q[seg]
        na = q1 - q0
        hal = o0 - q0
        xsrc = xp if seg < 4 else xsw
        xoff = seg * T if seg < 4 else 128 * q0
        ydst = yp_d if seg < 4 else ys_d
        yoff = seg * T if seg < 4 else 0
        for (GB, base) in ((G1B, 16), (G2B, 40)):
            for k in range(8):
                S.op("dve", _mk("tensor_scalar", out=dg[:], in0=ident_f[:], scalar1=modT[:, base + k, seg:seg + 1],
                                                                      scalar2=None, op0=ALU.mult), [ident_f, modT], [dg])
                S.op("pe", _mk("matmul", p_b[:, 128 * (k % 4):128 * (k % 4 + 1)], lhsT=ones_f[:], rhs=dg[:], start=True, stop=True),
                     [ones_f, dg], [p_b])
                if k % 4 == 3:
                    S.op("act", _mk("activation", out=GB[:, 128 * (k - 3):128 * (k + 1)], in_=p_b[:], func=AF.Copy),
                         [p_b], [GB])
        S.op("pool", _mk("memset", hT2[:, :, 0:1], 0.0), [], [hT2])
        S.op("pool", _mk("memset", hT2[:, :, 1 + 128 * na:2 + 128 * na], 0.0), [], [hT2])

        def stage_a(ta0, nta):
            N = 128 * nta
            c0 = 128 * ta0
            mx = mixT
            S.dma(yh_t[:, :, 0:N], YH[seg][:, c0:c0 + N].rearrange("(k p) t -> p k t", p=128), [YH[seg]], [yh_t])
            S.dma(mx[:, 4:8, 0:N], AT[seg][:, c0:c0 + N].rearrange("(k p) t -> p k t", p=128), [AT[seg]], [mx])
            S.op("pool", _mk("tensor_tensor", out=sq[:, :, 0:N], in0=yh_t[:, :, 0:N], in1=yh_t[:, :, 0:N], op=ALU.mult), [yh_t], [sq])
            for k in range(4):
                S.op("pe", _mk("matmul", p_ss[:, 0:N], lhsT=ones_b[:], rhs=sq[:, k, 0:N], start=(k == 0), stop=(k == 3)),
                     [ones_b, sq], [p_ss])
            S.op("act", _mk("activation", out=rsb[:, 0:N], in_=p_ss[:, 0:N], func=AF.Sqrt, bias=epsb[:], scale=1.0 / HW),
                 [p_ss, epsb], [rsb])
            S.op("dve", _mk("reciprocal", out=rsb[:, 0:N], in_=rsb[:, 0:N]), [rsb], [rsb])
            for k in range(4):
                S.op("dve", _mk("scalar_tensor_tensor", out=mx[:, k, 0:N], in0=yh_t[:, k, 0:N], scalar=hyg[:, k:k + 1],
                                                                  in1=rsb[:, 0:N], op0=ALU.mult, op1=ALU.mult), [yh_t, hyg, rsb], [mx])
            def part1(t):
                xi = xin[c2["xin"] % 2]
                c2["xin"] += 1
                x1t = x1[c2["x1"] % 3]
                c2["x1"] += 1
                r0 = xoff + c0 + 128 * t
                S.dma(xi[:], xsrc[r0:r0 + 128, :], [xsrc], [xi])
                for nh_ in range(2):
                    p_ = p_o[c2["o"] % 2]
                    c2["o"] += 1
                    for k in range(8):
                        S.op("pe", _mk("matmul", p_[:], lhsT=mx[:, k, 128 * t:128 * (t + 1)], rhs=wo_t[:, k, 512 * nh_:512 * (nh_ + 1)],
                                       start=(k == 0), stop=(k == 7)), [mx, wo_t], [p_])
                    tm = tmp[c2["e"] % 2]
                    c2["e"] += 1
                    S.op("dve", _mk("tensor_tensor", out=tm[:], in0=p_[:], in1=G1B[:, 512 * nh_:512 * (nh_ + 1)], op=ALU.mult), [p_, G1B], [tm])
                    S.op("pool", _mk("tensor_tensor", out=x1t[:, 512 * nh_:512 * (nh_ + 1)], in0=xi[:, 512 * nh_:512 * (nh_ + 1)], in1=tm[:],
                                     op=ALU.add), [xi, tm], [x1t])
                S.dma(X1S[c0 + 128 * t:c0 + 128 * (t + 1), :], x1t[:], [x1t], [X1S], q="actq")
                return x1t

            def part2(t, x1t):
                ti = c2["t"] % 4
                c2["t"] += 1
                ss, rs, xnb = ss2[ti], rs2[ti], xn2[ti % 2]
                S.op("act", _mk("activation", out=junk2[:], in_=x1t[:], func=AF.Square, accum_out=ss[:]), [x1t], [junk2, ss])
                rstd_from_ss(ss, rs, DM)
                S.op("dve", _mk("tensor_scalar", out=xnb[:], in0=x1t[:], scalar1=rs[:], scalar2=None, op0=ALU.mult), [x1t, rs], [xnb])
                for k in range(8):
                    S.op("pe", _mk("transpose", p_tr[:, 128 * k:128 * (k + 1)], xnb[:, 128 * k:128 * (k + 1)], ident_b[:]), [xnb, ident_b], [p_trb])
                col = 1 + c0 + 128 * t
                for k in range(8):
                    if k % 2 == 0:
                        S.op("act", _mk("activation", out=hT2[:, k, col:col + 128], in_=p_tr[:, 128 * k:128 * (k + 1)], func=AF.Identity,
                                        scale=SC2[:, k, seg:seg + 1], bias=modT[:, 24 + k, seg:seg + 1]), [p_trb, SC2, modT], [hT2])
                    else:
                        S.op("dve", _mk("tensor_scalar", out=hT2[:, k, col:col + 128], in0=p_tr[:, 128 * k:128 * (k + 1)],
                                        scalar1=SC2[:, k, seg:seg + 1], scalar2=modT[:, 24 + k, seg:seg + 1], op0=ALU.mult, op1=ALU.add),
                             [p_trb, SC2, modT], [hT2])

            prev = None
            for t in range(nta):
                x1t = part1(t)
                if prev is not None:
                    part2(*prev)
                prev = (t, x1t)
            part2(*prev)

        def stage_b(tb0):
            colb = 1 + 128 * tb0
            pgs = [p_g, p_h]

            def ffn_mm(j):
                wbuf = wgu_t[c2["w"] % 4]
                c2["w"] += 1
                S.dma(wbuf[:].rearrange("p a k m -> p (a k m)"), WGU[j].rearrange("p a k m -> p (a k m)"), [WGU], [wbuf])
                wgt, wut = wbuf[:, 0], wbuf[:, 1]
                pg_, pu_ = pgs[j % 2], p_us[j % 2]
                for k in range(8):
                    S.op("pe", _mk("matmul", pg_[:], lhsT=wgt[:, k, :], rhs=hT2[:, k, colb:colb + 512], start=(k == 0), stop=(k == 7)),
                         [wbuf, hT2], [pg_])
                for k in range(8):
                    S.op("pe", _mk("matmul", p_ss[:, 0:2], lhsT=wgt[:, k, :], rhs=hT2[:, k, colb - 1:colb + 513:513], start=(k == 0), stop=(k == 7)),
                         [wbuf, hT2], [p_ss])
                for k in range(8):
                    S.op("pe", _mk("matmul", pu_[:], lhsT=wut[:, k, :], rhs=hT2[:, k, colb:colb + 512], start=(k == 0), stop=(k == 7)),
                         [wbuf, hT2], [pu_])
                ge = gext[j % 3]
                S.op("act", _mk("activation", out=ge[:, 1:513], in_=pg_[:], func=AF.Copy), [pg_], [ge])
                S.op("act", _mk("activation", out=ge[:, 0:514:513], in_=p_ss[:, 0:2], func=AF.Copy), [p_ss], [ge])
                S.op("act", _mk("activation", out=upsb[j % 3][:], in_=pu_[:], func=AF.Copy), [pu_], [upsb[j % 3]])

            def ffn_chain(j):
                ge, pu_ = gext[j % 3], upsb[j % 3]
                cv, sgj = cvs[j % 2], sgs[j % 2]
                if seg == 4 and tb0 == hal:
                    S.op("dve", _mk("tensor_scalar", out=ge[:, 0:1], in0=ge[:, 0:1], scalar1=edge[:, 0:1], scalar2=None, op0=ALU.mult), [ge, edge], [ge])
                if seg == 4 and tb0 + 4 == na - hal:
                    S.op("dve", _mk("tensor_scalar", out=ge[:, 513:514], in0=ge[:, 513:514], scalar1=edge[:, 1:2], scalar2=None, op0=ALU.mult),
                         [ge, edge], [ge])
                S.op("dve", _mk("tensor_scalar", out=cv[:], in0=ge[:, 1:513], scalar1=fcw[:, j, 1:2], scalar2=fcb[:, j:j + 1], op0=ALU.mult, op1=ALU.add),
                     [ge, fcw, fcb], [cv])
                S.op("dve", _mk("scalar_tensor_tensor", out=cv[:], in0=ge[:, 0:512], scalar=fcw[:, j, 0:1], in1=cv[:], op0=ALU.mult, op1=ALU.add),
                     [ge, fcw, cv], [cv])
                S.op("dve", _mk("scalar_tensor_tensor", out=cv[:], in0=ge[:, 2:514], scalar=fcw[:, j, 2:3], in1=cv[:], op0=ALU.mult, op1=ALU.add),
                     [ge, fcw, cv], [cv])
                S.op("act", _mk("activation", out=sgj[:], in_=cv[:], func=AF.Gelu_apprx_tanh), [cv], [sgj])
                S.op("pool", _mk("tensor_tensor", out=aT[:, j, :], in0=sgj[:], in1=pu_[:], op=ALU.mult), [sgj, pu_], [aT])

            for j in range(NFF):
                ffn_mm(j)
                if j >= 2:
                    ffn_chain(j - 2)
            ffn_chain(NFF - 2)
            ffn_chain(NFF - 1)
            accs = [p_o[0], p_o[1], p_us[0], p_us[1], p_g, p_h, p_ss, p_trb]
            for j in range(NFF):
                wdt = wd_t[c2["wd"] % 3]
                c2["wd"] += 1
                S.dma(wdt[:], WD[j], [WD], [wdt])
                for t in range(4):
                    for nh_ in range(2):
                        acc_ = accs[2 * t + nh_]
                        S.op("pe", _mk("matmul", acc_[:], lhsT=aT[:, j, 128 * t:128 * (t + 1)], rhs=wdt[:, 512 * nh_:512 * (nh_ + 1)],
                                       start=(j == 0), stop=(j == NFF - 1)), [aT, wdt], [acc_])
            for t in range(4):
                x1t = x1r[c2["x1"] % 2]
                c2["x1"] += 1
                S.dma(x1t[:], X1S[128 * (tb0 + t):128 * (tb0 + t + 1), :], [X1S], [x1t])
                for nh_ in range(2):
                    acc_ = accs[2 * t + nh_]
                    tm = tmp[c2["e"] % 2]
                    c2["e"] += 1
                    S.op("dve", _mk("tensor_tensor", out=tm[:], in0=acc_[:], in1=G2B[:, 512 * nh_:512 * (nh_ + 1)], op=ALU.mult), [acc_, G2B], [tm])
                    S.op("pool", _mk("tensor_tensor", out=x2[:, 512 * nh_:512 * (nh_ + 1)], in0=x1t[:, 512 * nh_:512 * (nh_ + 1)], in1=tm[:],
                                     op=ALU.add), [x1t, tm], [x2])
                yot = yo[c2["y"] % 2]
                c2["y"] += 1
                ti = c2["t"] % 4
                c2["t"] += 1
                ss, rs = ss2[ti], rs2[ti]
                S.op("act", _mk("activation", out=junk2[:], in_=x2[:], func=AF.Square, accum_out=ss[:]), [x2], [junk2, ss])
                rstd_from_ss(ss, rs, DM)
                S.op("dve", _mk("scalar_tensor_tensor", out=yot[:], in0=x2[:], scalar=rs[:], in1=fgb[:], op0=ALU.mult, op1=ALU.mult),
                     [x2, rs, fgb], [yot])
                r0 = yoff + 128 * (tb0 - hal + t)
                S.dma(ydst[r0:r0 + 128, :], yot[:], [yot], [ydst], q="actq")

        if seg < 4:
            a_groups = [(0, 4), (4, 4), (8, 4), (12, 4)]
            b_groups = [(0, 0), (4, 1), (8, 2), (12, 3)]
        else:
            a_groups = [(0, 1), (1, 4), (5, 4), (9, 4), (13, 4), (17, 1)]
            b_groups = [(1, 1), (5, 2), (9, 3), (13, 4)]
        bi = 0
        for ai, (ta0, nta) in enumerate(a_groups):
            stage_a(ta0, nta)
            while bi < len(b_groups) and b_groups[bi][1] < ai:
                stage_b(b_groups[bi][0])
                bi += 1
        while bi < len(b_groups):
            stage_b(b_groups[bi][0])
            bi += 1
    S.pop_pool()
    n = S.emit(final_wait_bufs=[yp_d, ys_d])
    return nc, n


_CACHE = {}


def fft_consts():
    out = {}
    bf = ml_dtypes.bfloat16
    for cf, P, I, Sq, L in (("p", 64, 32, 4, T), ("s", 128, 128, 1, LS)):
        N2, N = 2 * I, 2 * L
        t = np.arange(L, dtype=np.float64)
        tn = (t / (L - 1)).astype(np.float32)
        bands = np.linspace(1e-4, 15.0, 16).astype(np.float32).astype(np.float64)
        ang = (2.0 * math.pi / L) * t[:, None] * bands[None, :]
        feat = np.concatenate([tn[:, None].astype(np.float64), np.cos(ang), np.sin(ang)], axis=1).astype(np.float32)
        rev = (L - np.arange(L)) % L
        out["feat_" + cf] = np.ascontiguousarray(np.stack([feat.T, feat[rev].T], axis=1))
        tn2 = np.stack([tn, tn[rev]], axis=0)
        out["tn_" + cf] = np.ascontiguousarray(np.broadcast_to(tn2[None], (128, 2, L)).astype(np.float32))
        i = np.arange(I)[:, None]
        fb = np.arange(I)[None, :]
        th1 = 2 * math.pi * i * (fb + 0.5) / N2
        E1 = np.zeros((Sq, I, 2, Sq, I))
        G = np.zeros((Sq, I, 2, Sq, I))
        for s_ in range(Sq):
            E1[s_, :, 0, s_, :] = np.cos(th1)
            E1[s_, :, 1, s_, :] = -np.sin(th1)
            G[s_, :, 0, s_, :] = (2.0 / N) * np.cos(th1).T
            G[s_, :, 1, s_, :] = -(2.0 / N) * np.sin(th1).T
        out["E1_" + cf] = E1.reshape(128, 256).astype(bf)
        G2 = G.reshape(128, 2, 128)
        out["G_" + cf] = np.concatenate([G2[:, 0], G2[:, 1], -G2[:, 0]], axis=1).astype(bf)
        p = np.arange(P)[:, None]
        th2 = 2 * math.pi * p * (np.arange(I)[None, :] + 0.5) / N
        cr = np.tile(np.cos(th2), (1, Sq))
        ci = np.tile(-np.sin(th2), (1, Sq))
        out["TW1_" + cf] = np.concatenate([cr, ci], axis=1).astype(np.float32)
        cr2 = np.tile(np.cos(th2).T, (Sq, 1))
        ci2 = np.tile(np.sin(th2).T, (Sq, 1))
        out["TW2_" + cf] = np.concatenate([cr2, ci2], axis=1).astype(np.float32)
        thw = 2 * math.pi * p * np.arange(P)[None, :] / P
        out["W_" + cf] = np.concatenate([np.cos(thw), -np.sin(thw), np.sin(thw), -np.cos(thw)], axis=1).astype(bf)
        out["V_" + cf] = np.concatenate([np.cos(thw), np.sin(thw), -np.sin(thw), np.cos(thw), -np.cos(thw), -np.sin(thw)],
                                        axis=1).astype(bf)
        ifull = np.arange(N2)[:, None]
        thf = 2 * math.pi * ifull * (fb + 0.5) / N2
        twf = np.exp(-1j * th2) * (1j * (-1.0) ** np.arange(I))[None, :]
        if cf == "p":
            EF = np.zeros((2, N2, 2, 2, I))
            for o in range(2):
                EF[o, :, 0, o, :] = np.cos(thf)
                EF[o, :, 1, o, :] = -np.sin(thf)
            out["EF_" + cf] = EF.reshape(128, 128).astype(bf)
            out["TWF_" + cf] = np.concatenate([np.tile(twf.real, (1, 2)), np.tile(twf.imag, (1, 2))], axis=1).astype(np.float32)
        else:
            EF = np.stack([np.cos(thf), -np.sin(thf)], axis=1)
            EF = EF.reshape(2, 128, 256).transpose(1, 0, 2)
            out["EF_" + cf] = np.ascontiguousarray(EF.reshape(128, 512)).astype(bf)
            out["TWF_" + cf] = np.concatenate([twf.real, twf.imag], axis=1).astype(np.float32)
    return out


def kernel(x_prompt, x_sample, c_prompt, c_sample, ada_w, ada_b, norm1_g, w_in, hy_short_w, hy_short_b,
           hy_pos_w1, hy_pos_b1, hy_sin_freq, hy_pos_w2, hy_pos_b2, hy_pos_w3, hy_decay, hy_skip, hy_out_g,
           attn_out_g, w_out, norm2_g, ffn_w_gate, ffn_w_up, ffn_conv_w, ffn_conv_b, ffn_w_down, final_g):
    f = lambda a: np.ascontiguousarray(np.asarray(a, dtype=np.float32))
    if "nc" not in _CACHE:
        _CACHE["nc"] = build_program()
    nc, nops = _CACHE["nc"]
    x_prompt, x_sample = f(x_prompt), f(x_sample)
    xs = x_sample[0]
    pc = lambda v: f(np.asarray(v).reshape(-1, 128).T)
    common = {
        "ada_w": f(ada_w[0]), "ada_b": pc(ada_b[0]), "n1g": pc(norm1_g[0]), "n2g": pc(norm2_g[0]),
        "fgb": f(np.broadcast_to(np.asarray(final_g)[None, :], (128, DM))),
        "w_in": f(w_in[0]), "w_out": f(w_out[0]), "wg": f(ffn_w_gate[0]), "wu": f(ffn_w_up[0]), "wd": f(ffn_w_down[0]),
        "fcw": f(np.asarray(ffn_conv_w[0]).reshape(3, NFF, 128).transpose(2, 1, 0)),
        "fcb": pc(ffn_conv_b[0]), "hyg": pc(hy_out_g[0]),
        "agb": f(np.broadcast_to(np.asarray(attn_out_g[0])[None, :], (128, HW))),
        "ident": np.eye(128, dtype=np.float32), "masks": build_masks(),
        "xsf": xs,
        "hw1": f(hy_pos_w1[0]), "hb1": f(np.asarray(hy_pos_b1[0]).reshape(64, 1)), "hsf": f(np.asarray(hy_sin_freq[0]).T),
        "hw2": f(hy_pos_w2[0]), "hb2": f(np.asarray(hy_pos_b2[0]).reshape(64, 1)), "hw3": f(hy_pos_w3[0]),
        "hdec": pc(np.asarray(hy_decay[0]).reshape(-1)),
        "hsw": f(np.asarray(hy_short_w[0]).reshape(3, 12, 128).transpose(2, 1, 0)), "hsb": pc(hy_short_b[0]),
        "hskip": f(np.broadcast_to(np.asarray(hy_skip[0]).reshape(1, 1024), (128, 1024))),
    }
    common.update(fft_consts())
    in_maps = []
    for c in range(NCORE):
        lo = CH * c - 128 * QT0_S - 0
        lo = CH * c - 128 * OT0_S
        win = np.zeros((WT_S * 128, DM), np.float32)
        valid = np.zeros((WT_S * 128,), np.float32)
        a, b = max(lo, 0), min(lo + WT_S * 128, LS)
        win[a - lo:b - lo] = xs[a:b]
        valid[a - lo:b - lo] = 1.0
        edge = np.zeros((128, 2), np.float32)
        edge[:, 0] = 1.0 if c > 0 else 0.0
        edge[:, 1] = 1.0 if c < NCORE - 1 else 0.0
        cc = np.concatenate([np.asarray(c_prompt[NSEQ * c:NSEQ * (c + 1)]), np.asarray(c_sample)], axis=0)
        m = dict(common)
        m.update({
            "xp": x_prompt[NSEQ * c:NSEQ * (c + 1)].reshape(NSEQ * T, DM),
            "xsw": win, "flags": f(valid.reshape(WT_S, 128).T), "edge": edge, "cc": f(cc.T),
        })
        sel = np.zeros((128, 18), np.float32)
        for r in range(18):
            gt_ = 16 * c - 1 + r
            if 0 <= gt_ < 128:
                sel[gt_, r] = 1.0
        m["sel"] = sel.astype(ml_dtypes.bfloat16)
        m["selr"] = np.zeros((128, 128), ml_dtypes.bfloat16)
        in_maps.append(m)
    res = run_bass_kernel_spmd(nc, in_maps, core_ids=list(range(NCORE)))
    yp = np.concatenate([np.asarray(r["yp"]).reshape(NSEQ, T, DM) for r in res.results], axis=0)
    ys = np.concatenate([np.asarray(r["ys"]) for r in res.results], axis=0).reshape(1, LS, DM)
    return (yp.astype(np.float32), ys.astype(np.float32))
```

```python
import contextlib
import math
import numpy as np
import ml_dtypes
import concourse.bass as bass
import concourse.mybir as mybir
from concourse.bass_utils import run_bass_kernel_spmd

F32 = mybir.dt.float32
BF16 = mybir.dt.bfloat16
AF = mybir.ActivationFunctionType
ALU = mybir.AluOpType

ENGS = ["pe", "act", "dve", "pool", "sp"]
N_DMA_SEMS = 18
DMA_POOLS = {"sp": list(range(0, 10)), "actq": list(range(10, 16)), "poolq": list(range(16, 18))}


def _mk(name, *args, **kw):
    def f(e):
        return getattr(e, name)(*args, **kw)
    return f


class Buf:
    def __init__(self, name, t):
        self.name = name
        self.t = t
        self.last_write = None
        self.reads = []

    def __getitem__(self, k):
        return self.t[k]


class Op:
    __slots__ = ("eng", "fn", "deps", "is_dma", "dma_sem", "dma_val", "marked", "cnt")

    def __init__(self, eng, fn, is_dma):
        self.eng = eng
        self.fn = fn
        self.deps = []
        self.is_dma = is_dma
        self.dma_sem = None
        self.dma_val = 0
        self.marked = False
        self.cnt = 0


class Sched:
    def __init__(self, nc):
        self.nc = nc
        self.ops = []
        self.stack = contextlib.ExitStack()
        self.n_dma = {q: 0 for q in DMA_POOLS}
        self.dma_last = [None] * N_DMA_SEMS
        self.dma_cnt = [0] * N_DMA_SEMS
        self.last_on = {e: None for e in ENGS}
        self.barrier_deps = set()
        self.pools = [self.stack]

    def push_pool(self):
        st = contextlib.ExitStack()
        self.pools.append(st)
        return st

    def pop_pool(self):
        self.barrier()
        st = self.pools.pop()
        st.close()

    def barrier(self):
        deps = set(self.barrier_deps)
        for e in ENGS:
            if self.last_on[e] is not None:
                deps.add(self.last_on[e])
        for s in range(N_DMA_SEMS):
            if self.dma_last[s] is not None:
                deps.add(self.dma_last[s])
        self.barrier_deps = deps

    def sbuf(self, name, shape, dtype):
        t = self.pools[-1].enter_context(self.nc.sbuf_tensor(name, list(shape), dtype))
        return Buf(name, t)

    def psum(self, name, shape, dtype=F32):
        t = self.pools[-1].enter_context(self.nc.psum_tensor(name, list(shape), dtype))
        return Buf(name, t)

    def dram(self, name, shape, dtype, kind="Internal"):
        t = self.nc.dram_tensor(name, list(shape), dtype, kind=kind)
        return Buf(name, t.ap())

    def op(self, eng, fn, reads=(), writes=(), acc=False):
        is_dma = eng in ("sp", "actq", "poolq")
        real_eng = {"actq": "act", "poolq": "pool"}.get(eng, eng)
        o = Op(real_eng, fn, is_dma)
        oid = len(self.ops)
        deps = set(self.barrier_deps)
        for b in reads:
            if b.last_write is not None:
                deps.add(b.last_write)
        for b in writes:
            if b.last_write is not None:
                lw = self.ops[b.last_write]
                if is_dma or lw.is_dma or lw.eng != real_eng:
                    deps.add(b.last_write)
            for r in b.reads:
                ro = self.ops[r]
                if is_dma or ro.is_dma or ro.eng != real_eng:
                    deps.add(r)
        if is_dma:
            pool_ = DMA_POOLS[eng]
            s = pool_[self.n_dma[eng] % len(pool_)]
            self.n_dma[eng] += 1
            if self.dma_last[s] is not None:
                deps.add(self.dma_last[s])
            self.dma_last[s] = oid
            self.dma_cnt[s] += 1
            o.dma_sem = s
            o.dma_val = 16 * self.dma_cnt[s]
        if real_eng == "pe" and not is_dma:
            deps = {d for d in deps if not (self.ops[d].eng == "pe" and not self.ops[d].is_dma)}
        o.deps = sorted(deps)
        self.ops.append(o)
        self.last_on[real_eng] = oid
        for b in reads:
            if not is_dma:
                b.reads = [r for r in b.reads if self.ops[r].is_dma or self.ops[r].eng != real_eng]
            b.reads.append(oid)
        for b in writes:
            b.last_write = oid
            b.reads = []
        return oid

    def dma(self, out, in_, reads=(), writes=(), q="sp", **kw):
        return self.op(q, _mk("dma_start", out=out, in_=in_, **kw), reads, writes)

    def emit(self, final_wait_bufs=()):
        nc = self.nc
        ops = self.ops
        final_deps = set()
        for b in final_wait_bufs:
            if b.last_write is not None:
                final_deps.add(b.last_write)
        for s in range(N_DMA_SEMS):
            if self.dma_last[s] is not None:
                final_deps.add(self.dma_last[s])
        for o in ops:
            for d in o.deps:
                ops[d].marked = True
        for d in final_deps:
            ops[d].marked = True
        cnt = {e: 0 for e in ENGS}
        for o in ops:
            if not o.is_dma:
                if o.marked:
                    cnt[o.eng] += 1
                o.cnt = cnt[o.eng]
        sems = {e: self.stack.enter_context(nc.semaphore("s_" + e)) for e in ENGS}
        dsems = [self.stack.enter_context(nc.semaphore("d_%d" % i)) for i in range(N_DMA_SEMS)]

        def tok(d):
            od = ops[d]
            if od.is_dma:
                return ("d", od.dma_sem), od.dma_val
            return ("e", od.eng), od.cnt

        streams = {e: [] for e in ENGS}
        seen = {e: {} for e in ENGS}
        for o in ops:
            waits = {}
            for d in o.deps:
                k, v = tok(d)
                if seen[o.eng].get(k, 0) >= v:
                    continue
                if waits.get(k, 0) < v:
                    waits[k] = v
            for k, v in waits.items():
                seen[o.eng][k] = v
            streams[o.eng].append((o, sorted(waits.items())))
        fw = {}
        for d in final_deps:
            k, v = tok(d)
            if fw.get(k, 0) < v:
                fw[k] = v

        def semof(k):
            return dsems[k[1]] if k[0] == "d" else sems[k[1]]

        def run(eng_name, e):
            for o, waits in streams[eng_name]:
                for k, v in waits:
                    e.wait_ge(semof(k), v)
                ins = o.fn(e)
                if o.is_dma:
                    ins.then_inc(dsems[o.dma_sem], 16)
                elif o.marked:
                    ins.then_inc(sems[eng_name], 1)
            if eng_name == "sp":
                for k, v in sorted(fw.items()):
                    e.wait_ge(semof(k), v)

        with nc.Block() as block:
            @block.sync
            def _(e):
                run("sp", e)

            @block.tensor
            def _(e):
                run("pe", e)

            @block.scalar
            def _(e):
                run("act", e)

            @block.vector
            def _(e):
                run("dve", e)

            @block.gpsimd
            def _(e):
                run("pool", e)
        while self.pools:
            self.pools.pop().close()
        return {e: len(streams[e]) for e in ENGS}


DM = 1024
T = 2048
NSEQ = 4
LS = 16384
NCORE = 8
CH = 2048
HW = 512
NH = 8
HD = 64
DFF = 2816
NFF = 22
EPS = 1e-6
WT_S = 34
QT0_S, QT1_S = 8, 26
OT0_S, OT1_S = 9, 25
SEG_WT = [16, 16, 16, 16, WT_S]
SEG_Q = [(0, 16)] * 4 + [(QT0_S, QT1_S)]
SEG_O = [(0, 16)] * 4 + [(OT0_S, OT1_S)]
HY_STUB = False

SLOPES = [2.0 ** (-8.0 * (h + 1) / NH) for h in range(NH)]


def head_deltas(h):
    dmax = min(8, int((30.0 / SLOPES[h] - 1.0) // 128) + 1)
    return list(range(-dmax, dmax + 1))


MASK_OFF = {}
_n = 0
for _h in range(NH):
    for _d in head_deltas(_h):
        MASK_OFF[(_h, _d)] = _n
        _n += 1
N_MASK = _n


def build_masks():
    m = np.zeros((128, N_MASK, 128), np.float32)
    k = np.arange(128)[:, None]
    q = np.arange(128)[None, :]
    for h in range(NH):
        for d in head_deltas(h):
            o = 128 * d + k - q
            a = np.abs(o)
            mult = (a <= 64).astype(np.float64) + ((o % 4 == 0) & (a <= 256)) + ((o % 16 == 0) & (a <= 1024))
            m[:, MASK_OFF[(h, d)], :] = mult * np.exp(-SLOPES[h] * a)
    return m.astype(ml_dtypes.bfloat16)


def build_program():
    nc = bass.Bass("TRN2", target_bir_lowering=False)
    S = Sched(nc)
    ein = lambda name, shape, dt=F32: S.dram(name, shape, dt, kind="ExternalInput")
    xp = ein("xp", [NSEQ * T, DM])
    xsw = ein("xsw", [WT_S * 128, DM])
    flags_d = ein("flags", [128, WT_S])
    edge_d = ein("edge", [128, 2])
    cc_d = ein("cc", [DM, 5])
    ada_w_d = ein("ada_w", [DM, 6 * DM])
    ada_b_d = ein("ada_b", [128, 48])
    n1g_d = ein("n1g", [128, 8])
    n2g_d = ein("n2g", [128, 8])
    fg_d = ein("fgb", [128, DM])
    w_in_d = ein("w_in", [DM, 3072])
    w_out_d = ein("w_out", [DM, DM])
    wg_d = ein("wg", [DM, DFF])
    wu_d = ein("wu", [DM, DFF])
    wd_d = ein("wd", [DFF, DM])
    fcw_d = ein("fcw", [128, NFF, 3])
    fcb_d = ein("fcb", [128, NFF])
    hyg_d = ein("hyg", [128, 4])
    agb_d = ein("agb", [128, HW])
    ident_d = ein("ident", [128, 128])
    masks_d = ein("masks", [128, N_MASK, 128], BF16)
    xsf = ein("xsf", [LS, DM])
    hw1_d = ein("hw1", [33, 64]); hb1_d = ein("hb1", [64, 1]); hsf_d = ein("hsf", [64, 2])
    hw2_d = ein("hw2", [64, 64]); hb2_d = ein("hb2", [64, 1]); hw3_d = ein("hw3", [64, 2048])
    hdec_d = ein("hdec", [128, 16]); hsw_d = ein("hsw", [128, 12, 3]); hsb_d = ein("hsb", [128, 12])
    hskip_d = ein("hskip", [128, 1024])
    feat_d = {"p": ein("feat_p", [33, 2, T]), "s": ein("feat_s", [33, 2, LS])}
    tn_d = {"p": ein("tn_p", [128, 2, T]), "s": ein("tn_s", [128, 2, LS])}
    fc_d = {}
    for cf, P in (("p", 64), ("s", 128)):
        fc_d[cf] = dict(E1=ein("E1_" + cf, [128, 256], BF16), TW1=ein("TW1_" + cf, [P, 256]), W=ein("W_" + cf, [P, 4 * P], BF16),
                        V=ein("V_" + cf, [P, 6 * P], BF16), TW2=ein("TW2_" + cf, [128, 2 * P]), G=ein("G_" + cf, [128, 384], BF16),
                        EF=ein("EF_" + cf, [128, 128] if cf == "p" else [128, 512], BF16),
                        TWF=ein("TWF_" + cf, [P, 128] if cf == "p" else [P, 256]))
    sel_d = ein("sel", [128, 18], BF16)
    selr_d = ein("selr", [128, 128], BF16)
    yp_d = S.dram("yp", [NSEQ * T, DM], F32, kind="ExternalOutput")
    ys_d = S.dram("ys", [CH, DM], F32, kind="ExternalOutput")
    WINF = S.dram("WINF", [24, 128, 8, 128], BF16)
    WV = S.dram("WV", [128, 8, 512], BF16)
    WGU = S.dram("WGU", [NFF, 128, 2, 8, 128], BF16)
    WD = S.dram("WD", [NFF, 128, DM], BF16)
    WO = S.dram("WO", [128, 8, DM], BF16)
    SEG_TOK = [T] * 4 + [(OT1_S - OT0_S + 2) * 128]
    YH = [S.dram("YH%d" % s, [HW, SEG_TOK[s]], BF16) for s in range(5)]
    AT = [S.dram("AT%d" % s, [HW, SEG_TOK[s]], BF16) for s in range(5)]
    UR = {"p": S.dram("UR_p", [1536, NSEQ, T + 2], BF16), "s": S.dram("UR_s", [1536, 1, LS + 2], BF16)}
    UC = {"p": S.dram("UC_p", [1536, NSEQ * T], BF16), "s": S.dram("UC_s", [1536, LS], BF16)}
    HF = {"p": S.dram("HF_p", [1024, 2 * T], BF16), "s": S.dram("HF_s", [1024, 2 * LS], BF16)}
    H2S = {"p": S.dram("H2S_p", [64, 2, T], BF16), "s": S.dram("H2S_s", [64, 2, LS], BF16)}

    ident_f = S.sbuf("ident_f", [128, 128], F32)
    ident_b = S.sbuf("ident_b", [128, 128], BF16)
    ones_f = S.sbuf("ones_f", [128, 128], F32)
    ones_b = S.sbuf("ones_b", [128, 128], BF16)
    modT = S.sbuf("modT", [128, 48, 5], F32)
    SC1 = S.sbuf("SC1", [128, 8, 5], F32)
    SC2 = S.sbuf("SC2", [128, 8, 5], F32)
    fgb = S.sbuf("fgb_t", [128, DM], F32)
    fcw = S.sbuf("fcw_t", [128, NFF, 3], F32)
    fcb = S.sbuf("fcb_t", [128, NFF], F32)
    hyg = S.sbuf("hyg_t", [128, 4], F32)
    agb = S.sbuf("agb_t", [128, HW], F32)
    edge = S.sbuf("edge_t", [128, 2], F32)
    epsb = S.sbuf("epsb", [128, 1], F32)

    S.dma(ident_f[:], ident_d[:], [ident_d], [ident_f])
    S.op("dve", _mk("tensor_copy", out=ident_b[:], in_=ident_f[:]), [ident_f], [ident_b])
    S.op("pool", _mk("memset", ones_f[:], 1.0), [], [ones_f])
    S.op("pool", _mk("memset", ones_b[:], 1.0), [], [ones_b])
    S.op("pool", _mk("memset", epsb[:], EPS), [], [epsb])
    for dst, src in ((fgb, fg_d), (fcw, fcw_d), (fcb, fcb_d), (hyg, hyg_d), (agb, agb_d), (edge, edge_d)):
        S.dma(dst[:], src[:], [src], [dst])

    S.push_pool()
    stg = [S.sbuf("stg%d" % i, [128, 3072], F32) for i in range(2)]
    stb = [S.sbuf("stb%d" % i, [128, 3072], BF16) for i in range(2)]
    cast_i = [0]

    def cast_rows(src_ap, ncols, writes):
        i = cast_i[0] % 2
        cast_i[0] += 1
        st, sb = stg[i], stb[i]
        S.dma(st[:, 0:ncols], src_ap, [], [st])
        eng = "dve" if i == 0 else "pool"
        S.op(eng, _mk("tensor_copy", out=sb[:, 0:ncols], in_=st[:, 0:ncols]), [st], [sb])
        for dbuf, dst_ap, src_view in writes:
            S.dma(dst_ap, src_view(sb), [sb], [dbuf])

    for k in range(8):
        cast_rows(w_in_d[128 * k:128 * (k + 1), :], 3072, [
            (WINF, WINF[:, :, k, :].rearrange("j p m -> p j m"), lambda sb: sb[:, 0:3072].rearrange("p (j m) -> p j m", m=128)),
            (WV, WV[:, k, :], lambda sb: sb[:, 2560:3072]),
        ])
        cast_rows(wg_d[128 * k:128 * (k + 1), :], DFF, [
            (WGU, WGU[:, :, 0, k, :].rearrange("j p m -> p j m"), lambda sb: sb[:, 0:DFF].rearrange("p (j m) -> p j m", m=128)),
        ])
        cast_rows(wu_d[128 * k:128 * (k + 1), :], DFF, [
            (WGU, WGU[:, :, 1, k, :].rearrange("j p m -> p j m"), lambda sb: sb[:, 0:DFF].rearrange("p (j m) -> p j m", m=128)),
        ])
        cast_rows(w_out_d[128 * k:128 * (k + 1), :], DM, [
            (WO, WO[:, k, :], lambda sb: sb[:, 0:DM]),
        ])
    for j in range(NFF):
        cast_rows(wd_d[128 * j:128 * (j + 1), :], DM, [
            (WD, WD[j], lambda sb: sb[:, 0:DM]),
        ])

    cT = S.sbuf("cT", [128, 8, 5], F32)
    scT = S.sbuf("scT", [128, 8, 5], F32)
    abT = S.sbuf("abT", [128, 48], F32)
    n1g = S.sbuf("n1g_t", [128, 8], F32)
    n2g = S.sbuf("n2g_t", [128, 8], F32)
    S.dma(cT[:], cc_d[:].rearrange("(k p) b -> p k b", p=128), [cc_d], [cT])
    S.dma(abT[:], ada_b_d[:], [ada_b_d], [abT])
    S.dma(n1g[:], n1g_d[:], [n1g_d], [n1g])
    S.dma(n2g[:], n2g_d[:], [n2g_d], [n2g])
    S.op("act", _mk("activation", out=scT[:], in_=cT[:], func=AF.Silu), [cT], [scT])
    awt = [S.sbuf("awt%d" % i, [128, 8, 128], F32) for i in range(2)]
    psm = S.psum("psm", [128, 512], F32)
    for j in range(48):
        a = awt[j % 2]
        S.dma(a[:], ada_w_d[:, 128 * j:128 * (j + 1)].rearrange("(k p) m -> p k m", p=128), [], [a])
        for k in range(8):
            S.op("pe", _mk("matmul", psm[:, 5 * j:5 * j + 5], lhsT=a[:, k, :], rhs=scT[:, k, :],
                                                      start=(k == 0), stop=(k == 7)), [a, scT], [psm], acc=(k > 0))
    S.op("dve", _mk("tensor_tensor", out=modT[:], in0=psm[:, 0:240].rearrange("p (j s) -> p j s", s=5),
                                          in1=abT[:].unsqueeze(2).broadcast_to([128, 48, 5]), op=ALU.add), [psm, abT], [modT])
    for (SC, base, ng) in ((SC1, 8, n1g), (SC2, 32, n2g)):
        S.op("dve", _mk("tensor_scalar", out=SC[:], in0=modT[:, base:base + 8, :], scalar1=1.0, scalar2=None,
                                                                  op0=ALU.add), [modT], [SC])
        S.op("dve", _mk("tensor_tensor", out=SC[:], in0=SC[:], in1=ng[:].unsqueeze(2).broadcast_to([128, 8, 5]),
                                                           op=ALU.mult), [SC, ng], [SC])
    S.pop_pool()

    def rstd_from_ss(ss, rs, n, eng_recip="dve"):
        S.op("act", _mk("activation", out=rs[:], in_=ss[:], func=AF.Sqrt, bias=epsb[:], scale=1.0 / n), [ss, epsb], [rs])
        S.op("dve", _mk("reciprocal", out=rs[:], in_=rs[:]), [rs], [rs])

    S.push_pool()
    masks = S.sbuf("masks_t", [128, N_MASK, 128], BF16)
    S.dma(masks[:], masks_d[:], [masks_d], [masks])
    NQMAX = QT1_S - QT0_S
    QT = S.sbuf("QT", [128, 4, NQMAX * 128], BF16)
    KT = S.sbuf("KT", [128, 4, WT_S * 128], BF16)
    VP = S.sbuf("VP", [128, WT_S, NH, HD + 1], BF16)
    flg = S.sbuf("flg", [128, WT_S], F32)
    wv_t = S.sbuf("wv_t", [128, 8, 512], BF16)
    S.dma(wv_t[:].rearrange("p k m -> p (k m)"), WV[:].rearrange("p k m -> p (k m)"), [WV], [wv_t])
    wqk = [S.sbuf("wqk%d" % i, [128, 8, 128], BF16) for i in range(3)]
    xg = [S.sbuf("xg%d" % i, [128, DM], F32) for i in range(3)]
    xn = [S.sbuf("xn%d" % i, [128, DM], BF16) for i in range(2)]
    hTg = [S.sbuf("hTg%d" % i, [128, 8, 512], BF16) for i in range(2)]

    def hk(h_t, k):
        return h_t, k

    ss_t = [S.sbuf("ss%d" % i, [128, 1], F32) for i in range(4)]
    rs_t = [S.sbuf("rs%d" % i, [128, 1], F32) for i in range(4)]
    junk = S.sbuf("junk", [128, DM], BF16)
    ptrs = [S.psum("ptr%d" % i, [128, 1024], BF16) for i in range(2)]
    ptr = ptrs[0]
    pp = [S.psum("pp%d" % i, [128, 512], F32) for i in range(2)]
    psc = [S.psum("psc%d" % i, [128, 512], F32) for i in range(2)]
    po = [S.psum("po%d" % i, [128, 512], F32) for i in range(2)]
    pT = [S.sbuf("pT%d" % i, [128, 512], BF16) for i in range(3)]
    o_t = [S.sbuf("o_t%d" % i, [128, HW], F32) for i in range(2)]
    on_t = [S.sbuf("on_t%d" % i, [128, HW], BF16) for i in range(2)]
    rden = [S.sbuf("rden%d" % i, [128, NH], F32) for i in range(2)]
    aTt = [S.sbuf("aTt%d" % i, [128, 4, 512], BF16) for i in range(2)]
    cnt = {"g": 0, "t": 0, "pp": 0, "psc": 0, "pT": 0, "q": 0, "x": 0, "w": 0}

    ub = [S.sbuf("ub%d" % i, [128, 512], BF16) for i in range(2)]
    zpad = S.sbuf("zpad", [128, 12, 8], BF16)
    S.op("pool", _mk("memset", zpad[:], 0.0), [], [zpad])

    def hy_proj(h_t, N, cf, sq_, tok0):
        for j in range(12):
            w_ = wqk[cnt["w"] % 3]
            cnt["w"] += 1
            S.dma(w_[:].rearrange("p k m -> p (k m)"), WINF[j].rearrange("p k m -> p (k m)"), [WINF], [w_])
            p_ = pp[cnt["pp"] % 2]
            cnt["pp"] += 1
            for k in range(8):
                S.op("pe", _mk("matmul", p_[:, 0:N], lhsT=w_[:, k, :], rhs=hk(h_t, k)[0][:, hk(h_t, k)[1], 0:N], start=(k == 0), stop=(k == 7)), [w_, hk(h_t, k)[0]], [p_])
            u_ = ub[cnt["pp"] % 2]
            S.op("act", _mk("activation", out=u_[:, 0:N], in_=p_[:, 0:N], func=AF.Copy), [p_], [u_])
            S.dma(UR[cf][128 * j:128 * (j + 1), sq_, 1 + tok0:1 + tok0 + N], u_[:, 0:N], [u_], [UR[cf]], q="actq")

    def norm_part(x_t):
        ti = cnt["t"] % 4
        cnt["t"] += 1
        ss, rs, xnb = ss_t[ti], rs_t[ti], xn[ti % 2]
        S.op("act", _mk("activation", out=junk[:], in_=x_t[:], func=AF.Square, accum_out=ss[:]), [x_t], [junk, ss])
        rstd_from_ss(ss, rs, DM)
        S.op("dve", _mk("tensor_scalar", out=xnb[:], in0=x_t[:], scalar1=rs[:], scalar2=None, op0=ALU.mult), [x_t, rs], [xnb])
        return xnb, ti

    def trans_part(xnb, ti, h_t, t, seg):
        ptr_ = ptrs[ti % 2]
        for k in range(8):
            S.op("pe", _mk("transpose", ptr_[:, 128 * k:128 * (k + 1)], xnb[:, 128 * k:128 * (k + 1)], ident_b[:]), [xnb, ident_b], [ptr_])
        for k in range(8):
            dst, kk = hk(h_t, k)
            if k % 2 == 0:
                S.op("act", _mk("activation", out=dst[:, kk, 128 * t:128 * (t + 1)], in_=ptr_[:, 128 * k:128 * (k + 1)],
                                func=AF.Identity, scale=SC1[:, k, seg:seg + 1], bias=modT[:, k, seg:seg + 1]), [ptr_, SC1, modT], [dst])
            else:
                S.op("dve", _mk("tensor_scalar", out=dst[:, kk, 128 * t:128 * (t + 1)], in0=ptr_[:, 128 * k:128 * (k + 1)],
                                scalar1=SC1[:, k, seg:seg + 1], scalar2=modT[:, k, seg:seg + 1], op0=ALU.mult, op1=ALU.add),
                     [ptr_, SC1, modT], [dst])

    def front_tiles(h_t, nt, load_fn, seg):
        prev = None
        for t in range(nt):
            x_t = xg[cnt["x"] % 3]
            cnt["x"] += 1
            load_fn(t, x_t)
            cur = norm_part(x_t)
            if prev is not None:
                trans_part(prev[0], prev[1], h_t, prev[2], seg)
            prev = (cur[0], cur[1], t)
        trans_part(prev[0], prev[1], h_t, prev[2], seg)

    if not HY_STUB:
        for cf, nsq, L_ in (("p", NSEQ, T), ("s", 1, LS)):
            for sq_ in range(nsq):
                for col in (0, L_ + 1):
                    S.dma(UR[cf][:, sq_, col:col + 1].rearrange("(j p) o -> p j o", p=128), zpad[:, :, 0:1], [zpad], [UR[cf]],
                          allow_slow_non_contiguous=True)
        for g0 in range(0, LS // 128, 4):
            h_t = hTg[cnt["g"] % 2]
            cnt["g"] += 1
            front_tiles(h_t, 4, lambda t, x_t, g0=g0: S.dma(x_t[:], xsf[128 * (g0 + t):128 * (g0 + t + 1), :], [xsf], [x_t]), 4)
            hy_proj(h_t, 512, "s", 0, 128 * g0)

    for seg in range(5):
        nwt = SEG_WT[seg]
        q0, q1 = SEG_Q[seg]
        xsrc = xp if seg < 4 else xsw
        xoff = seg * T if seg < 4 else 0
        if seg < 4:
            S.op("pool", _mk("memset", flg[:], 1.0), [], [flg])
        else:
            S.dma(flg[:], flags_d[:], [flags_d], [flg])
        for g0 in range(0, nwt, 4):
            nt = min(4, nwt - g0)
            h_t = hTg[cnt["g"] % 2]
            cnt["g"] += 1
            front_tiles(h_t, nt, lambda t, x_t, g0=g0: S.dma(x_t[:], xsrc[xoff + 128 * (g0 + t):xoff + 128 * (g0 + t + 1), :], [xsrc], [x_t]), seg)
            N = 128 * nt
            tok0 = 128 * g0
            qa, qb = max(g0, q0), min(g0 + nt, q1)
            for j in range(8):
                if j < 4 and qa >= qb:
                    continue
                w_ = wqk[cnt["w"] % 3]
                cnt["w"] += 1
                S.dma(w_[:].rearrange("p k m -> p (k m)"), WINF[12 + j].rearrange("p k m -> p (k m)"), [WINF], [w_])
                p_ = pp[cnt["pp"] % 2]
                cnt["pp"] += 1
                if j < 4:
                    ca, cb = 128 * (qa - g0), 128 * (qb - g0)
                else:
                    ca, cb = 0, N
                for k in range(8):
                    S.op("pe", _mk("matmul",
                        p_[:, 0:cb - ca], lhsT=w_[:, k, :], rhs=hk(h_t, k)[0][:, hk(h_t, k)[1], ca:cb], start=(k == 0), stop=(k == 7)), [w_, hk(h_t, k)[0]], [p_])
                if j < 4:
                    S.op("act", _mk("activation",
                        out=QT[:, j, 128 * (qa - q0):128 * (qa - q0) + cb - ca], in_=p_[:, 0:cb - ca], func=AF.Copy, scale=0.125), [p_], [QT])
                else:
                    S.op("dve", _mk("tensor_copy", out=KT[:, j - 4, tok0:tok0 + N], in_=p_[:, 0:N]), [p_], [KT])
            if seg < 4 and not HY_STUB:
                hy_proj(h_t, N, "p", seg, tok0)
            for t in range(nt):
                p_ = pp[cnt["pp"] % 2]
                cnt["pp"] += 1
                for k in range(8):
                    S.op("pe", _mk("matmul", p_[:, :], lhsT=hk(h_t, k)[0][:, hk(h_t, k)[1], 128 * t:128 * (t + 1)], rhs=wv_t[:, k, :],
                                                                 start=(k == 0), stop=(k == 7)), [wv_t, hk(h_t, k)[0]], [p_])
                wt = g0 + t
                S.op("dve", _mk("tensor_scalar", out=VP[:, wt, :, 0:HD], in0=p_[:, :].rearrange("p (h e) -> p h e", e=HD),
                                                                   scalar1=flg[:, wt:wt + 1], scalar2=None, op0=ALU.mult), [p_, flg], [VP])
                S.op("pool", _mk("tensor_copy", out=VP[:, wt, :, HD:HD + 1],
                                                           in_=flg[:, wt:wt + 1].unsqueeze(1).broadcast_to([128, NH, 1])), [flg], [VP])
        for qt in range(q0, q1):
            qi = cnt["q"] % 2
            cnt["q"] += 1
            qc = 128 * (qt - q0)
            chunks = []
            for h in range(NH):
                kts = [(d, qt + d) for d in head_deltas(h) if 0 <= qt + d < nwt]
                for c0 in range(0, len(kts), 4):
                    chunks.append((h, c0, kts[c0:c0 + 4], len(kts)))

            def emit_scores(ch):
                h, c0, blk, nk = ch
                hp, hb = h // 2, 64 * (h % 2)
                nb = len(blk)
                sc = psc[cnt["psc"] % 2]
                cnt["psc"] += 1
                for bi, (d, kt) in enumerate(blk):
                    S.op("pe", _mk("matmul", sc[:, 128 * bi:128 * (bi + 1)], lhsT=KT[hb:hb + 64, hp, 128 * kt:128 * (kt + 1)],
                                   rhs=QT[hb:hb + 64, hp, qc:qc + 128], start=True, stop=True), [KT, QT], [sc])
                pt_ = pT[cnt["pT"] % 3]
                cnt["pT"] += 1
                S.op("act", _mk("activation", out=pt_[:, 0:128 * nb], in_=sc[:, 0:128 * nb], func=AF.Exp), [sc], [pt_])
                m0 = MASK_OFF[(h, blk[0][0])]
                S.op("dve", _mk("tensor_tensor", out=pt_[:, 0:128 * nb], in0=pt_[:, 0:128 * nb],
                                in1=masks[:, m0:m0 + nb, :].rearrange("p b q -> p (b q)"), op=ALU.mult), [pt_, masks], [pt_])
                return pt_

            def emit_pv(ch, pt_):
                h, c0, blk, nk = ch
                pob, hs = po[h // 4], h % 4
                for bi, (d, kt) in enumerate(blk):
                    first = (c0 == 0 and bi == 0)
                    last = (c0 + bi == nk - 1)
                    S.op("pe", _mk("matmul", pob[:, 65 * hs:65 * hs + 65], lhsT=pt_[:, 128 * bi:128 * (bi + 1)], rhs=VP[:, kt, h, :],
                                   start=first, stop=last), [pt_, VP], [pob])

            prev = None
            for ch in chunks:
                pt_ = emit_scores(ch)
                if prev is not None:
                    emit_pv(*prev)
                prev = (ch, pt_)
            emit_pv(*prev)
            ot, ont, rd = o_t[qi], on_t[qi], rden[qi]
            for half in range(2):
                S.op("dve", _mk("tensor_scalar",
                    out=rd[:, 4 * half:4 * half + 4], in0=po[half][:, 0:260].rearrange("p (h e) -> p h e", e=65)[:, :, 64],
                    scalar1=1e-30, scalar2=None, op0=ALU.add), [po[half]], [rd])
            S.op("dve", _mk("reciprocal", out=rd[:], in_=rd[:]), [rd], [rd])
            for half in range(2):
                S.op("dve", _mk("tensor_tensor",
                    out=ot[:, 256 * half:256 * half + 256].rearrange("p (h e) -> p h e", e=64),
                    in0=po[half][:, 0:260].rearrange("p (h e) -> p h e", e=65)[:, :, 0:64],
                    in1=rd[:, 4 * half:4 * half + 4].unsqueeze(2).broadcast_to([128, 4, 64]), op=ALU.mult), [po[half], rd], [ot])
            ti = cnt["t"] % 4
            cnt["t"] += 1
            ss, rs = ss_t[ti], rs_t[ti]
            S.op("act", _mk("activation", out=junk[:, 0:HW], in_=ot[:], func=AF.Square, accum_out=ss[:]), [ot], [junk, ss])
            rstd_from_ss(ss, rs, HW)
            S.op("dve", _mk("scalar_tensor_tensor", out=ont[:], in0=ot[:], scalar=rs[:], in1=agb[:],
                                                                               op0=ALU.mult, op1=ALU.mult), [ot, rs, agb], [ont])
            grp = (qt - q0) // 4
            a_t = aTt[grp % 2]
            for k in range(4):
                S.op("pe", _mk("transpose", ptr[:, 128 * k:128 * (k + 1)], ont[:, 128 * k:128 * (k + 1)], ident_b[:]),
                     [ont, ident_b], [ptr])
            tl = (qt - q0) % 4
            S.op("dve", _mk("tensor_copy",
                out=a_t[:, :, 128 * tl:128 * (tl + 1)], in_=ptr[:, 0:512].rearrange("p (k t) -> p k t", t=128)), [ptr], [a_t])
            if tl == 3 or qt == q1 - 1:
                ntok = 128 * (tl + 1)
                c0 = 128 * (qt - q0 - tl)
                S.dma(AT[seg][:, c0:c0 + ntok].rearrange("(k p) t -> p k t", p=128), a_t[:, :, 0:ntok], [a_t], [AT[seg]], q="actq")
    S.pop_pool()

    if HY_STUB:
        S.push_pool()
        z = S.sbuf("zz", [128, 4, T + 512], BF16)
        S.op("pool", _mk("memset", z[:], 0.0), [], [z])
        for s in range(5):
            S.dma(YH[s][:, :].rearrange("(k p) t -> p k t", p=128), z[:, :, 0:SEG_TOK[s]], [z], [YH[s]])
        S.pop_pool()
    else:
        S.push_pool()
        PI = math.pi
        pAh = [S.psum("pA%d" % i, [128, 512], F32) for i in range(2)]
        pZh = [S.psum("pZ%d" % i, [128, 512], F32) for i in range(2)]
        pCh = [S.psum("pC%d" % i, [128, 512], F32) for i in range(2)]
        pY = [S.psum("pY%d" % i, [128, 512], F32) for i in range(2)]
        pA, pZ, pB = pAh[0], pZh[0], pAh[1]
        hw1 = S.sbuf("t_hw1", [33, 64], F32); hb1 = S.sbuf("t_hb1", [64, 1], F32); hsf = S.sbuf("t_hsf", [64, 2], F32)
        hw2 = S.sbuf("t_hw2", [64, 64], F32); hb2 = S.sbuf("t_hb2", [64, 1], F32)
        hw3f = S.sbuf("t_hw3f", [64, 2048], F32); hw3b = S.sbuf("t_hw3b", [64, 2048], BF16)
        hdec = S.sbuf("t_hdec", [128, 16], F32); negd = S.sbuf("t_negd", [128, 16], F32)
        hsw = S.sbuf("t_hsw", [128, 12, 3], F32); hsb = S.sbuf("t_hsb", [128, 12], F32)
        skb = S.sbuf("t_skb", [128, 1024], F32)
        sel = S.sbuf("t_sel", [128, 18], BF16)
        for dst, src in ((hw1, hw1_d), (hb1, hb1_d), (hsf, hsf_d), (hw2, hw2_d), (hb2, hb2_d), (hw3f, hw3_d), (hdec, hdec_d),
                         (hsw, hsw_d), (hsb, hsb_d), (skb, hskip_d), (sel, sel_d)):
            S.dma(dst[:], src[:], [src], [dst])
        S.op("dve", _mk("tensor_copy", out=hw3b[:], in_=hw3f[:]), [hw3f], [hw3b])
        S.op("dve", _mk("tensor_scalar", out=negd[:], in0=hdec[:], scalar1=-1.0, scalar2=None, op0=ALU.mult), [hdec], [negd])
        S.op("dve", _mk("tensor_tensor", out=negd[:], in0=negd[:], in1=hdec[:], op=ALU.min), [negd, hdec], [negd])
        RNB = {cf: S.sbuf("RNB_" + cf, [128, 2, 512], F32) for cf in ("p", "s")}
        SB = S.sbuf("SBt", [128, 2048], F32)
        dg3 = S.sbuf("dg3", [128, 128], F32)
        S.push_pool()
        ft = [S.sbuf("ft%d" % i, [33, 2, 512], F32) for i in range(2)]
        tnc = [S.sbuf("tnc%d" % i, [128, 2, 512], F32) for i in range(2)]
        a1 = S.sbuf("a1", [64, 512], F32); wtmp = S.sbuf("wtmp", [64, 512], F32)
        h1 = S.sbuf("h1", [64, 512], F32); h2b = S.sbuf("h2b", [64, 512], BF16)
        Et = [S.sbuf("Et%d" % i, [128, 512], F32) for i in range(2)]
        hq = [S.sbuf("hq%d" % i, [128, 512], F32) for i in range(2)]
        hqb = [S.sbuf("hqb%d" % i, [128, 512], BF16) for i in range(2)]
        junkf = S.sbuf("junkf", [128, 512], BF16)
        asum = S.sbuf("asum", [128, 16, 32], F32)
        tot = S.sbuf("tot", [128, 16], F32)

        def sin_layer(ps, bias, sfc, out_t):
            S.op("dve", _mk("tensor_scalar", out=a1[:], in0=ps[0:64, :], scalar1=bias[:, 0:1], scalar2=sfc, op0=ALU.add, op1=ALU.mult),
                 [ps, bias, hsf], [a1])
            S.op("dve", _mk("tensor_scalar", out=wtmp[:], in0=a1[:], scalar1=PI, scalar2=-2.0 * PI, op0=ALU.is_gt, op1=ALU.mult), [a1], [wtmp])
            S.op("dve", _mk("tensor_tensor", out=a1[:], in0=a1[:], in1=wtmp[:], op=ALU.add), [a1, wtmp], [a1])
            S.op("dve", _mk("tensor_scalar", out=wtmp[:], in0=a1[:], scalar1=-PI, scalar2=2.0 * PI, op0=ALU.is_lt, op1=ALU.mult), [a1], [wtmp])
            S.op("dve", _mk("tensor_tensor", out=a1[:], in0=a1[:], in1=wtmp[:], op=ALU.add), [a1, wtmp], [a1])
            S.op("act", _mk("activation", out=out_t[:], in_=a1[:], func=AF.Sin), [a1], [out_t])

        ur = [S.sbuf("ur%d" % i, [128, 2050], BF16) for i in range(2)]
        ct = [S.sbuf("ct%d" % i, [128, 2048], F32) for i in range(2)]
        ucb = [S.sbuf("ucb%d" % i, [128, 2048], BF16) for i in range(2)]
        conv_items = []

        def mk_conv(cf, sq_, L_, pc, j, ci):
            def ld():
                u_ = ur[ci % 2]
                S.dma(u_[:], UR[cf][128 * j:128 * (j + 1), sq_, 2048 * pc:2048 * pc + 2050], [UR[cf]], [u_])

            def f():
                u_, c_, o_ = ur[ci % 2], ct[ci % 2], ucb[ci % 2]
                S.op("pool", _mk("tensor_scalar", out=c_[:], in0=u_[:, 1:2049], scalar1=hsw[:, j, 1:2], scalar2=hsb[:, j:j + 1],
                                 op0=ALU.mult, op1=ALU.add), [u_, hsw, hsb], [c_])
                S.op("dve", _mk("scalar_tensor_tensor", out=c_[:], in0=u_[:, 0:2048], scalar=hsw[:, j, 0:1], in1=c_[:],
                                op0=ALU.mult, op1=ALU.add), [u_, hsw, c_], [c_])
                S.op("dve", _mk("scalar_tensor_tensor", out=o_[:], in0=u_[:, 2:2050], scalar=hsw[:, j, 2:3], in1=c_[:],
                                op0=ALU.mult, op1=ALU.add), [u_, hsw, c_], [o_])
                S.dma(UC[cf][128 * j:128 * (j + 1), sq_ * L_ + 2048 * pc: sq_ * L_ + 2048 * (pc + 1)], o_[:], [o_], [UC[cf]], q="actq")
            return ld, f

        _ci = 0
        _pairs = []
        for cf, nsq, L_ in (("p", NSEQ, T), ("s", 1, LS)):
            for sq_ in range(nsq):
                for pc in range(L_ // 2048):
                    for j in range(12):
                        _ci += 1
                        _pairs.append(mk_conv(cf, sq_, L_, pc, j, _ci))
        _pairs[0][0]()
        for i_, (ld_, f_) in enumerate(_pairs):
            nld = _pairs[i_ + 1][0] if i_ + 1 < len(_pairs) else None
            conv_items.append((lambda f_=f_, nld=nld: (nld() if nld else None, f_())))

        hqb3 = [S.sbuf("hqb3_%d" % i, [128, 512], BF16) for i in range(3)]
        NCH = 4
        ma = [S.sbuf("ma%d" % i, [64, 512], F32) for i in range(NCH)]
        mw = [S.sbuf("mw%d" % i, [64, 512], F32) for i in range(NCH)]
        mh1 = [S.sbuf("mh1_%d" % i, [64, 512], F32) for i in range(NCH)]
        mh2 = [S.sbuf("mh2_%d" % i, [64, 512], BF16) for i in range(NCH)]
        mps = [pAh[0], pAh[1], pZh[0], pZh[1]]
        h2l = [S.sbuf("h2l%d" % i, [64, 2, 512], BF16) for i in range(3)]
        fc = {"ci": 0}

        def wrap_sin(ps, bias, sfc, a_, w_, out_t):
            S.op("dve", _mk("tensor_scalar", out=a_[:], in0=ps[0:64, :], scalar1=bias[:, 0:1], scalar2=sfc, op0=ALU.add, op1=ALU.mult),
                 [ps, bias, hsf], [a_])
            S.op("dve", _mk("tensor_scalar", out=w_[:], in0=a_[:], scalar1=PI, scalar2=-2.0 * PI, op0=ALU.is_gt, op1=ALU.mult), [a_], [w_])
            S.op("dve", _mk("tensor_tensor", out=a_[:], in0=a_[:], in1=w_[:], op=ALU.add), [a_, w_], [a_])
            S.op("dve", _mk("tensor_scalar", out=w_[:], in0=a_[:], scalar1=-PI, scalar2=2.0 * PI, op0=ALU.is_lt, op1=ALU.mult), [a_], [w_])
            S.op("dve", _mk("tensor_tensor", out=a_[:], in0=a_[:], in1=w_[:], op=ALU.add), [a_, w_], [a_])
            S.op("act", _mk("activation", out=out_t[:], in_=a_[:], func=AF.Sin), [a_], [out_t])

        for cf, L_ in (("p", T), ("s", LS)):
            nch = L_ // 512
            for pc0 in range(0, nch, 2):
                jobs = []
                for pc in (pc0, pc0 + 1):
                    f_ = ft[pc % 2]
                    S.dma(f_[:], feat_d[cf][:, :, 512 * pc:512 * (pc + 1)], [], [f_])
                    for d in range(2):
                        jobs.append((pc, d, f_))
                for i, (pc, d, f_) in enumerate(jobs):
                    S.op("pe", _mk("matmul", mps[i][0:64, :], lhsT=hw1[:], rhs=f_[:, d, :], start=True, stop=True), [hw1, f_], [mps[i]])
                for i, (pc, d, f_) in enumerate(jobs):
                    wrap_sin(mps[i], hb1, hsf[:, 0:1], ma[i], mw[i], mh1[i])
                for i, (pc, d, f_) in enumerate(jobs):
                    S.op("pe", _mk("matmul", mps[i][0:64, :], lhsT=hw2[:], rhs=mh1[i][:], start=True, stop=True), [hw2, mh1[i]], [mps[i]])
                for i, (pc, d, f_) in enumerate(jobs):
                    wrap_sin(mps[i], hb2, hsf[:, 1:2], ma[i], mw[i], mh2[i])
                for i, (pc, d, f_) in enumerate(jobs):
                    S.dma(H2S[cf][:, d, 512 * pc:512 * (pc + 1)], mh2[i][:], [mh2[i]], [H2S[cf]], q="actq")
                if conv_items:
                    conv_items.pop(0)()

            def load_chunk(pc):
                h_, t_ = h2l[pc % 3], tnc[pc % 2]
                S.dma(h_[:], H2S[cf][:, :, 512 * pc:512 * (pc + 1)], [H2S[cf]], [h_])
                S.dma(t_[:], tn_d[cf][:, :, 512 * pc:512 * (pc + 1)], [], [t_])

            def rc_front(pc, rc):
                fc["ci"] += 1
                ci = fc["ci"]
                d = (rc // 4) % 2
                p3, E_, q_ = pY[ci % 2], Et[ci % 2], hq[ci % 2]
                h_, t_ = h2l[pc % 3], tnc[pc % 2]
                S.op("pe", _mk("matmul", p3[:], lhsT=hw3b[:, 128 * rc:128 * (rc + 1)], rhs=h_[:, d, :], start=True, stop=True), [hw3b, h_], [p3])
                S.op("act", _mk("activation", out=E_[:], in_=t_[:, d, :], func=AF.Exp, scale=negd[:, rc:rc + 1]), [t_, negd], [E_])
                S.op("dve", _mk("tensor_tensor", out=q_[:], in0=p3[:], in1=E_[:], op=ALU.mult), [p3, E_], [q_])
                return q_, ci

            def rc_back(pc, rc, q_, ci):
                o_, d, cch = rc // 8, (rc // 4) % 2, rc % 4
                qb_ = hqb3[ci % 3]
                S.op("act", _mk("activation", out=junkf[:], in_=q_[:], func=AF.Abs, accum_out=asum[:, rc, pc:pc + 1]), [q_], [junkf, asum])
                S.op("pool", _mk("tensor_copy", out=qb_[:], in_=q_[:]), [q_], [qb_])
                if d == 1 and pc == 0:
                    S.op("pool", _mk("memset", qb_[:, 0:1], 0.0), [], [qb_])
                r0 = 512 * o_ + 128 * cch
                col0 = (0 if d == 1 else L_) + 512 * pc
                S.dma(HF[cf][r0:r0 + 128, col0:col0 + 512], qb_[:], [qb_], [HF[cf]])

            load_chunk(0)
            for pc in range(nch):
                if pc + 1 < nch:
                    load_chunk(pc + 1)
                cur = rc_front(pc, 0)
                for rc in range(16):
                    nxt = rc_front(pc, rc + 1) if rc + 1 < 16 else None
                    rc_back(pc, rc, *cur)
                    cur = nxt
                    if rc % 4 == 3 and conv_items:
                        conv_items.pop(0)()
            S.op("dve", _mk("tensor_reduce", out=tot[:], in_=asum[:, :, 0:nch], op=ALU.add, axis=mybir.AxisListType.X), [asum], [tot])
            for rc in range(16):
                S.op("dve", _mk("tensor_scalar", out=dg3[:], in0=ident_f[:], scalar1=tot[:, rc:rc + 1], scalar2=None, op0=ALU.mult),
                     [ident_f, tot], [dg3])
                S.op("pe", _mk("matmul", pB[:, 128 * (rc % 4):128 * (rc % 4 + 1)], lhsT=ones_f[:], rhs=dg3[:], start=True, stop=True),
                     [ones_f, dg3], [pB])
                if rc % 4 == 3:
                    S.op("act", _mk("activation", out=SB[:, 128 * (rc - 3):128 * (rc + 1)], in_=pB[:], func=AF.Copy), [pB], [SB])
            sbv = SB[:].rearrange("p (o d c) -> p o d c", o=2, d=2)
            S.op("dve", _mk("tensor_tensor", out=RNB[cf][:], in0=sbv[:, :, 0, :], in1=sbv[:, :, 1, :], op=ALU.add), [SB], [RNB[cf]])
            S.op("dve", _mk("reciprocal", out=RNB[cf][:], in_=RNB[cf][:]), [RNB[cf]], [RNB[cf]])

        while conv_items:
            conv_items.pop(0)()
        S.pop_pool()
        GC, NB, NHF = 16, 2, 8
        HR = list(range(NHF))
        Xb = [[S.sbuf("Xb%d_%d" % (h, i), [128, NB, 2, 128], BF16) for i in range(1)] for h in HR]
        Kc = [[S.sbuf("Kc%d%d" % (h, i), [128, NB, 2, 2, 128], F32) for i in range(1)] for h in HR]
        gt = [S.sbuf("gt%d" % h, [128, NB, 128], F32) for h in HR]
        z1 = [S.sbuf("z1_%d" % h, [128, NB, 128], BF16) for h in HR]
        pH = [pAh[0], pAh[1], pZh[0], pZh[1], pCh[0], pCh[1], pY[0], pY[1]]
        pAh = pZh = pCh = pYh = pH
        cmi = [0] * NHF
        xbi = [0] * NHF

        def next_xb(h):
            xbi[h] += 1
            return Xb[h][0]

        T1b = [[S.sbuf("T1b%d_%d" % (h, i), [128, NB, 2, 128], BF16) for i in range(1)] for h in HR]
        T2b = [[S.sbuf("T2b%d_%d" % (h, i), [128, NB, 2, 128], BF16) for i in range(1)] for h in HR]

        def cprod(h, src4, R, W, cr3, ci3, src_buf, cbufs):
            cmi[h] += 1
            a_, b_ = T1b[h][0], T2b[h][0]
            crb = cr3.unsqueeze(2).broadcast_to([R, NB, 2, W])
            cib = ci3.unsqueeze(2).broadcast_to([R, NB, 2, W])
            S.op("dve", _mk("tensor_tensor", out=a_[0:R, :, :, 0:W], in0=src4, in1=crb, op=ALU.mult), [src_buf] + cbufs, [a_])
            S.op("dve", _mk("tensor_tensor", out=b_[0:R, :, :, 0:W], in0=src4, in1=cib, op=ALU.mult), [src_buf] + cbufs, [b_])
            return a_, b_

        for cf, P in (("p", 64), ("s", 128)):
            S.push_pool()
            E1 = S.sbuf("E1" + cf, [128, 256], BF16); TW1 = S.sbuf("TW1" + cf, [P, 2, 128], F32)
            Wc_ = S.sbuf("W" + cf, [P, 4 * P], BF16); Vc = S.sbuf("V" + cf, [P, 6 * P], BF16)
            TW2 = S.sbuf("TW2" + cf, [128, 2, P], F32); Gc = S.sbuf("G" + cf, [128, 384], BF16)
            WF = 64 if cf == "p" else 128
            EF = S.sbuf("EF" + cf, [128, 128] if cf == "p" else [128, 2, 256], BF16)
            TWF = S.sbuf("TWF" + cf, [P, 2, WF], F32)
            S.dma(E1[:], fc_d[cf]["E1"][:], [], [E1])
            S.dma(TW1[:], fc_d[cf]["TW1"][:].rearrange("p (c w) -> p c w", c=2), [], [TW1])
            S.dma(Wc_[:], fc_d[cf]["W"][:], [], [Wc_])
            S.dma(Vc[:], fc_d[cf]["V"][:], [], [Vc])
            S.dma(TW2[:], fc_d[cf]["TW2"][:].rearrange("p (c w) -> p c w", c=2), [], [TW2])
            S.dma(Gc[:], fc_d[cf]["G"][:], [], [Gc])
            if cf == "p":
                S.dma(EF[:], fc_d[cf]["EF"][:], [], [EF])
            else:
                S.dma(EF[:], fc_d[cf]["EF"][:].rearrange("p (k w) -> p k w", k=2), [], [EF])
            S.dma(TWF[:], fc_d[cf]["TWF"][:].rearrange("p (c w) -> p c w", c=2), [], [TWF])
            if cf == "p":
                hf = [S.sbuf("hf%s%d" % (cf, i), [128, GC, P], BF16) for i in range(2)]
            else:
                hf = [S.sbuf("hf%s%d" % (cf, i), [128, 2, 2, GC, P], BF16) for i in range(2)]
            vx = [S.sbuf("vx%s%d" % (cf, i), [128, 3, GC, P], BF16) for i in range(2)]
            zo = [S.sbuf("zo%s%d" % (cf, i), [128, GC, P], BF16) for i in range(2)]
            zs = [S.sbuf("zs%s%d" % (cf, i), [18, GC, 128], BF16) for i in range(2)]

            def s2_complex(h, t1_, t2_, Wd):
                pz = pZh[h][0:P, :].rearrange("p (g c w) -> p g c w", g=NB, c=2)
                Wre, Wim, nWim, nWre = (Wc_[:, 0:P], Wc_[:, P:2 * P], Wc_[:, 2 * P:3 * P], Wc_[:, 3 * P:4 * P])
                for ri, terms in ((0, ((Wre, t1_, 0), (nWre, t2_, 1), (nWim, t2_, 0), (nWim, t1_, 1))),
                                  (1, ((Wre, t2_, 0), (Wre, t1_, 1), (Wim, t1_, 0), (nWim, t2_, 1)))):
                    for ti, (w_, tb, c_) in enumerate(terms):
                        S.op("pe", _mk("matmul", pz[:, :, ri, 0:Wd], lhsT=w_, rhs=tb[0:P, :, c_, 0:Wd], start=(ti == 0), stop=(ti == 3)),
                             [Wc_, tb], [pZh[h]])
                return pz

            def combine(h, t1_, t2_, R, W):
                xb = next_xb(h)
                S.op("pool", _mk("tensor_tensor", out=xb[0:R, :, 0, 0:W], in0=t1_[0:R, :, 0, 0:W], in1=t2_[0:R, :, 1, 0:W], op=ALU.subtract), [t1_, t2_], [xb])
                S.op("pool", _mk("tensor_tensor", out=xb[0:R, :, 1, 0:W], in0=t1_[0:R, :, 1, 0:W], in1=t2_[0:R, :, 0, 0:W], op=ALU.add), [t1_, t2_], [xb])
                return xb

            for c0 in range(0, HW, GC):
                bi_ = (c0 // GC) % 2
                hf_, vx_, zo_, zs_ = hf[bi_], vx[bi_], zo[bi_], zs[bi_]
                L_ = T if cf == "p" else LS
                for o in range(2):
                    src_rows = HF[cf][512 * o + c0:512 * o + c0 + GC, :]
                    if cf == "p":
                        S.dma(hf_[64 * o:64 * (o + 1), :, :], src_rows.rearrange("g (i p) -> i g p", p=P), [HF[cf]], [hf_])
                    else:
                        for kc in range(2):
                            S.dma(hf_[:, o, kc, :, :], src_rows[:, L_ * kc:L_ * (kc + 1)].rearrange("g (i p) -> i g p", p=P), [HF[cf]], [hf_])
                for w3 in range(3):
                    S.dma(vx_[:, w3, :, :], UC[cf][512 * w3 + c0:512 * w3 + c0 + GC, :].rearrange("g (r p) -> r g p", p=P), [UC[cf]], [vx_])
                for q4 in range(1):
                    gi = 0
                    cbase = [c0 + NB * h for h in HR]
                    gl = [NB * h for h in HR]
                    for o in ([None] if cf == "p" else [0, 1]):
                        for h in HR:
                            for g in range(NB):
                                if cf == "p":
                                    S.op("pe", _mk("matmul", pAh[h][0:P, 256 * g:256 * g + 128], lhsT=hf_[:, gl[h] + g, :], rhs=EF[:], start=True, stop=True),
                                         [hf_, EF], [pAh[h]])
                                else:
                                    for kc in range(2):
                                        S.op("pe", _mk("matmul", pAh[h][:, 256 * g:256 * (g + 1)], lhsT=hf_[:, o, kc, gl[h] + g, :], rhs=EF[:, kc, :],
                                                       start=(kc == 0), stop=(kc == 1)), [hf_, EF], [pAh[h]])
                        fps = {}
                        for h in HR:
                            src4 = pAh[h][0:P, :].rearrange("p (g c w) -> p g c w", g=NB, c=2)[:, :, :, 0:WF] if cf == "s" else \
                                pAh[h][0:P, :].rearrange("p (g x) -> p g x", g=NB)[:, :, 0:128].rearrange("p g (c w) -> p g c w", c=2)
                            fps[h] = cprod(h, src4, P, WF, TWF[:, 0, :].unsqueeze(1).broadcast_to([P, NB, WF]),
                                           TWF[:, 1, :].unsqueeze(1).broadcast_to([P, NB, WF]), pAh[h], [TWF])
                        for h in HR:
                            pz = s2_complex(h, fps[h][0], fps[h][1], WF)
                            for g in range(NB):
                                c = cbase[h] + g
                                if cf == "p":
                                    for oo in range(2):
                                        S.op("act", _mk("activation", out=Kc[h][gi][0:P, g, oo, :, :].rearrange("p c (s f) -> p c s f", s=4),
                                                        in_=pz[:, g, :, 32 * oo:32 * (oo + 1)].unsqueeze(2).broadcast_to([P, 2, 4, 32]),
                                                        func=AF.Copy, scale=RNB[cf][0:P, oo, c:c + 1]), [pZh[h], RNB[cf]], [Kc[h][gi]])
                                else:
                                    S.op("act", _mk("activation", out=Kc[h][gi][:, g, o, :, :], in_=pz[:, g, :, :], func=AF.Copy,
                                                    scale=RNB[cf][:, o, c:c + 1]), [pZh[h], RNB[cf]], [Kc[h][gi]])
                    for o in range(2):
                        for h in HR:
                            for g in range(NB):
                                zin = vx_[:, 0, gl[h] + g, :] if o == 0 else z1[h][:, g, 0:P]
                                S.op("pe", _mk("matmul", pAh[h][0:P, 256 * g:256 * (g + 1)], lhsT=zin, rhs=E1[:], start=True, stop=True),
                                     [vx_ if o == 0 else z1[h], E1], [pAh[h]])
                        aps, yhs, ccs = {}, {}, {}
                        for h in HR:
                            aps[h] = cprod(h, pAh[h][0:P, :].rearrange("p (g c w) -> p g c w", g=NB, c=2), P, 128,
                                           TW1[:, 0, :].unsqueeze(1).broadcast_to([P, NB, 128]), TW1[:, 1, :].unsqueeze(1).broadcast_to([P, NB, 128]),
                                           pAh[h], [TW1])
                        for h in HR:
                            s2_complex(h, aps[h][0], aps[h][1], 128)
                        for h in HR:
                            yhs[h] = cprod(h, pZh[h][0:P, :].rearrange("p (g c w) -> p g c w", g=NB, c=2), P, 128,
                                           Kc[h][gi][0:P, :, o, 0, :], Kc[h][gi][0:P, :, o, 1, :], pZh[h], [Kc[h][gi]])
                        for h in HR:
                            yhs[h] = combine(h, yhs[h][0], yhs[h][1], P, 128)
                        for h in HR:
                            pc_ = pCh[h][:, :].rearrange("p (g x) -> p g x", g=NB)
                            VA, VB = Vc[:, 0:2 * P], Vc[:, 2 * P:4 * P]
                            for g in range(NB):
                                S.op("pe", _mk("matmul", pc_[:, g, 0:2 * P], lhsT=yhs[h][0:P, g, 0, :], rhs=VA, start=True, stop=False), [yhs[h], Vc], [pCh[h]])
                                S.op("pe", _mk("matmul", pc_[:, g, 0:2 * P], lhsT=yhs[h][0:P, g, 1, :], rhs=VB, start=False, stop=True), [yhs[h], Vc], [pCh[h]])
                        for h in HR:
                            src4 = pCh[h][:, :].rearrange("p (g x) -> p g x", g=NB)[:, :, 0:2 * P].rearrange("p g (c w) -> p g c w", c=2)
                            ccs[h] = cprod(h, src4, 128, P, TW2[:, 0, :].unsqueeze(1).broadcast_to([128, NB, P]),
                                           TW2[:, 1, :].unsqueeze(1).broadcast_to([128, NB, P]), pCh[h], [TW2])
                        for h in HR:
                            ccs[h] = combine(h, ccs[h][0], ccs[h][1], 128, P)
                        for h in HR:
                            Gre, Gim = Gc[:, 0:128], Gc[:, 128:256]
                            py3 = pYh[h][:, 0:128 * NB].rearrange("p (g x) -> p g x", g=NB)[:, :, 0:P]
                            S.op("pe", _mk("matmul", py3, lhsT=Gre, rhs=ccs[h][:, :, 0, 0:P], start=True, stop=False), [Gc, ccs[h]], [pYh[h]])
                            S.op("pe", _mk("matmul", py3, lhsT=Gim, rhs=ccs[h][:, :, 1, 0:P], start=False, stop=True), [Gc, ccs[h]], [pYh[h]])
                        for h in HR:
                            cb = 512 * o + cbase[h]
                            zin3 = vx_[:, 0, gl[h]:gl[h] + NB, :] if o == 0 else z1[h][:, :, 0:P]
                            xg3 = vx_[:, 1 + o, gl[h]:gl[h] + NB, :]
                            for g in range(NB):
                                S.op("act", _mk("activation", out=gt[h][:, g, 0:P], in_=zin3[:, g, :], func=AF.Copy, scale=skb[:, cb + g:cb + g + 1]),
                                     [vx_ if o == 0 else z1[h], skb], [gt[h]])
                            S.op("dve", _mk("tensor_tensor", out=gt[h][:, :, 0:P], in0=pYh[h][:, 0:128 * NB].rearrange("p (g x) -> p g x", g=NB)[:, :, 0:P],
                                            in1=gt[h][:, :, 0:P], op=ALU.add), [pYh[h], gt[h]], [gt[h]])
                            dst = z1[h][:, :, 0:P] if o == 0 else zo_[:, gl[h]:gl[h] + NB, :]
                            S.op("pool", _mk("tensor_tensor", out=dst, in0=gt[h][:, :, 0:P], in1=xg3, op=ALU.mult), [gt[h], vx_], [z1[h] if o == 0 else zo_])
                    if cf == "s":
                        for h in HR:
                            for g in range(NB):
                                S.op("pe", _mk("matmul", pAh[h][0:18, 128 * g:128 * (g + 1)], lhsT=sel[:], rhs=zo_[:, gl[h] + g, :], start=True, stop=True),
                                     [sel, zo_], [pAh[h]])
                            S.op("act", _mk("activation", out=zs_[:, gl[h]:gl[h] + NB, :], in_=pAh[h][0:18, 0:128 * NB].rearrange("p (g x) -> p g x", g=NB),
                                            func=AF.Copy), [pAh[h]], [zs_])
                if cf == "p":
                    for sq_ in range(NSEQ):
                        S.dma(YH[sq_][c0:c0 + GC, :].rearrange("g (i p) -> i g p", p=P), zo_[32 * sq_:32 * (sq_ + 1), :, :], [zo_], [YH[sq_]], q="actq")
                else:
                    S.dma(YH[4][c0:c0 + GC, :].rearrange("g (r p) -> r g p", p=128), zs_[:], [zs_], [YH[4]], q="actq")
            S.pop_pool()
        S.pop_pool()

    S.push_pool()
    X1S = S.dram("X1S", [18 * 128, DM], F32)
    G1B = S.sbuf("G1B", [128, DM], F32)
    G2B = S.sbuf("G2B", [128, DM], F32)
    dg = S.sbuf("dg", [128, 128], F32)
    hT2 = S.sbuf("hT2", [128, 8, 18 * 128 + 2], BF16)
    x1 = [S.sbuf("x1_%d" % i, [128, DM], F32) for i in range(3)]
    x1r = [S.sbuf("x1r%d" % i, [128, DM], F32) for i in range(2)]
    xin = [S.sbuf("xin%d" % i, [128, DM], F32) for i in range(2)]
    mixT = S.sbuf("mixT", [128, 8, 512], BF16)
    yh_t = S.sbuf("yh", [128, 4, 512], BF16)
    sq = S.sbuf("sq", [128, 4, 512], BF16)
    rsb = S.sbuf("rsb", [128, 512], F32)
    wo_t = S.sbuf("wo_t", [128, 8, DM], BF16)
    S.dma(wo_t[:].rearrange("p k m -> p (k m)"), WO[:].rearrange("p k m -> p (k m)"), [WO], [wo_t])
    wgu_t = [S.sbuf("wgu_t%d" % i, [128, 2, 8, 128], BF16) for i in range(4)]
    wd_t = [S.sbuf("wd_t%d" % i, [128, DM], BF16) for i in range(3)]
    aT = S.sbuf("aT", [128, NFF, 512], BF16)
    gext = [S.sbuf("gext%d" % i, [128, 514], F32) for i in range(3)]
    upsb = [S.sbuf("upsb%d" % i, [128, 512], F32) for i in range(3)]
    cvs = [S.sbuf("cv%d" % i, [128, 512], F32) for i in range(2)]
    sgs = [S.sbuf("sg%d" % i, [128, 512], F32) for i in range(2)]
    tmp = [S.sbuf("tmp%d" % i, [128, 512], F32) for i in range(2)]
    xn2 = [S.sbuf("xn2_%d" % i, [128, DM], BF16) for i in range(2)]
    x2 = S.sbuf("x2", [128, DM], F32)
    yo = [S.sbuf("yo%d" % i, [128, DM], F32) for i in range(2)]
    ss2 = [S.sbuf("ss2_%d" % i, [128, 1], F32) for i in range(4)]
    rs2 = [S.sbuf("rs2_%d" % i, [128, 1], F32) for i in range(4)]
    junk2 = S.sbuf("junk2", [128, DM], BF16)
    p_ss = S.psum("p_ss", [128, 512], F32)
    p_o = [S.psum("p_o%d" % i, [128, 512], F32) for i in range(2)]
    p_trb = S.psum("p_trf", [128, 512], F32)
    p_tr = p_trb[:, :].bitcast(BF16)
    p_g = S.psum("p_g", [128, 512], F32)
    p_h = S.psum("p_h", [128, 512], F32)
    p_us = [S.psum("p_u%d" % i, [128, 512], F32) for i in range(2)]
    p_u = p_us[0]
    p_b = p_ss
    c2 = {"o": 0, "t": 0, "w": 0, "e": 0, "x1": 0, "xin": 0, "y": 0, "g": 0, "wd": 0}

    for seg in range(5):
        o0, o1 = SEG_O[seg]
        q0, q1 = SEG_Q[seg]
        na = q1 - q0
        hal = o0 - q0
        xsrc = xp if seg < 4 else xsw
        xoff = seg * T if seg < 4 else 128 * q0
        ydst = yp_d if seg < 4 else ys_d
        yoff = seg * T if seg < 4 else 0
        for (GB, base) in ((G1B, 16), (G2B, 40)):
            for k in range(8):
                S.op("dve", _mk("tensor_scalar", out=dg[:], in0=ident_f[:], scalar1=modT[:, base + k, seg:seg + 1],
                                                                      scalar2=None, op0=ALU.mult), [ident_f, modT], [dg])
                S.op("pe", _mk("matmul", p_b[:, 128 * (k % 4):128 * (k % 4 + 1)], lhsT=ones_f[:], rhs=dg[:], start=True, stop=True),
                     [ones_f, dg], [p_b])
                if k % 4 == 3:
                    S.op("act", _mk("activation", out=GB[:, 128 * (k - 3):128 * (k + 1)], in_=p_b[:], func=AF.Copy),
                         [p_b], [GB])
        S.op("pool", _mk("memset", hT2[:, :, 0:1], 0.0), [], [hT2])
        S.op("pool", _mk("memset", hT2[:, :, 1 + 128 * na:2 + 128 * na], 0.0), [], [hT2])

        def stage_a(ta0, nta):
            N = 128 * nta
            c0 = 128 * ta0
            mx = mixT
            S.dma(yh_t[:, :, 0:N], YH[seg][:, c0:c0 + N].rearrange("(k p) t -> p k t", p=128), [YH[seg]], [yh_t])
            S.dma(mx[:, 4:8, 0:N], AT[seg][:, c0:c0 + N].rearrange("(k p) t -> p k t", p=128), [AT[seg]], [mx])
            S.op("pool", _mk("tensor_tensor", out=sq[:, :, 0:N], in0=yh_t[:, :, 0:N], in1=yh_t[:, :, 0:N], op=ALU.mult), [yh_t], [sq])
            for k in range(4):
                S.op("pe", _mk("matmul", p_ss[:, 0:N], lhsT=ones_b[:], rhs=sq[:, k, 0:N], start=(k == 0), stop=(k == 3)),
                     [ones_b, sq], [p_ss])
            S.op("act", _mk("activation", out=rsb[:, 0:N], in_=p_ss[:, 0:N], func=AF.Sqrt, bias=epsb[:], scale=1.0 / HW),
                 [p_ss, epsb], [rsb])
            S.op("dve", _mk("reciprocal", out=rsb[:, 0:N], in_=rsb[:, 0:N]), [rsb], [rsb])
            for k in range(4):
                S.op("dve", _mk("scalar_tensor_tensor", out=mx[:, k, 0:N], in0=yh_t[:, k, 0:N], scalar=hyg[:, k:k + 1],
                                                                  in1=rsb[:, 0:N], op0=ALU.mult, op1=ALU.mult), [yh_t, hyg, rsb], [mx])
            def part1(t):
                xi = xin[c2["xin"] % 2]
                c2["xin"] += 1
                x1t = x1[c2["x1"] % 3]
                c2["x1"] += 1
                r0 = xoff + c0 + 128 * t
                S.dma(xi[:], xsrc[r0:r0 + 128, :], [xsrc], [xi])
                for nh_ in range(2):
                    p_ = p_o[c2["o"] % 2]
                    c2["o"] += 1
                    for k in range(8):
                        S.op("pe", _mk("matmul", p_[:], lhsT=mx[:, k, 128 * t:128 * (t + 1)], rhs=wo_t[:, k, 512 * nh_:512 * (nh_ + 1)],
                                       start=(k == 0), stop=(k == 7)), [mx, wo_t], [p_])
                    tm = tmp[c2["e"] % 2]
                    c2["e"] += 1
                    S.op("dve", _mk("tensor_tensor", out=tm[:], in0=p_[:], in1=G1B[:, 512 * nh_:512 * (nh_ + 1)], op=ALU.mult), [p_, G1B], [tm])
                    S.op("pool", _mk("tensor_tensor", out=x1t[:, 512 * nh_:512 * (nh_ + 1)], in0=xi[:, 512 * nh_:512 * (nh_ + 1)], in1=tm[:],
                                     op=ALU.add), [xi, tm], [x1t])
                S.dma(X1S[c0 + 128 * t:c0 + 128 * (t + 1), :], x1t[:], [x1t], [X1S], q="actq")
                return x1t

            def part2(t, x1t):
                ti = c2["t"] % 4
                c2["t"] += 1
                ss, rs, xnb = ss2[ti], rs2[ti], xn2[ti % 2]
                S.op("act", _mk("activation", out=junk2[:], in_=x1t[:], func=AF.Square, accum_out=ss[:]), [x1t], [junk2, ss])
                rstd_from_ss(ss, rs, DM)
                S.op("dve", _mk("tensor_scalar", out=xnb[:], in0=x1t[:], scalar1=rs[:], scalar2=None, op0=ALU.mult), [x1t, rs], [xnb])
                for k in range(8):
                    S.op("pe", _mk("transpose", p_tr[:, 128 * k:128 * (k + 1)], xnb[:, 128 * k:128 * (k + 1)], ident_b[:]), [xnb, ident_b], [p_trb])
                col = 1 + c0 + 128 * t
                for k in range(8):
                    if k % 2 == 0:
                        S.op("act", _mk("activation", out=hT2[:, k, col:col + 128], in_=p_tr[:, 128 * k:128 * (k + 1)], func=AF.Identity,
                                        scale=SC2[:, k, seg:seg + 1], bias=modT[:, 24 + k, seg:seg + 1]), [p_trb, SC2, modT], [hT2])
                    else:
                        S.op("dve", _mk("tensor_scalar", out=hT2[:, k, col:col + 128], in0=p_tr[:, 128 * k:128 * (k + 1)],
                                        scalar1=SC2[:, k, seg:seg + 1], scalar2=modT[:, 24 + k, seg:seg + 1], op0=ALU.mult, op1=ALU.add),
                             [p_trb, SC2, modT], [hT2])

            prev = None
            for t in range(nta):
                x1t = part1(t)
                if prev is not None:
                    part2(*prev)
                prev = (t, x1t)
            part2(*prev)

        def stage_b(tb0):
            colb = 1 + 128 * tb0
            pgs = [p_g, p_h]

            def ffn_mm(j):
                wbuf = wgu_t[c2["w"] % 4]
                c2["w"] += 1
                S.dma(wbuf[:].rearrange("p a k m -> p (a k m)"), WGU[j].rearrange("p a k m -> p (a k m)"), [WGU], [wbuf])
                wgt, wut = wbuf[:, 0], wbuf[:, 1]
                pg_, pu_ = pgs[j % 2], p_us[j % 2]
                for k in range(8):
                    S.op("pe", _mk("matmul", pg_[:], lhsT=wgt[:, k, :], rhs=hT2[:, k, colb:colb + 512], start=(k == 0), stop=(k == 7)),
                         [wbuf, hT2], [pg_])
                for k in range(8):
                    S.op("pe", _mk("matmul", p_ss[:, 0:2], lhsT=wgt[:, k, :], rhs=hT2[:, k, colb - 1:colb + 513:513], start=(k == 0), stop=(k == 7)),
                         [wbuf, hT2], [p_ss])
                for k in range(8):
                    S.op("pe", _mk("matmul", pu_[:], lhsT=wut[:, k, :], rhs=hT2[:, k, colb:colb + 512], start=(k == 0), stop=(k == 7)),
                         [wbuf, hT2], [pu_])
                ge = gext[j % 3]
                S.op("act", _mk("activation", out=ge[:, 1:513], in_=pg_[:], func=AF.Copy), [pg_], [ge])
                S.op("act", _mk("activation", out=ge[:, 0:514:513], in_=p_ss[:, 0:2], func=AF.Copy), [p_ss], [ge])
                S.op("act", _mk("activation", out=upsb[j % 3][:], in_=pu_[:], func=AF.Copy), [pu_], [upsb[j % 3]])

            def ffn_chain(j):
                ge, pu_ = gext[j % 3], upsb[j % 3]
                cv, sgj = cvs[j % 2], sgs[j % 2]
                if seg == 4 and tb0 == hal:
                    S.op("dve", _mk("tensor_scalar", out=ge[:, 0:1], in0=ge[:, 0:1], scalar1=edge[:, 0:1], scalar2=None, op0=ALU.mult), [ge, edge], [ge])
                if seg == 4 and tb0 + 4 == na - hal:
                    S.op("dve", _mk("tensor_scalar", out=ge[:, 513:514], in0=ge[:, 513:514], scalar1=edge[:, 1:2], scalar2=None, op0=ALU.mult),
                         [ge, edge], [ge])
                S.op("dve", _mk("tensor_scalar", out=cv[:], in0=ge[:, 1:513], scalar1=fcw[:, j, 1:2], scalar2=fcb[:, j:j + 1], op0=ALU.mult, op1=ALU.add),
                     [ge, fcw, fcb], [cv])
                S.op("dve", _mk("scalar_tensor_tensor", out=cv[:], in0=ge[:, 0:512], scalar=fcw[:, j, 0:1], in1=cv[:], op0=ALU.mult, op1=ALU.add),
                     [ge, fcw, cv], [cv])
                S.op("dve", _mk("scalar_tensor_tensor", out=cv[:], in0=ge[:, 2:514], scalar=fcw[:, j, 2:3], in1=cv[:], op0=ALU.mult, op1=ALU.add),
                     [ge, fcw, cv], [cv])
                S.op("act", _mk("activation", out=sgj[:], in_=cv[:], func=AF.Gelu_apprx_tanh), [cv], [sgj])
                S.op("pool", _mk("tensor_tensor", out=aT[:, j, :], in0=sgj[:], in1=pu_[:], op=ALU.mult), [sgj, pu_], [aT])

            for j in range(NFF):
                ffn_mm(j)
                if j >= 2:
                    ffn_chain(j - 2)
            ffn_chain(NFF - 2)
            ffn_chain(NFF - 1)
            accs = [p_o[0], p_o[1], p_us[0], p_us[1], p_g, p_h, p_ss, p_trb]
            for j in range(NFF):
                wdt = wd_t[c2["wd"] % 3]
                c2["wd"] += 1
                S.dma(wdt[:], WD[j], [WD], [wdt])
                for t in range(4):
                    for nh_ in range(2):
                        acc_ = accs[2 * t + nh_]
                        S.op("pe", _mk("matmul", acc_[:], lhsT=aT[:, j, 128 * t:128 * (t + 1)], rhs=wdt[:, 512 * nh_:512 * (nh_ + 1)],
                                       start=(j == 0), stop=(j == NFF - 1)), [aT, wdt], [acc_])
            for t in range(4):
                x1t = x1r[c2["x1"] % 2]
                c2["x1"] += 1
                S.dma(x1t[:], X1S[128 * (tb0 + t):128 * (tb0 + t + 1), :], [X1S], [x1t])
                for nh_ in range(2):
                    acc_ = accs[2 * t + nh_]
                    tm = tmp[c2["e"] % 2]
                    c2["e"] += 1
                    S.op("dve", _mk("tensor_tensor", out=tm[:], in0=acc_[:], in1=G2B[:, 512 * nh_:512 * (nh_ + 1)], op=ALU.mult), [acc_, G2B], [tm])
                    S.op("pool", _mk("tensor_tensor", out=x2[:, 512 * nh_:512 * (nh_ + 1)], in0=x1t[:, 512 * nh_:512 * (nh_ + 1)], in1=tm[:],
                                     op=ALU.add), [x1t, tm], [x2])
                yot = yo[c2["y"] % 2]
                c2["y"] += 1
                ti = c2["t"] % 4
                c2["t"] += 1
                ss, rs = ss2[ti], rs2[ti]
                S.op("act", _mk("activation", out=junk2[:], in_=x2[:], func=AF.Square, accum_out=ss[:]), [x2], [junk2, ss])
                rstd_from_ss(ss, rs, DM)
                S.op("dve", _mk("scalar_tensor_tensor", out=yot[:], in0=x2[:], scalar=rs[:], in1=fgb[:], op0=ALU.mult, op1=ALU.mult),
                     [x2, rs, fgb], [yot])
                r0 = yoff + 128 * (tb0 - hal + t)
                S.dma(ydst[r0:r0 + 128, :], yot[:], [yot], [ydst], q="actq")

        if seg < 4:
            a_groups = [(0, 4), (4, 4), (8, 4), (12, 4)]
            b_groups = [(0, 0), (4, 1), (8, 2), (12, 3)]
        else:
            a_groups = [(0, 1), (1, 4), (5, 4), (9, 4), (13, 4), (17, 1)]
            b_groups = [(1, 1), (5, 2), (9, 3), (13, 4)]
        bi = 0
        for ai, (ta0, nta) in enumerate(a_groups):
            stage_a(ta0, nta)
            while bi < len(b_groups) and b_groups[bi][1] < ai:
                stage_b(b_groups[bi][0])
                bi += 1
        while bi < len(b_groups):
            stage_b(b_groups[bi][0])
            bi += 1
    S.pop_pool()
    n = S.emit(final_wait_bufs=[yp_d, ys_d])
    return nc, n


_CACHE = {}


def fft_consts():
    out = {}
    bf = ml_dtypes.bfloat16
    for cf, P, I, Sq, L in (("p", 64, 32, 4, T), ("s", 128, 128, 1, LS)):
        N2, N = 2 * I, 2 * L
        t = np.arange(L, dtype=np.float64)
        tn = (t / (L - 1)).astype(np.float32)
        bands = np.linspace(1e-4, 15.0, 16).astype(np.float32).astype(np.float64)
        ang = (2.0 * math.pi / L) * t[:, None] * bands[None, :]
        feat = np.concatenate([tn[:, None].astype(np.float64), np.cos(ang), np.sin(ang)], axis=1).astype(np.float32)
        rev = (L - np.arange(L)) % L
        out["feat_" + cf] = np.ascontiguousarray(np.stack([feat.T, feat[rev].T], axis=1))
        tn2 = np.stack([tn, tn[rev]], axis=0)
        out["tn_" + cf] = np.ascontiguousarray(np.broadcast_to(tn2[None], (128, 2, L)).astype(np.float32))
        i = np.arange(I)[:, None]
        fb = np.arange(I)[None, :]
        th1 = 2 * math.pi * i * (fb + 0.5) / N2
        E1 = np.zeros((Sq, I, 2, Sq, I))
        G = np.zeros((Sq, I, 2, Sq, I))
        for s_ in range(Sq):
            E1[s_, :, 0, s_, :] = np.cos(th1)
            E1[s_, :, 1, s_, :] = -np.sin(th1)
            G[s_, :, 0, s_, :] = (2.0 / N) * np.cos(th1).T
            G[s_, :, 1, s_, :] = -(2.0 / N) * np.sin(th1).T
        out["E1_" + cf] = E1.reshape(128, 256).astype(bf)
        G2 = G.reshape(128, 2, 128)
        out["G_" + cf] = np.concatenate([G2[:, 0], G2[:, 1], -G2[:, 0]], axis=1).astype(bf)
        p = np.arange(P)[:, None]
        th2 = 2 * math.pi * p * (np.arange(I)[None, :] + 0.5) / N
        cr = np.tile(np.cos(th2), (1, Sq))
        ci = np.tile(-np.sin(th2), (1, Sq))
        out["TW1_" + cf] = np.concatenate([cr, ci], axis=1).astype(np.float32)
        cr2 = np.tile(np.cos(th2).T, (Sq, 1))
        ci2 = np.tile(np.sin(th2).T, (Sq, 1))
        out["TW2_" + cf] = np.concatenate([cr2, ci2], axis=1).astype(np.float32)
        thw = 2 * math.pi * p * np.arange(P)[None, :] / P
        out["W_" + cf] = np.concatenate([np.cos(thw), -np.sin(thw), np.sin(thw), -np.cos(thw)], axis=1).astype(bf)
        out["V_" + cf] = np.concatenate([np.cos(thw), np.sin(thw), -np.sin(thw), np.cos(thw), -np.cos(thw), -np.sin(thw)],
                                        axis=1).astype(bf)
        ifull = np.arange(N2)[:, None]
        thf = 2 * math.pi * ifull * (fb + 0.5) / N2
        twf = np.exp(-1j * th2) * (1j * (-1.0) ** np.arange(I))[None, :]
        if cf == "p":
            EF = np.zeros((2, N2, 2, 2, I))
            for o in range(2):
                EF[o, :, 0, o, :] = np.cos(thf)
                EF[o, :, 1, o, :] = -np.sin(thf)
            out["EF_" + cf] = EF.reshape(128, 128).astype(bf)
            out["TWF_" + cf] = np.concatenate([np.tile(twf.real, (1, 2)), np.tile(twf.imag, (1, 2))], axis=1).astype(np.float32)
        else:
            EF = np.stack([np.cos(thf), -np.sin(thf)], axis=1)
            EF = EF.reshape(2, 128, 256).transpose(1, 0, 2)
            out["EF_" + cf] = np.ascontiguousarray(EF.reshape(128, 512)).astype(bf)
            out["TWF_" + cf] = np.concatenate([twf.real, twf.imag], axis=1).astype(np.float32)
    return out


def kernel(x_prompt, x_sample, c_prompt, c_sample, ada_w, ada_b, norm1_g, w_in, hy_short_w, hy_short_b,
           hy_pos_w1, hy_pos_b1, hy_sin_freq, hy_pos_w2, hy_pos_b2, hy_pos_w3, hy_decay, hy_skip, hy_out_g,
           attn_out_g, w_out, norm2_g, ffn_w_gate, ffn_w_up, ffn_conv_w, ffn_conv_b, ffn_w_down, final_g):
    f = lambda a: np.ascontiguousarray(np.asarray(a, dtype=np.float32))
    if "nc" not in _CACHE:
        _CACHE["nc"] = build_program()
    nc, nops = _CACHE["nc"]
    x_prompt, x_sample = f(x_prompt), f(x_sample)
    xs = x_sample[0]
    pc = lambda v: f(np.asarray(v).reshape(-1, 128).T)
    common = {
        "ada_w": f(ada_w[0]), "ada_b": pc(ada_b[0]), "n1g": pc(norm1_g[0]), "n2g": pc(norm2_g[0]),
        "fgb": f(np.broadcast_to(np.asarray(final_g)[None, :], (128, DM))),
        "w_in": f(w_in[0]), "w_out": f(w_out[0]), "wg": f(ffn_w_gate[0]), "wu": f(ffn_w_up[0]), "wd": f(ffn_w_down[0]),
        "fcw": f(np.asarray(ffn_conv_w[0]).reshape(3, NFF, 128).transpose(2, 1, 0)),
        "fcb": pc(ffn_conv_b[0]), "hyg": pc(hy_out_g[0]),
        "agb": f(np.broadcast_to(np.asarray(attn_out_g[0])[None, :], (128, HW))),
        "ident": np.eye(128, dtype=np.float32), "masks": build_masks(),
        "xsf": xs,
        "hw1": f(hy_pos_w1[0]), "hb1": f(np.asarray(hy_pos_b1[0]).reshape(64, 1)), "hsf": f(np.asarray(hy_sin_freq[0]).T),
        "hw2": f(hy_pos_w2[0]), "hb2": f(np.asarray(hy_pos_b2[0]).reshape(64, 1)), "hw3": f(hy_pos_w3[0]),
        "hdec": pc(np.asarray(hy_decay[0]).reshape(-1)),
        "hsw": f(np.asarray(hy_short_w[0]).reshape(3, 12, 128).transpose(2, 1, 0)), "hsb": pc(hy_short_b[0]),
        "hskip": f(np.broadcast_to(np.asarray(hy_skip[0]).reshape(1, 1024), (128, 1024))),
    }
    common.update(fft_consts())
    in_maps = []
    for c in range(NCORE):
        lo = CH * c - 128 * QT0_S - 0
        lo = CH * c - 128 * OT0_S
        win = np.zeros((WT_S * 128, DM), np.float32)
        valid = np.zeros((WT_S * 128,), np.float32)
        a, b = max(lo, 0), min(lo + WT_S * 128, LS)
        win[a - lo:b - lo] = xs[a:b]
        valid[a - lo:b - lo] = 1.0
        edge = np.zeros((128, 2), np.float32)
        edge[:, 0] = 1.0 if c > 0 else 0.0
        edge[:, 1] = 1.0 if c < NCORE - 1 else 0.0
        cc = np.concatenate([np.asarray(c_prompt[NSEQ * c:NSEQ * (c + 1)]), np.asarray(c_sample)], axis=0)
        m = dict(common)
        m.update({
            "xp": x_prompt[NSEQ * c:NSEQ * (c + 1)].reshape(NSEQ * T, DM),
            "xsw": win, "flags": f(valid.reshape(WT_S, 128).T), "edge": edge, "cc": f(cc.T),
        })
        sel = np.zeros((128, 18), np.float32)
        for r in range(18):
            gt_ = 16 * c - 1 + r
            if 0 <= gt_ < 128:
                sel[gt_, r] = 1.0
        m["sel"] = sel.astype(ml_dtypes.bfloat16)
        m["selr"] = np.zeros((128, 128), ml_dtypes.bfloat16)
        in_maps.append(m)
    res = run_bass_kernel_spmd(nc, in_maps, core_ids=list(range(NCORE)))
    yp = np.concatenate([np.asarray(r["yp"]).reshape(NSEQ, T, DM) for r in res.results], axis=0)
    ys = np.concatenate([np.asarray(r["ys"]) for r in res.results], axis=0).reshape(1, LS, DM)
    return (yp.astype(np.float32), ys.astype(np.float32))
```
